# Optimizing a Trainium2 kernel written in Bass

```python
import math
import jax, jax.numpy as jnp
from jax import lax
import numpy as np

D_MODEL = 1024
BATCH = 4
SEQ = 4096
DEPTH = 4
DEC_BATCH = 32
DEC_SEQ = 4
PAST_LEN = 8192
PAGE_SIZE = 128

N_MIXERS = 3
N_CONV_LAYERS = (DEPTH + N_MIXERS - 1) // N_MIXERS
N_LRU_LAYERS = (DEPTH + N_MIXERS - 2) // N_MIXERS
N_ATTN_LAYERS = (DEPTH + N_MIXERS - 3) // N_MIXERS

D_CONV = D_MODEL
CONV_A_WIDTH = 3
D_LRU = D_MODEL
LRU_CONV_WIDTH = 4
LRU_BLOCKS = 4
LRU_BLOCK = D_LRU // LRU_BLOCKS
LRU_C = 8.0
N_HEADS = 8
HEAD_DIM = D_MODEL // (2 * N_HEADS)
V_HEAD_DIM = 2 * HEAD_DIM
QK_WIDTH = 2 * N_HEADS * HEAD_DIM
V_WIDTH = N_HEADS * V_HEAD_DIM
ROT_DIM = HEAD_DIM // 4
ROPE_THETA = 500000.0
Q_BLOCK = 128
LN_EPS = 1e-5
SUBLN_EPS = 1e-5
DEEPNORM_ALPHA = (2 * DEPTH) ** 0.25
DEEPNORM_BETA = (8 * DEPTH) ** -0.25

kernel_name = "hybrid_conv_lru_diffattn_step"


def layer_norm(x, g, b):
    xf = x.astype(jnp.float32)
    mu = jnp.mean(xf, axis=-1, keepdims=True)
    var = jnp.mean(jnp.square(xf - mu), axis=-1, keepdims=True)
    return ((xf - mu) * lax.rsqrt(var + LN_EPS) * g.astype(jnp.float32) + b.astype(jnp.float32)).astype(x.dtype)


def adaln(c, w, b):
    m = (jax.nn.silu(c) @ w + b)[:, None, :]
    shift, scale, gate = jnp.split(m, 3, axis=-1)
    return shift, scale, gate


def causal_dwconv(u, buf, w):
    width = w.shape[0]
    t = u.shape[1]
    full = jnp.concatenate([buf.astype(u.dtype), u], axis=1)
    y = full[:, 0:t] * w[0]
    for k in range(1, width):
        y = y + full[:, k:k + t] * w[k]
    return y, full[:, -(width - 1):]


def short_conv_mixer(u, buf, w_in, conv_w, w_out):
    h, bg, cg, z = jnp.split(u @ w_in, 4, axis=-1)
    y, new_buf = causal_dwconv(cg * h, buf, conv_w)
    return (jax.nn.silu(z) * bg * y) @ w_out, new_buf


def _linear_combine(left, right):
    a1, b1 = left
    a2, b2 = right
    return a1 * a2, a2 * b1 + b2


def rglru_mixer(u, h0, buf, w_in, conv_w, conv_b, w_ga, b_ga, w_gx, b_gx, lru_param, w_out):
    bsz, t, _ = u.shape
    xb, z = jnp.split(u @ w_in, 2, axis=-1)
    xc, new_buf = causal_dwconv(xb, buf, conv_w)
    xc = xc + conv_b
    xh = xc.reshape(bsz, t, LRU_BLOCKS, LRU_BLOCK)
    r = jax.nn.sigmoid(jnp.einsum('btnc,ncd->btnd', xh, w_ga).reshape(bsz, t, D_LRU) + b_ga)
    gi = jax.nn.sigmoid(jnp.einsum('btnc,ncd->btnd', xh, w_gx).reshape(bsz, t, D_LRU) + b_gx)
    log_a = LRU_C * r.astype(jnp.float32) * jax.nn.log_sigmoid(lru_param.astype(jnp.float32))
    a = jnp.exp(log_a)
    mult = jnp.sqrt(-jnp.expm1(2.0 * log_a))
    b = mult * (gi * xc).astype(jnp.float32)
    b = b.at[:, 0].add(a[:, 0] * h0.astype(jnp.float32))
    _, h = lax.associative_scan(_linear_combine, (a, b), axis=1)
    y = h.astype(u.dtype) * jax.nn.silu(z)
    return y @ w_out, h[:, -1], new_buf


def rope_partial(x, pos):
    half = ROT_DIM // 2
    inv_freq = jnp.exp(jnp.arange(half, dtype=jnp.float32) * (-2.0 * math.log(ROPE_THETA) / ROT_DIM))
    ang = pos.astype(jnp.float32)[:, None] * inv_freq[None, :]
    cos = jnp.cos(ang)[None, :, None, :]
    sin = jnp.sin(ang)[None, :, None, :]
    x1 = x[..., :half].astype(jnp.float32)
    x2 = x[..., half:ROT_DIM].astype(jnp.float32)
    rot = jnp.concatenate([x1 * cos - x2 * sin, x2 * cos + x1 * sin], axis=-1).astype(x.dtype)
    return jnp.concatenate([rot, x[..., ROT_DIM:]], axis=-1)


def diff_lambda(lq1, lk1, lq2, lk2, lam_init):
    f = jnp.float32
    return (jnp.exp(jnp.sum(lq1.astype(f) * lk1.astype(f))) - jnp.exp(jnp.sum(lq2.astype(f) * lk2.astype(f))) + lam_init)


def diff_attn_project(u, w_in, pos):
    bsz, t, _ = u.shape
    q, k, v, z = jnp.split(u @ w_in, [QK_WIDTH, 2 * QK_WIDTH, 2 * QK_WIDTH + V_WIDTH], axis=-1)
    q = rope_partial(q.reshape(bsz, t, 2 * N_HEADS, HEAD_DIM), pos) * (HEAD_DIM ** -0.5)
    k = rope_partial(k.reshape(bsz, t, 2 * N_HEADS, HEAD_DIM), pos)
    v = v.reshape(bsz, t, N_HEADS, V_HEAD_DIM)
    return q, k, v, z


def diff_combine(p, lam):
    p = p.reshape(p.shape[0], N_HEADS, 2, p.shape[2], p.shape[3])
    return p[:, :, 0] - lam * p[:, :, 1]


def diff_attn_prompt(q, k, v, lam):
    bsz, s = q.shape[:2]
    n_blk = s // Q_BLOCK
    qb = jnp.moveaxis(q.reshape(bsz, n_blk, Q_BLOCK, 2 * N_HEADS, HEAD_DIM), 1, 0)
    kpos = jnp.arange(s)

    def one_block(args):
        qi, blk = args
        sc = jnp.einsum('bqhd,bkhd->bhqk', qi, k).astype(jnp.float32)
        qpos = blk * Q_BLOCK + jnp.arange(Q_BLOCK)
        sc = jnp.where(kpos[None, :] <= qpos[:, None], sc, -jnp.inf)
        w = diff_combine(jax.nn.softmax(sc, axis=-1), lam).astype(v.dtype)
        return jnp.einsum('bhqk,bkhd->bqhd', w, v)

    out = lax.map(one_block, (qb, jnp.arange(n_blk)))
    return jnp.moveaxis(out, 0, 1).reshape(bsz, s, N_HEADS, V_HEAD_DIM)


def diff_attn_sample(q, k_new, v_new, cache_k, cache_v, page_table, lam):
    bsz, t = q.shape[:2]
    k_past = cache_k[page_table].reshape(bsz, -1, 2 * N_HEADS, HEAD_DIM)
    v_past = cache_v[page_table].reshape(bsz, -1, N_HEADS, V_HEAD_DIM)
    n_past = k_past.shape[1]
    s_past = jnp.einsum('bqhd,bkhd->bhqk', q, k_past).astype(jnp.float32)
    s_new = jnp.einsum('bqhd,bkhd->bhqk', q, k_new).astype(jnp.float32)
    causal = jnp.tril(jnp.ones((t, t), dtype=bool))
    s_new = jnp.where(causal, s_new, -jnp.inf)
    p = jax.nn.softmax(jnp.concatenate([s_past, s_new], axis=-1), axis=-1)
    w = diff_combine(p, lam).astype(v_new.dtype)
    return (jnp.einsum('bhqk,bkhd->bqhd', w[..., :n_past], v_past)
            + jnp.einsum('bhqk,bkhd->bqhd', w[..., n_past:], v_new))


def diff_attn_output(o, z, subln_g, lam_init, w_out):
    bsz, t = o.shape[:2]
    of = o.astype(jnp.float32)
    of = of * lax.rsqrt(jnp.mean(jnp.square(of), axis=-1, keepdims=True) + SUBLN_EPS)
    of = of * subln_g.astype(jnp.float32) * (1.0 - lam_init)
    o = of.astype(z.dtype).reshape(bsz, t, V_WIDTH)
    return (o * jax.nn.silu(z)) @ w_out


def setup_inputs(seed: int = 0) -> dict:
    key = jax.random.key(seed)
    ks = iter(jax.random.split(key, 40))
    nrm = lambda shape, s=1.0: jax.random.normal(next(ks), shape, jnp.float32) * s
    n_pages = PAST_LEN // PAGE_SIZE
    n_phys = (5 * DEC_BATCH * n_pages + 3) // 4
    d = D_MODEL
    page_table = jax.random.permutation(next(ks), n_phys)[:DEC_BATCH * n_pages].reshape(DEC_BATCH, n_pages).astype(jnp.int32)
    u = jax.random.uniform(next(ks), (N_LRU_LAYERS, D_LRU), jnp.float32, minval=0.9, maxval=0.999)
    s_lru = u ** (1.0 / LRU_C)
    return {
        'x_prompt': nrm((BATCH, SEQ, d)),
        'x_sample': nrm((DEC_BATCH, DEC_SEQ, d)),
        'state_conv_a': nrm((N_CONV_LAYERS, DEC_BATCH, CONV_A_WIDTH - 1, D_CONV)),
        'state_lru_h': nrm((N_LRU_LAYERS, DEC_BATCH, D_LRU), 0.5),
        'state_lru_conv': nrm((N_LRU_LAYERS, DEC_BATCH, LRU_CONV_WIDTH - 1, D_LRU)),
        'cache_k': nrm((N_ATTN_LAYERS, n_phys, PAGE_SIZE, 2 * N_HEADS, HEAD_DIM)),
        'cache_v': nrm((N_ATTN_LAYERS, n_phys, PAGE_SIZE, N_HEADS, V_HEAD_DIM)),
        'page_table': page_table,
        'c_prompt': nrm((BATCH, d)),
        'c_sample': nrm((DEC_BATCH, d)),
        'w_ada': nrm((DEPTH, d, 3 * d), d ** -0.5),
        'b_ada': nrm((DEPTH, 3 * d), 0.02),
        'ln_g': 1.0 + nrm((DEPTH, d), 0.05),
        'ln_b': nrm((DEPTH, d), 0.02),
        'a_w_in': nrm((N_CONV_LAYERS, d, 4 * D_CONV), d ** -0.5),
        'a_conv_w': nrm((N_CONV_LAYERS, CONV_A_WIDTH, D_CONV), CONV_A_WIDTH ** -0.5),
        'a_w_out': nrm((N_CONV_LAYERS, D_CONV, d), DEEPNORM_BETA * D_CONV ** -0.5),
        'r_w_in': nrm((N_LRU_LAYERS, d, 2 * D_LRU), d ** -0.5),
        'r_conv_w': nrm((N_LRU_LAYERS, LRU_CONV_WIDTH, D_LRU), LRU_CONV_WIDTH ** -0.5),
        'r_conv_b': nrm((N_LRU_LAYERS, D_LRU), 0.02),
        'r_w_ga': nrm((N_LRU_LAYERS, LRU_BLOCKS, LRU_BLOCK, LRU_BLOCK), LRU_BLOCK ** -0.5),
        'r_b_ga': nrm((N_LRU_LAYERS, D_LRU), 0.1),
        'r_w_gx': nrm((N_LRU_LAYERS, LRU_BLOCKS, LRU_BLOCK, LRU_BLOCK), LRU_BLOCK ** -0.5),
        'r_b_gx': nrm((N_LRU_LAYERS, D_LRU), 0.1),
        'r_lru_param': jnp.log(s_lru) - jnp.log1p(-s_lru),
        'r_w_out': nrm((N_LRU_LAYERS, D_LRU, d), DEEPNORM_BETA * D_LRU ** -0.5),
        'd_w_in': nrm((N_ATTN_LAYERS, d, 2 * QK_WIDTH + 2 * V_WIDTH), d ** -0.5),
        'd_lq1': nrm((N_ATTN_LAYERS, HEAD_DIM), 0.1),
        'd_lk1': nrm((N_ATTN_LAYERS, HEAD_DIM), 0.1),
        'd_lq2': nrm((N_ATTN_LAYERS, HEAD_DIM), 0.1),
        'd_lk2': nrm((N_ATTN_LAYERS, HEAD_DIM), 0.1),
        'd_subln_g': 1.0 + nrm((N_ATTN_LAYERS, V_HEAD_DIM), 0.05),
        'd_w_out': nrm((N_ATTN_LAYERS, V_WIDTH, d), DEEPNORM_BETA * V_WIDTH ** -0.5),
    }


def reference(x_prompt, x_sample, state_conv_a, state_lru_h, state_lru_conv, cache_k, cache_v, page_table,
              c_prompt, c_sample, w_ada, b_ada, ln_g, ln_b, a_w_in, a_conv_w, a_w_out,
              r_w_in, r_conv_w, r_conv_b, r_w_ga, r_b_ga, r_w_gx, r_b_gx, r_lru_param, r_w_out,
              d_w_in, d_lq1, d_lk1, d_lq2, d_lk2, d_subln_g, d_w_out):
    xp, xs = x_prompt, x_sample
    bp, tp = xp.shape[:2]
    bs, ts = xs.shape[:2]
    pos_p = jnp.arange(tp)
    pos_s = PAST_LEN + jnp.arange(ts)
    conv_p, conv_s, lruh_p, lruh_s, lruc_p, lruc_s = [], [], [], [], [], []
    k_p, v_p, k_s, v_s = [], [], [], []
    for i in range(DEPTH):
        kind, j = i % N_MIXERS, i // N_MIXERS
        shp, scp, gtp = adaln(c_prompt, w_ada[i], b_ada[i])
        shs, scs, gts = adaln(c_sample, w_ada[i], b_ada[i])
        up = xp * (1.0 + scp) + shp
        us = xs * (1.0 + scs) + shs
        if kind == 0:
            zero_buf = jnp.zeros((bp, CONV_A_WIDTH - 1, D_CONV), xp.dtype)
            op, nbp = short_conv_mixer(up, zero_buf, a_w_in[j], a_conv_w[j], a_w_out[j])
            osm, nbs = short_conv_mixer(us, state_conv_a[j], a_w_in[j], a_conv_w[j], a_w_out[j])
            conv_p.append(nbp)
            conv_s.append(nbs)
        elif kind == 1:
            h0 = jnp.zeros((bp, D_LRU), jnp.float32)
            zero_buf = jnp.zeros((bp, LRU_CONV_WIDTH - 1, D_LRU), xp.dtype)
            op, hp, nbp = rglru_mixer(up, h0, zero_buf, r_w_in[j], r_conv_w[j], r_conv_b[j], r_w_ga[j], r_b_ga[j],
                                      r_w_gx[j], r_b_gx[j], r_lru_param[j], r_w_out[j])
            osm, hs, nbs = rglru_mixer(us, state_lru_h[j], state_lru_conv[j], r_w_in[j], r_conv_w[j], r_conv_b[j],
                                       r_w_ga[j], r_b_ga[j], r_w_gx[j], r_b_gx[j], r_lru_param[j], r_w_out[j])
            lruh_p.append(hp)
            lruh_s.append(hs)
            lruc_p.append(nbp)
            lruc_s.append(nbs)
        else:
            lam_init = 0.8 - 0.6 * math.exp(-0.3 * i)
            lam = diff_lambda(d_lq1[j], d_lk1[j], d_lq2[j], d_lk2[j], lam_init)
            qp, kp, vp, zp = diff_attn_project(up, d_w_in[j], pos_p)
            op = diff_attn_output(diff_attn_prompt(qp, kp, vp, lam), zp, d_subln_g[j], lam_init, d_w_out[j])
            qs, ksn, vsn, zs = diff_attn_project(us, d_w_in[j], pos_s)
            o_heads = diff_attn_sample(qs, ksn, vsn, cache_k[j], cache_v[j], page_table, lam)
            osm = diff_attn_output(o_heads, zs, d_subln_g[j], lam_init, d_w_out[j])
            k_p.append(kp)
            v_p.append(vp)
            k_s.append(ksn)
            v_s.append(vsn)
        xp = layer_norm(DEEPNORM_ALPHA * xp + gtp * op, ln_g[i], ln_b[i])
        xs = layer_norm(DEEPNORM_ALPHA * xs + gts * osm, ln_g[i], ln_b[i])
    return (xp, xs,
            jnp.stack(conv_p), jnp.stack(conv_s),
            jnp.stack(lruh_p), jnp.stack(lruh_s),
            jnp.stack(lruc_p), jnp.stack(lruc_s),
            jnp.stack(k_p), jnp.stack(v_p), jnp.stack(k_s), jnp.stack(v_s))
```

```python
import numpy as np
import concourse.bass as bass
import concourse.mybir as mybir

F32 = mybir.dt.float32
BF16 = mybir.dt.bfloat16
I32 = mybir.dt.int32
ALU = mybir.AluOpType
AF = mybir.ActivationFunctionType
AX = mybir.AxisListType


class Buf:
    def __init__(self, name):
        self.name = name
        self.wc = {}
        self.wd = []
        self.rc = {}
        self.rd = []


class T:
    def __init__(self, h, name):
        self.h = h
        self.b = Buf(name)

    def __getitem__(self, k):
        return self.h[k]

    def ap(self):
        return self.h.ap()


class Sched:
    EP = 4000
    NDS = 40

    def __init__(self, nc):
        self.nc = nc
        self.eng = dict(pe=nc.tensor, act=nc.scalar, dve=nc.vector, pool=nc.gpsimd, sp=nc.sync)
        self.cnt = {e: 0 for e in self.eng}
        self.esems = {e: [] for e in self.eng}
        self.seen = {e: {} for e in self.eng}
        self.dsems = [nc.alloc_semaphore("dq%d" % i) for i in range(self.NDS)]
        self.duse = [0] * self.NDS
        self.dn = 0
        self.nwaits = 0

    def sb(self, name, shape, dt=F32):
        return T(self.nc.alloc_sbuf_tensor(name, list(shape), dt), name)

    def ps(self, name, shape, dt=F32):
        return T(self.nc.alloc_psum_tensor(name, list(shape), dt), name)

    def dram(self, name, shape, dt=F32, kind="Internal"):
        return T(self.nc.dram_tensor(name, list(shape), dt, kind=kind), name)

    def _wait(self, e, tok):
        if tok[0] == 'c':
            _, f, c = tok
            if self.seen[e].get(('c', f), 0) >= c:
                return
            ep, v = (c - 1) // self.EP, (c - 1) % self.EP + 1
            self.eng[e].wait_ge(self.esems[f][ep], v)
            self.seen[e][('c', f)] = c
        else:
            _, i, v = tok
            if self.seen[e].get(('d', i), 0) >= v:
                return
            self.eng[e].wait_ge(self.dsems[i], v)
            self.seen[e][('d', i)] = v
        self.nwaits += 1

    def _deps(self, e, reads, writes):
        for b in reads:
            for f, c in b.wc.items():
                self._wait(e, ('c', f, c))
            for t in b.wd:
                self._wait(e, t)
        for b in writes:
            for f, c in b.rc.items():
                if f != e:
                    self._wait(e, ('c', f, c))
            for t in b.rd:
                self._wait(e, t)
            for f, c in b.wc.items():
                if f != e:
                    self._wait(e, ('c', f, c))
            for t in b.wd:
                self._wait(e, t)

    def _record(self, tok, reads, writes):
        for b in reads:
            if tok[0] == 'c':
                b.rc[tok[1]] = max(b.rc.get(tok[1], 0), tok[2])
            else:
                b.rd.append(tok)
        for b in writes:
            if b.rc or b.rd:
                b.rc = {}
                b.rd = []
                b.wc = {}
                b.wd = []
            if tok[0] == 'c':
                b.wc[tok[1]] = max(b.wc.get(tok[1], 0), tok[2])
            else:
                b.wd.append(tok)

    @staticmethod
    def _bufs(xs):
        out = []
        for x in xs:
            if x is None:
                continue
            out.append(x.b if isinstance(x, T) else x)
        return out

    def op(self, e, fn, reads=(), writes=()):
        reads = self._bufs(reads)
        writes = self._bufs(writes)
        self._deps(e, reads, writes)
        ins = fn()
        self.cnt[e] += 1
        c = self.cnt[e]
        ep = (c - 1) // self.EP
        while len(self.esems[e]) <= ep:
            self.esems[e].append(self.nc.alloc_semaphore("c_%s_%d" % (e, len(self.esems[e]))))
        ins.then_inc(self.esems[e][ep], 1)
        tok = ('c', e, c)
        self._record(tok, reads, writes)
        return tok

    def dma(self, q, out, in_, reads=(), writes=(), **kw):
        reads = self._bufs(reads)
        writes = self._bufs(writes)
        self._deps(q, reads, writes)
        i = self.dn % self.NDS
        self.dn += 1
        self.duse[i] += 1
        v = 16 * self.duse[i]
        if v > 16:
            self._wait(q, ('d', i, v - 16))
        ins = self.eng[q].dma_start(out=out, in_=in_, **kw)
        ins.then_inc(self.dsems[i], 16)
        tok = ('d', i, v)
        self._record(tok, reads, writes)
        return tok

    def idma(self, out, in_, idx_ap, reads=(), writes=()):
        return self.coll(lambda: self.nc.gpsimd.indirect_dma_start(
            out=out, out_offset=None, in_=in_, in_offset=bass.IndirectOffsetOnAxis(ap=idx_ap, axis=0)), reads=reads, writes=writes)

    def coll(self, fn, reads=(), writes=()):
        q = 'pool'
        reads = self._bufs(reads)
        writes = self._bufs(writes)
        self._deps(q, reads, writes)
        i = self.dn % self.NDS
        self.dn += 1
        self.duse[i] += 1
        v = 16 * self.duse[i]
        if v > 16:
            self._wait(q, ('d', i, v - 16))
        ins = fn()
        ins.then_inc(self.dsems[i], 16)
        tok = ('d', i, v)
        self._record(tok, reads, writes)
        return tok

    def barrier(self):
        for e in ('pe', 'act', 'dve', 'pool', 'sp'):
            for i in range(self.NDS):
                if self.duse[i]:
                    self._wait(e, ('d', i, 16 * self.duse[i]))
            for f in ('pe', 'act', 'dve', 'pool'):
                if self.cnt[f] and f != e:
                    self._wait(e, ('c', f, self.cnt[f]))

    def finish(self):
        for i in range(self.NDS):
            if self.duse[i]:
                self._wait('sp', ('d', i, 16 * self.duse[i]))
        for f in ('pe', 'act', 'dve', 'pool'):
            if self.cnt[f]:
                self._wait('sp', ('c', f, self.cnt[f]))
import math
from concourse.bass_utils import run_bass_kernel_spmd


D = 1024
ALPHA = (2 * 4) ** 0.25
LN_EPS = 1e-5
ROT = 16
HALF = 8
THETA = 500000.0
TWO_PI = 2.0 * math.pi


def build(SEQ, NPAGE, NPHYS, PAST):
    NCH = SEQ // 512
    nc = bass.Bass("TRN2", target_bir_lowering=False)
    S = Sched(nc)
    V, A, G, PE = nc.vector, nc.scalar, nc.gpsimd, nc.tensor

    def DI(name, shape, dt=F32):
        return S.dram(name, shape, dt, kind="ExternalInput")

    def DO(name, shape, dt=F32):
        return S.dram(name, shape, dt, kind="ExternalOutput")

    xp = DI("xp", [SEQ, D]); xs = DI("xs", [16, D])
    cp = DI("cp", [1, D]); cs = DI("cs", [4, D])
    st_conv = DI("st_conv", [2, 8, D])
    st_h = DI("st_h", [4, D]); st_lc = DI("st_lc", [12, D])
    ck = DI("ck", [NPHYS, 128, D]); cv = DI("cv", [NPHYS, 128, D])
    ptab = DI("ptab", [1, 4 * NPAGE], I32)
    w_ada = DI("w_ada", [4, D, 3 * D]); b_ada = DI("b_ada", [4, 3 * D])
    ln_g = DI("ln_g", [4, D]); ln_b = DI("ln_b", [4, D])
    a_w_in = DI("a_w_in", [2, D, 4 * D]); a_conv_w = DI("a_conv_w", [2, 3, D]); a_w_out = DI("a_w_out", [2, D, D])
    r_w_in = DI("r_w_in", [1, D, 2 * D]); r_conv_w = DI("r_conv_w", [1, 4, D]); r_conv_b = DI("r_conv_b", [1, D])
    r_w_ga = DI("r_w_ga", [1, 4, 256, 256]); r_b_ga = DI("r_b_ga", [1, D])
    r_w_gx = DI("r_w_gx", [1, 4, 256, 256]); r_b_gx = DI("r_b_gx", [1, D])
    r_lru = DI("r_lru", [1, D]); r_w_out = DI("r_w_out", [1, D, D])
    d_w_in = DI("d_w_in", [1, D, 4 * D]); d_lam = DI("d_lam", [4, 64])
    d_subg = DI("d_subg", [1, 128]); d_w_out = DI("d_w_out", [1, D, D])

    yp = DO("yp", [SEQ, D]); ys = DO("ys", [16, D])
    conv_p = DO("conv_p", [2, 2, D]); conv_s = DO("conv_s", [2, 8, D])
    lruh_p = DO("lruh_p", [1, D]); lruh_s = DO("lruh_s", [4, D])
    lruc_p = DO("lruc_p", [3, D]); lruc_s = DO("lruc_s", [12, D])
    kp = DO("kp", [SEQ, D]); vp = DO("vp", [SEQ, D]); ks = DO("ks", [16, D]); vs = DO("vs", [16, D])
    kT_d = S.dram("kT_d", [8, 128, SEQ], BF16)
    v_d = S.dram("v_d", [8, 128, SEQ // 128, 129], BF16)

    ident = S.sb("ident", [128, 128], BF16); identf = S.sb("identf", [128, 128])
    tri = S.sb("tri", [128, 128], BF16); onesf = S.sb("onesf", [128, 128])
    xbf = S.sb("xbf", [128, D], BF16)
    wslot = [S.sb("w%d" % i, [128, 8, 512], BF16) for i in range(4)]
    modp = S.sb("modp", [128, 4, 24])
    badaT = S.sb("badaT", [128, 4, 24])
    lngT = S.sb("lngT", [128, 4, 8]); lnbT = S.sb("lnbT", [128, 4, 8])
    bc = S.sb("bc", [128, 3, D])
    scT = S.sb("scT", [128, 8, 17])
    fb = [S.sb("fb%d" % i, [128, 512], BF16) for i in range(2)]
    fz = [S.sb("fz%d" % i, [128, 512], BF16) for i in range(2)]
    tm = [S.sb("tm%d" % i, [128, D]) for i in range(3)]
    small = S.sb("small", [128, 64])
    cw_a = S.sb("cw_a", [128, 2, 3, 8]); halo_a = S.sb("halo_a", [128, 2, 8, 2])
    cw_r = S.sb("cw_r", [128, 4, 8]); cb_r = S.sb("cb_r", [128, 8]); halo_r = S.sb("halo_r", [128, 8, 3])
    bga = S.sb("bga", [128, 8]); bgx = S.sb("bgx", [128, 8]); clam = S.sb("clam", [128, 8]); clam2 = S.sb("clam2", [128, 8])
    hst = S.sb("hst", [128, 8])
    stT = S.sb("stT", [128, 8, 12]); sttm = tm[2]
    cos_t = S.sb("cos_t", [128, SEQ // 128, HALF]); sin_t = S.sb("sin_t", [128, SEQ // 128, HALF])
    cos_s = S.sb("cos_s", [16, HALF]); sin_s = S.sb("sin_s", [16, HALF])
    lam = S.sb("lam", [128, 2]); subg = S.sb("subg", [128, 128])
    qf = S.sb("qf", [128, D]); kf = S.sb("kf", [128, D]); vf = S.sb("vf", [128, D])
    pTS = S.sb("pTS", [128, 64], BF16)
    import contextlib
    wdefs = {}
    kp_ = lambda ap: ap.rearrange("(k p) n -> p k n", p=128)
    for j in range(2):
        for c in range(8):
            wdefs[("a_in", j, c)] = [(lambda w, g4=g4: w[:, :, g4 * 128:(g4 + 1) * 128], kp_(a_w_in[j, :, g4 * D + c * 128:g4 * D + (c + 1) * 128])) for g4 in range(4)]
        for nb in range(2):
            wdefs[(("a_out", j), nb)] = [(lambda w: w[:, :, :], kp_(a_w_out[j, :, nb * 512:(nb + 1) * 512]))]
    for nb in range(4):
        wdefs[("r_in", nb)] = [(lambda w, gz=gz: w[:, :, gz * 256:(gz + 1) * 256], kp_(r_w_in[0, :, gz * D + nb * 256:gz * D + (nb + 1) * 256])) for gz in range(2)]
        wdefs[("r_g", nb)] = [(lambda w: w[:, 0:2, 0:256], kp_(r_w_ga[0, nb])), (lambda w: w[:, 2:4, 0:256], kp_(r_w_gx[0, nb]))]
    for nb in range(2):
        wdefs[(("r_out", 0), nb)] = [(lambda w: w[:, :, :], kp_(r_w_out[0, :, nb * 512:(nb + 1) * 512]))]
        wdefs[(("d_out", 0), nb)] = [(lambda w: w[:, :, :], kp_(d_w_out[0, :, nb * 512:(nb + 1) * 512]))]
    for blk in range(8):
        wdefs[("d_in", blk)] = [(lambda w: w[:, :, :], kp_(d_w_in[0, :, blk * 512:(blk + 1) * 512]))]
    widx = {k: i for i, k in enumerate(wdefs)}
    wbuf = [Buf("wsc%d" % i) for i in range(len(wdefs))]
    wsc = S.dram("wsc", [len(wdefs), 128, 4096], BF16)

    def layer_keys(l):
        kind, j = l % 3, l // 3
        if kind == 0:
            return [("a_in", j, c) for c in range(8)] + [(("a_out", j), nb) for nb in range(2)]
        if kind == 1:
            ks_ = []
            for nb in range(4):
                ks_ += [("r_in", nb), ("r_g", nb)]
            return ks_ + [(("r_out", 0), nb) for nb in range(2)]
        return [("d_in", blk) for blk in range(8)] + [(("d_out", 0), nb) for nb in range(2)]
    WSEQ = []
    for _rep in range(1 + NCH):
        for l in range(4):
            WSEQ += layer_keys(l)

    with contextlib.ExitStack() as st0:
        st32 = [T(st0.enter_context(nc.sbuf_tensor("st32_%d" % i, [128, 8, 512], F32)), "st32_%d" % i) for i in range(3)]
        stbf = [T(st0.enter_context(nc.sbuf_tensor("stbf_%d" % i, [128, 8, 512], BF16)), "stbf_%d" % i) for i in range(3)]
        for i in range(3):
            S.op('pool', lambda i=i: G.memset(st32[i][:], 0.0), writes=[st32[i]])
        ceng = ['dve', 'act', 'pool']
        for i, (key, parts) in enumerate(wdefs.items()):
            a32, abf = st32[i % 3], stbf[i % 3]
            for (dfn, src) in parts:
                S.dma('sp', dfn(a32), src, writes=[a32])
            e = ceng[i % 3]
            if e == 'act':
                S.op('act', lambda a32=a32, abf=abf: A.copy(out=abf[:], in_=a32[:]), reads=[a32], writes=[abf])
            elif e == 'dve':
                S.op('dve', lambda a32=a32, abf=abf: V.tensor_copy(out=abf[:], in_=a32[:]), reads=[a32], writes=[abf])
            else:
                S.op('pool', lambda a32=a32, abf=abf: G.tensor_copy(out=abf[:], in_=a32[:]), reads=[a32], writes=[abf])
            S.dma('sp', wsc[widx[key], :, :], abf[:].rearrange("p k n -> p (k n)"), reads=[abf], writes=[wbuf[widx[key]]])
        S.barrier()
    stack = contextlib.ExitStack()

    def sbt(name, shape, dt=F32):
        return T(stack.enter_context(nc.sbuf_tensor(name, list(shape), dt)), name)

    uT = sbt("uT_s", [128, 8, 16], BF16); gT = sbt("gT_s", [128, 8, 16], BF16)
    qT = sbt("qT_s", [128, 8, 16], BF16); kTc = sbt("kTc_s", [128, 8, 16], BF16)
    vac = sbt("vac_s", [16, 1, 8, 129], BF16)
    szb = sbt("szb_s", [16, 1, D], BF16); gtm = sbt("gtm_s", [16, 1, D], BF16)
    X = None; kTh = None; vh = None; pTs = None
    mods = sbt("mods", [16, 3 * D]); c17 = tm[0]
    Xs = sbt("Xs", [16, D]); badar = sbt("badar", [16, 512])
    wadaf = [sbt("wadaf0", [128, 8, 256])]
    fa = [sbt("fa%d_s" % i, [128, 264]) for i in range(8)]
    ptab_sb = sbt("ptab_sb", [128, 4 * NPAGE], I32); pidx = sbt("pidx", [128, 4 * NPAGE], I32)
    pgf = [sbt("pgf%d" % i, [128, D]) for i in range(2)]
    pidf = sbt("pidf", [128, 2, 4 * NPAGE])
    kpg = [sbt("kpg%d" % i, [128, D], BF16) for i in range(2)]
    vpg = [sbt("vpg%d" % i, [128, D], BF16) for i in range(2)]
    kTp = [sbt("kTp%d" % i, [128, 8, 128], BF16) for i in range(2)]
    qblk = sbt("qblk", [128, 8, 4, 64], BF16)
    sS = sbt("sS", [64, PAST + 16])
    pS = sbt("pS", [64, PAST + 16], BF16)
    cmask = sbt("cmask", [64, 4, 16])

    pA = [S.ps("pA%d" % i, [128, 512]) for i in range(4)]
    pO = [S.ps("pO%d" % i, [128, 512]) for i in range(2)]
    pT = S.ps("pT", [128, 1024], BF16)
    pX = S.ps("pX", [128, 512])
    pTf = T(pT.h.bitcast(F32), "pTf"); pTf.b = pT.b

    def vec_fm(dst_ap, src_row_ap, rd, wr):
        S.dma('sp', dst_ap, src_row_ap.rearrange("(c p) -> p c", p=128), writes=[wr], reads=rd,
              allow_slow_non_contiguous=True)

    it = sbt("it", [128, 128], I32)
    S.op('pool', lambda: G.iota(it[:], pattern=[[1, 128]], base=0, channel_multiplier=-1), writes=[it])
    S.op('dve', lambda: V.tensor_copy(out=identf[:], in_=it[:]), reads=[it], writes=[identf])
    S.op('dve', lambda: V.tensor_single_scalar(out=tm[0][:, 0:128], in_=identf[:], scalar=0.0, op=ALU.is_ge), reads=[identf], writes=[tm[0]])
    S.op('dve', lambda: V.tensor_copy(out=tri[:], in_=tm[0][:, 0:128]), reads=[tm[0]], writes=[tri])
    S.op('dve', lambda: V.tensor_single_scalar(out=identf[:], in_=identf[:], scalar=0.0, op=ALU.is_equal), reads=[identf], writes=[identf])
    S.op('dve', lambda: V.tensor_copy(out=ident[:], in_=identf[:]), reads=[identf], writes=[ident])
    S.op('pool', lambda: G.memset(halo_a[:], 0.0), writes=[halo_a])
    S.op('pool', lambda: G.memset(halo_r[:], 0.0), writes=[halo_r])
    S.op('pool', lambda: G.memset(hst[:], 0.0), writes=[hst])

    pos_i = sbt("pos_i", [128, SEQ // 128], I32)
    S.op('pool', lambda: G.iota(pos_i[:], pattern=[[128, SEQ // 128]], base=0, channel_multiplier=1), writes=[pos_i])
    posf = fa[0]
    S.op('dve', lambda: V.tensor_copy(out=posf[:, 0:SEQ // 128], in_=pos_i[:]), reads=[pos_i], writes=[posf])
    NTL = SEQ // 128

    def rope_tables(cosd, sind, pos_ap, npart, n):
        for i in range(HALF):
            inv = math.exp(-2.0 * math.log(THETA) * i / ROT) / TWO_PI
            for (dst, off) in ((sind, 0.0), (cosd, 0.25)):
                S.op('dve', lambda dst=dst, off=off, inv=inv, i=i: V.tensor_scalar(
                    out=dst[0:npart, :, i] if n > 1 else dst[0:npart, i:i + 1], in0=pos_ap, scalar1=inv, scalar2=off,
                    op0=ALU.mult, op1=ALU.add), reads=[posf], writes=[dst])
        for dst in (sind, cosd):
            full = dst[0:npart, :, :] if n > 1 else dst[0:npart, :]
            ki = rki[0:npart, 0:n * HALF] if n == 1 else rki[0:npart, 0:n * HALF].rearrange("p (a b) -> p a b", b=HALF)
            kf = rkf[0:npart, 0:n * HALF] if n == 1 else rkf[0:npart, 0:n * HALF].rearrange("p (a b) -> p a b", b=HALF)
            S.op('dve', lambda full=full, ki=ki: V.tensor_copy(out=ki, in_=full), reads=[dst], writes=[rki])
            S.op('dve', lambda ki=ki, kf=kf: V.tensor_copy(out=kf, in_=ki), reads=[rki], writes=[rkf])
            S.op('dve', lambda full=full, kf=kf: V.tensor_tensor(out=full, in0=full, in1=kf, op=ALU.subtract), reads=[dst, rkf], writes=[dst])
            S.op('dve', lambda full=full, kf=kf: V.tensor_single_scalar(out=kf, in_=full, scalar=0.5, op=ALU.is_gt), reads=[dst], writes=[rkf])
            S.op('dve', lambda full=full, kf=kf: V.tensor_tensor(out=full, in0=full, in1=kf, op=ALU.subtract), reads=[dst, rkf], writes=[dst])
            S.op('dve', lambda full=full, kf=kf: V.tensor_single_scalar(out=kf, in_=full, scalar=-0.5, op=ALU.is_lt), reads=[dst], writes=[rkf])
            S.op('dve', lambda full=full, kf=kf: V.tensor_tensor(out=full, in0=full, in1=kf, op=ALU.add), reads=[dst, rkf], writes=[dst])
            S.op('act', lambda full=full: A.activation(out=full, in_=full, func=AF.Sin, scale=TWO_PI * 0.999999), reads=[dst], writes=[dst])

    rki = sbt("rki", [128, NTL * HALF], I32); rkf = sbt("rkf", [128, NTL * HALF])
    rope_tables(cos_t, sin_t, posf[:, 0:NTL], 128, NTL)
    pos_s = sbt("pos_s", [16, 1], I32)
    S.op('pool', lambda: G.iota(pos_s[:], pattern=[[0, 1]], base=0, channel_multiplier=1), writes=[pos_s])
    S.op('dve', lambda: V.tensor_single_scalar(out=pos_s[:], in_=pos_s[:], scalar=3, op=ALU.bitwise_and), reads=[pos_s], writes=[pos_s])
    S.op('dve', lambda: V.tensor_copy(out=posf[0:16, 0:1], in_=pos_s[:]), reads=[pos_s, cos_t, sin_t], writes=[posf])
    S.op('dve', lambda: V.tensor_scalar_add(out=posf[0:16, 0:1], in0=posf[0:16, 0:1], scalar1=float(PAST)), reads=[posf], writes=[posf])
    rope_tables(cos_s, sin_s, posf[0:16, 0:1], 16, 1)

    for l in range(4):
        vec_fm(lngT[:, l, :], ln_g[l, :], [], lngT)
        vec_fm(lnbT[:, l, :], ln_b[l, :], [], lnbT)
        for g3 in range(3):
            vec_fm(badaT[:, l, 8 * g3:8 * g3 + 8], b_ada[l, g3 * D:(g3 + 1) * D], [], badaT)
    for j in range(2):
        for k in range(3):
            vec_fm(cw_a[:, j, k, :], a_conv_w[j, k, :], [], cw_a)
    for k in range(4):
        vec_fm(cw_r[:, k, :], r_conv_w[0, k, :], [], cw_r)
    vec_fm(cb_r[:], r_conv_b[0, :], [], cb_r)
    vec_fm(bga[:], r_b_ga[0, :], [], bga)
    vec_fm(bgx[:], r_b_gx[0, :], [], bgx)
    vec_fm(clam[:], r_lru[0, :], [], clam)
    S.op('act', lambda: A.activation(out=clam[:], in_=clam[:], func=AF.Exp, scale=-1.0), reads=[clam], writes=[clam])
    S.op('act', lambda: A.activation(out=clam[:], in_=clam[:], func=AF.Ln, bias=1.0), reads=[clam], writes=[clam])
    S.op('dve', lambda: V.tensor_scalar_mul(out=clam2[:], in0=clam[:], scalar1=-16.0), reads=[clam], writes=[clam2])
    S.op('dve', lambda: V.tensor_scalar_mul(out=clam[:], in0=clam[:], scalar1=-8.0), reads=[clam, clam2], writes=[clam])
    lam_init = 0.8 - 0.6 * math.exp(-0.3 * 2)
    lq = sbt("lq", [128, 4, 64])
    S.dma('sp', lq[:], d_lam.ap().rearrange("(o a) d -> o a d", o=1).broadcast_to([128, 4, 64]), writes=[lq])
    S.op('dve', lambda: V.tensor_tensor(out=lq[:, 0, :], in0=lq[:, 0, :], in1=lq[:, 1, :], op=ALU.mult), reads=[lq], writes=[lq])
    S.op('dve', lambda: V.tensor_tensor(out=lq[:, 2, :], in0=lq[:, 2, :], in1=lq[:, 3, :], op=ALU.mult), reads=[lq], writes=[lq])
    S.op('dve', lambda: V.tensor_reduce(out=small[:, 0:1], in_=lq[:, 0, :], axis=AX.X, op=ALU.add), reads=[lq], writes=[small])
    S.op('dve', lambda: V.tensor_reduce(out=small[:, 1:2], in_=lq[:, 2, :], axis=AX.X, op=ALU.add), reads=[lq], writes=[small])
    S.op('act', lambda: A.activation(out=small[:, 0:2], in_=small[:, 0:2], func=AF.Exp), reads=[small], writes=[small])
    S.op('dve', lambda: V.tensor_tensor(out=lam[:, 0:1], in0=small[:, 0:1], in1=small[:, 1:2], op=ALU.subtract), reads=[small], writes=[lam])
    S.op('dve', lambda: V.tensor_scalar(out=lam[:, 0:1], in0=lam[:, 0:1], scalar1=lam_init, scalar2=-1.0, op0=ALU.add, op1=ALU.mult), reads=[lam], writes=[lam])
    S.dma('sp', subg[:], d_subg[0:1, :].broadcast_to([128, 128]), writes=[subg])
    S.op('dve', lambda: V.tensor_scalar_mul(out=subg[:], in0=subg[:], scalar1=1.0 - lam_init), reads=[subg], writes=[subg])

    NSLOT = 4
    wstate = dict(wp=0, issued=0)

    def wload(key):
        assert WSEQ[wstate['wp']] == key, (WSEQ[wstate['wp']], key)
        while wstate['issued'] < min(len(WSEQ), wstate['wp'] + NSLOT - 1):
            k2 = WSEQ[wstate['issued']]
            idx = widx[k2]
            w = wslot[wstate['issued'] % NSLOT]
            S.dma('sp', w[:].rearrange("p k n -> p (k n)"), wsc[idx, :, :], reads=[wbuf[idx]], writes=[w])
            wstate['issued'] += 1
        w = wslot[wstate['wp'] % NSLOT]
        wstate['wp'] += 1
        return w

    def mm(out_ap, lhsT, rhs, start, stop, reads, writes, skip=False):
        if skip:
            S.op('pe', lambda: PE.matmul(out_ap, lhsT, rhs, start=start, stop=stop, skip_group_check=True), reads=reads, writes=writes)
        else:
            S.op('pe', lambda: PE.matmul(out_ap, lhsT, rhs, start=start, stop=stop), reads=reads, writes=writes)

    S.dma('sp', c17[0:1, :], cp[0:1, :], writes=[c17])
    for b in range(4):
        S.dma('sp', c17[1 + 4 * b:5 + 4 * b, :], cs[b:b + 1, :].broadcast_to([4, D]), writes=[c17])
    S.op('act', lambda: A.activation(out=c17[0:17, :], in_=c17[0:17, :], func=AF.Silu), reads=[c17], writes=[c17])
    for c in range(8):
        S.op('pe', lambda c=c: PE.transpose(out=pX[:, 0:17], in_=c17[0:17, c * 128:(c + 1) * 128], identity=identf[0:17, 0:17]),
             reads=[c17, identf], writes=[pX])
        S.op('dve', lambda c=c: V.tensor_copy(out=scT[:, c, :], in_=pX[:, 0:17]), reads=[pX], writes=[scT])

    def adaln_layer(l):
        for nb in range(12):
            wf = wadaf[0]
            S.dma('sp', wf[:], w_ada[l, :, nb * 256:(nb + 1) * 256].rearrange("(k p) n -> p k n", p=128), writes=[wf])
            S.dma('sp', badar[:, 0:256], b_ada[l:l + 1, nb * 256:(nb + 1) * 256].broadcast_to([16, 256]), writes=[badar])
            for k in range(8):
                mm(pO[0][0:16, 0:256], scT[:, k, 1:17], wf[:, k, :], k == 0, k == 7, [scT, wf], [pO[0]])
            S.op('dve', lambda nb=nb: V.tensor_tensor(out=mods[:, nb * 256:(nb + 1) * 256], in0=pO[0][0:16, 0:256], in1=badar[:, 0:256], op=ALU.add),
                 reads=[pO[0], badar], writes=[mods])
            for cc in range(2):
                for k in range(8):
                    mm(pX[:, cc:cc + 1], wf[:, k, cc * 128:(cc + 1) * 128], scT[:, k, 0:1], k == 0, k == 7, [scT, wf], [pX])
            S.op('dve', lambda nb=nb: V.tensor_tensor(out=modp[:, l, nb * 2:nb * 2 + 2], in0=pX[:, 0:2], in1=badaT[:, l, nb * 2:nb * 2 + 2], op=ALU.add),
                 reads=[pX, badaT], writes=[modp])
        S.op('dve', lambda: V.tensor_scalar_add(out=modp[:, l, 8:16], in0=modp[:, l, 8:16], scalar1=1.0), reads=[modp], writes=[modp])
        S.op('dve', lambda: V.tensor_scalar_add(out=mods[:, D:2 * D], in0=mods[:, D:2 * D], scalar1=1.0), reads=[mods], writes=[mods])

    def bcast_layer(l, with_gate):
        srcs = [(0, modp[:, l, 16:24], modp)] if with_gate else []
        srcs += [(1, lngT[:, l, :], lngT), (2, lnbT[:, l, :], lnbT)]
        n = 0
        for (slot, src, srcT) in srcs:
            for half in range(2):
                dg = tm[1 + n % 2]
                pb = (pX, pO[0])[n % 2]
                n += 1
                for c4 in range(4):
                    S.op('dve', lambda half=half, src=src, dg=dg, c4=c4: V.tensor_scalar_mul(
                        out=dg[:, c4 * 128:(c4 + 1) * 128], in0=identf[:, :], scalar1=src[:, half * 4 + c4:half * 4 + c4 + 1]),
                        reads=[identf, srcT], writes=[dg])
                mm(pb[:, :], onesf[:, :], dg[:, 0:512], True, True, [dg, onesf], [pb])
                S.op('act', lambda slot=slot, half=half, pb=pb: A.copy(out=bc[:, slot, half * 512:(half + 1) * 512], in_=pb[:, :]), reads=[pb], writes=[bc])

    S.op('pool', lambda: G.memset(onesf[:], 1.0), writes=[onesf])

    def make_uT(l, xtile_ap, ntok, tok0, sample):
        if sample:
            S.op('dve', lambda: V.tensor_tensor(out=tm[0][0:16, :], in0=xtile_ap, in1=mods[:, D:2 * D], op=ALU.mult), reads=[Xs, mods], writes=[tm[0]])
            S.op('dve', lambda: V.tensor_tensor(out=xbf[0:16, :], in0=tm[0][0:16, :], in1=mods[:, 0:D], op=ALU.add), reads=[tm[0], mods], writes=[xbf])
        else:
            S.op('pool', lambda: G.tensor_copy(out=xbf[:, :], in_=xtile_ap), reads=[X], writes=[xbf])
        for c in range(8):
            S.op('pe', lambda c=c: PE.transpose(out=pT[:, c * 128:c * 128 + ntok], in_=xbf[0:ntok, c * 128:(c + 1) * 128], identity=ident[0:ntok, 0:ntok]),
                 reads=[xbf, ident], writes=[pT])
        for c in range(8):
            if sample:
                S.op('act', lambda c=c: A.copy(out=uT[:, c, tok0:tok0 + ntok], in_=pT[:, c * 128:c * 128 + ntok]), reads=[pT], writes=[uT])
            else:
                S.op('act', lambda c=c: A.activation(out=uT[:, c, tok0:tok0 + ntok], in_=pT[:, c * 128:c * 128 + ntok], func=AF.Identity,
                                                     bias=modp[:, l, c:c + 1], scale=modp[:, l, 8 + c:9 + c]), reads=[pT, modp], writes=[uT])

    def out_proj_ln(l, w_out_d, ntiles, ntok, xt, xap, sample, after_tile=None):
        wo = [wload((w_out_d, nb)) for nb in range(2)]
        for t in range(ntiles):
            r = tm[0]
            for nb in range(2):
                for k in range(8):
                    mm(pO[nb][0:ntok, :], gT[:, k, t * 128:t * 128 + ntok], wo[nb][:, k, :], k == 0, k == 7, [gT, wo[nb]], [pO[nb]])
                gate_ap = mods[:, 2 * D + nb * 512:2 * D + (nb + 1) * 512] if sample else bc[:, 0, nb * 512:(nb + 1) * 512]
                S.op('dve', lambda nb=nb, gate_ap=gate_ap: V.tensor_tensor(out=tm[1][0:ntok, nb * 512:(nb + 1) * 512], in0=pO[nb][0:ntok, :], in1=gate_ap, op=ALU.mult),
                     reads=[pO[nb], mods if sample else bc], writes=[tm[1]])
            xa = xap(t)
            S.op('dve', lambda xa=xa: V.scalar_tensor_tensor(out=r[0:ntok, :], in0=xa, scalar=ALPHA, in1=tm[1][0:ntok, :], op0=ALU.mult, op1=ALU.add),
                 reads=[xt, tm[1]], writes=[r])
            for hh in range(2):
                S.op('dve', lambda hh=hh: V.bn_stats(out=small[0:ntok, 8 + 6 * hh:14 + 6 * hh], in_=r[0:ntok, hh * 512:(hh + 1) * 512]), reads=[r], writes=[small])
            S.op('dve', lambda: V.bn_aggr(out=small[0:ntok, 20:22], in_=small[0:ntok, 8:20]), reads=[small], writes=[small])
            S.op('dve', lambda: V.tensor_scalar_add(out=small[0:ntok, 22:23], in0=small[0:ntok, 21:22], scalar1=LN_EPS), reads=[small], writes=[small])
            S.op('act', lambda: A.activation(out=small[0:ntok, 22:23], in_=small[0:ntok, 22:23], func=AF.Ln), reads=[small], writes=[small])
            S.op('act', lambda: A.activation(out=small[0:ntok, 22:23], in_=small[0:ntok, 22:23], func=AF.Exp, scale=-0.5), reads=[small], writes=[small])
            S.op('dve', lambda: V.scalar_tensor_tensor(out=small[0:ntok, 23:24], in0=small[0:ntok, 20:21], scalar=-1.0, in1=small[0:ntok, 22:23], op0=ALU.mult, op1=ALU.mult), reads=[small], writes=[small])
            S.op('act', lambda: A.activation(out=tm[1][0:ntok, :], in_=r[0:ntok, :], func=AF.Identity, bias=small[0:ntok, 23:24], scale=small[0:ntok, 22:23]),
                 reads=[r, small], writes=[tm[1]])
            S.op('pool', lambda: G.tensor_tensor(out=tm[1][0:ntok, :], in0=tm[1][0:ntok, :], in1=bc[0:ntok, 1, :], op=ALU.mult), reads=[tm[1], bc], writes=[tm[1]])
            S.op('pool', lambda xa=xa: G.tensor_tensor(out=xa, in0=tm[1][0:ntok, :], in1=bc[0:ntok, 2, :], op=ALU.add), reads=[tm[1], bc], writes=[xt])
            if after_tile is not None:
                after_tile(t)

    def fm_to_rows(src_fn, nrow, dst_dram_ap, rd):
        for c in range(8):
            S.op('pe', lambda c=c: PE.transpose(out=pX[0:nrow, (c % 4) * 128:(c % 4) * 128 + 128], in_=src_fn(c), identity=identf[:, :]),
                 reads=rd + [identf], writes=[pX])
            if c % 4 == 3:
                h0 = (c // 4) * 512
                S.op('act', lambda h0=h0: A.copy(out=sttm[0:nrow, h0:h0 + 512], in_=pX[0:nrow, :]), reads=[pX], writes=[sttm])
        S.dma('sp', dst_dram_ap, sttm[0:nrow, :], reads=[sttm], writes=[])

    def rows_to_fm(src_dram_ap, nrow, dst_fn, wr):
        S.dma('sp', sttm[0:nrow, :], src_dram_ap, writes=[sttm])
        for c in range(8):
            S.op('pe', lambda c=c: PE.transpose(out=pX[:, 0:nrow], in_=sttm[0:nrow, c * 128:(c + 1) * 128], identity=identf[0:nrow, 0:nrow]),
                 reads=[sttm, identf], writes=[pX])
            S.op('dve', lambda c=c: V.tensor_copy(out=dst_fn(c), in_=pX[:, 0:nrow]), reads=[pX], writes=[wr])

    def conv_layer(l, j, ntok, sample, last):
        N = ntok
        for c in range(8):
            w = wload(("a_in", j, c))
            pq = pA if c % 2 == 0 else [pO[0], pO[1], pX, pA[3]]
            for g4 in range(4):
                for k in range(8):
                    mm(pq[g4][:, 0:N], w[:, k, g4 * 128:(g4 + 1) * 128], uT[:, k, 0:N], k == 0, k == 7, [w, uT], [pq[g4]])
            hs, pe_, y, sz = fa[0 + 4 * (c % 2)], fa[1 + 4 * (c % 2)], fa[2 + 4 * (c % 2)], fa[3 + 4 * (c % 2)]
            S.op('act', lambda: A.copy(out=hs[:, 0:N], in_=pq[0][:, 0:N]), reads=[pq[0]], writes=[hs])
            S.op('act', lambda: A.activation(out=sz[:, 0:N], in_=pq[3][:, 0:N], func=AF.Silu), reads=[pq[3]], writes=[sz])
            if not sample:
                S.op('pool', lambda c=c: G.tensor_copy(out=pe_[:, 0:2], in_=halo_a[:, j, c, :]), reads=[halo_a], writes=[pe_])
                S.op('dve', lambda: V.tensor_tensor(out=pe_[:, 2:2 + N], in0=pq[2][:, 0:N], in1=hs[:, 0:N], op=ALU.mult), reads=[pq[2], hs], writes=[pe_])
                S.op('dve', lambda: V.tensor_tensor(out=sz[:, 0:N], in0=pq[1][:, 0:N], in1=sz[:, 0:N], op=ALU.mult), reads=[pq[1], sz], writes=[sz])
                S.op('pool', lambda c=c: G.tensor_copy(out=halo_a[:, j, c, :], in_=pe_[:, N:N + 2]), reads=[pe_], writes=[halo_a])
                v0, v1, v2, yo = pe_[:, 0:N], pe_[:, 1:N + 1], pe_[:, 2:N + 2], y[:, 0:N]
            else:
                p3 = pe_[:, 0:24].rearrange("p (b t) -> p b t", b=4)
                S.op('pool', lambda c=c: G.tensor_copy(out=p3[:, :, 0:2], in_=stT[:, c, 0:8].rearrange("p (b r) -> p b r", b=4)), reads=[stT], writes=[pe_])
                S.op('dve', lambda: V.tensor_tensor(out=p3[:, :, 2:6], in0=pq[2][:, 0:16].rearrange("p (b t) -> p b t", b=4),
                                                    in1=hs[:, 0:16].rearrange("p (b t) -> p b t", b=4), op=ALU.mult), reads=[pq[2], hs], writes=[pe_])
                S.op('dve', lambda: V.tensor_tensor(out=sz[:, 0:N], in0=pq[1][:, 0:N], in1=sz[:, 0:N], op=ALU.mult), reads=[pq[1], sz], writes=[sz])
                S.op('pool', lambda c=c: G.tensor_copy(out=stT[:, c, 0:8].rearrange("p (b r) -> p b r", b=4), in_=p3[:, :, 4:6]), reads=[pe_], writes=[stT])
                v0, v1, v2 = p3[:, :, 0:4], p3[:, :, 1:5], p3[:, :, 2:6]
                yo = y[:, 0:16].rearrange("p (b t) -> p b t", b=4)
            S.op('dve', lambda c=c: V.tensor_scalar_mul(out=yo, in0=v0, scalar1=cw_a[:, j, 0, c:c + 1]), reads=[pe_, cw_a], writes=[y])
            S.op('dve', lambda c=c: V.scalar_tensor_tensor(out=yo, in0=v1, scalar=cw_a[:, j, 1, c:c + 1], in1=yo, op0=ALU.mult, op1=ALU.add), reads=[pe_, cw_a, y], writes=[y])
            S.op('dve', lambda c=c: V.scalar_tensor_tensor(out=yo, in0=v2, scalar=cw_a[:, j, 2, c:c + 1], in1=yo, op0=ALU.mult, op1=ALU.add), reads=[pe_, cw_a, y], writes=[y])
            S.op('dve', lambda c=c: V.tensor_tensor(out=gT[:, c, 0:N], in0=sz[:, 0:N], in1=y[:, 0:N], op=ALU.mult), reads=[sz, y], writes=[gT])
        if sample:
            fm_to_rows(lambda c: stT[:, c, 0:8], 8, conv_s[j, :, :], [stT])
        elif last:
            fm_to_rows(lambda c: halo_a[:, j, c, :], 2, conv_p[j, :, :], [halo_a])

    def lru_layer(l, ntok, sample, last):
        N = ntok
        for nb in range(4):
            w = wload(("r_in", nb))
            xbk = [pA[0], pA[1]] if nb % 2 == 0 else [pX, pA[1]]
            for e in range(2):
                for gz in range(2):
                    dst = xbk[e] if gz == 0 else pA[2 + e]
                    for k in range(8):
                        mm(dst[:, 0:N], w[:, k, gz * 256 + e * 128:gz * 256 + (e + 1) * 128], uT[:, k, 0:N], k == 0, k == 7, [w, uT], [dst])
            for e in range(2):
                S.op('act', lambda e=e: A.activation(out=fz[e][:, 0:N], in_=pA[2 + e][:, 0:N], func=AF.Silu), reads=[pA[2 + e]], writes=[fz[e]])
            wg = wload(("r_g", nb))
            xcs = []
            for e in range(2):
                ch = nb * 2 + e
                xe, xc = fa[e], fa[2 + e]
                if not sample:
                    S.op('pool', lambda ch=ch, xe=xe: G.tensor_copy(out=xe[:, 0:3], in_=halo_r[:, ch, :]), reads=[halo_r], writes=[xe])
                    S.op('act', lambda e=e, xe=xe: A.copy(out=xe[:, 3:3 + N], in_=xbk[e][:, 0:N]), reads=[xbk[e]], writes=[xe])
                    S.op('pool', lambda ch=ch, xe=xe: G.tensor_copy(out=halo_r[:, ch, :], in_=xe[:, N:N + 3]), reads=[xe], writes=[halo_r])
                    vk = [xe[:, k:k + N] for k in range(4)]
                    xo = xc[:, 0:N]
                else:
                    x3 = xe[:, 0:28].rearrange("p (b t) -> p b t", b=4)
                    S.op('pool', lambda ch=ch, x3=x3: G.tensor_copy(out=x3[:, :, 0:3], in_=stT[:, ch, 0:12].rearrange("p (b r) -> p b r", b=4)), reads=[stT], writes=[xe])
                    S.op('act', lambda e=e, x3=x3: A.copy(out=x3[:, :, 3:7], in_=xbk[e][:, 0:16].rearrange("p (b t) -> p b t", b=4)), reads=[xbk[e]], writes=[xe])
                    S.op('pool', lambda ch=ch, x3=x3: G.tensor_copy(out=stT[:, ch, 0:12].rearrange("p (b r) -> p b r", b=4), in_=x3[:, :, 4:7]), reads=[xe], writes=[stT])
                    vk = [x3[:, :, k:k + 4] for k in range(4)]
                    xo = xc[:, 0:16].rearrange("p (b t) -> p b t", b=4)
                S.op('dve', lambda ch=ch, xo=xo, vk=vk: V.tensor_scalar(out=xo, in0=vk[0], scalar1=cw_r[:, 0, ch:ch + 1], scalar2=cb_r[:, ch:ch + 1], op0=ALU.mult, op1=ALU.add),
                     reads=[xe, cw_r, cb_r], writes=[xc])
                for k in range(1, 4):
                    S.op('dve', lambda ch=ch, xo=xo, vk=vk, k=k: V.scalar_tensor_tensor(out=xo, in0=vk[k], scalar=cw_r[:, k, ch:ch + 1], in1=xo, op0=ALU.mult, op1=ALU.add),
                         reads=[xe, cw_r, xc], writes=[xc])
                S.op('act', lambda e=e, xc=xc: A.copy(out=fb[e][:, 0:N], in_=xc[:, 0:N]), reads=[xc], writes=[fb[e]])
                xcs.append(xc)
            for e in range(2):
                ch = nb * 2 + e
                xc = xcs[e]
                for (ko, dst) in ((0, pO[0]), (2, pO[1])):
                    for kk in range(2):
                        mm(dst[:, 0:N], wg[:, ko + kk, e * 128:(e + 1) * 128], fb[kk][:, 0:N], kk == 0, kk == 1, [wg, fb[kk]], [dst])
                rr, gi, aa, bb, hh = fa[4], fa[5], fa[6], fa[7], fa[4]
                S.op('act', lambda ch=ch: A.activation(out=rr[:, 0:N], in_=pO[0][:, 0:N], func=AF.Sigmoid, bias=bga[:, ch:ch + 1]), reads=[pO[0], bga], writes=[rr])
                S.op('act', lambda ch=ch: A.activation(out=gi[:, 0:N], in_=pO[1][:, 0:N], func=AF.Sigmoid, bias=bgx[:, ch:ch + 1]), reads=[pO[1], bgx], writes=[gi])
                S.op('act', lambda ch=ch: A.activation(out=aa[:, 0:N], in_=rr[:, 0:N], func=AF.Exp, scale=clam[:, ch:ch + 1]), reads=[rr, clam], writes=[aa])
                S.op('act', lambda ch=ch: A.activation(out=bb[:, 0:N], in_=rr[:, 0:N], func=AF.Exp, scale=clam2[:, ch:ch + 1]), reads=[rr, clam2], writes=[bb])
                S.op('act', lambda: A.activation(out=bb[:, 0:N], in_=bb[:, 0:N], func=AF.Sqrt, bias=1.0, scale=-1.0), reads=[bb], writes=[bb])
                S.op('dve', lambda xc=xc: V.tensor_tensor(out=gi[:, 0:N], in0=gi[:, 0:N], in1=xc[:, 0:N], op=ALU.mult), reads=[gi, xc], writes=[gi])
                S.op('dve', lambda: V.tensor_tensor(out=bb[:, 0:N], in0=bb[:, 0:N], in1=gi[:, 0:N], op=ALU.mult), reads=[bb, gi], writes=[bb])
                if not sample:
                    S.op('dve', lambda ch=ch: V.tensor_tensor_scan(out=hh[:, 0:N], data0=aa[:, 0:N], data1=bb[:, 0:N], initial=hst[:, ch:ch + 1], op0=ALU.mult, op1=ALU.add),
                         reads=[aa, bb, hst], writes=[hh])
                    S.op('pool', lambda ch=ch: G.tensor_copy(out=hst[:, ch:ch + 1], in_=hh[:, N - 1:N]), reads=[hh], writes=[hst])
                else:
                    a3 = aa[:, 0:16].rearrange("p (b t) -> p b t", b=4)
                    b3 = bb[:, 0:16].rearrange("p (b t) -> p b t", b=4)
                    h3 = hh[:, 0:16].rearrange("p (b t) -> p b t", b=4)
                    for t in range(4):
                        prev = stT2[:, ch, :] if t == 0 else h3[:, :, t - 1]
                        S.op('dve', lambda t=t, prev=prev: V.tensor_tensor(out=h3[:, :, t], in0=a3[:, :, t], in1=prev, op=ALU.mult), reads=[aa, hh, stT2], writes=[hh])
                        S.op('dve', lambda t=t: V.tensor_tensor(out=h3[:, :, t], in0=h3[:, :, t], in1=b3[:, :, t], op=ALU.add), reads=[bb, hh], writes=[hh])
                    S.op('pool', lambda ch=ch: G.tensor_copy(out=stT2[:, ch, :], in_=h3[:, :, 3]), reads=[hh], writes=[stT2])
                S.op('dve', lambda ch=ch, e=e: V.tensor_tensor(out=gT[:, ch, 0:N], in0=hh[:, 0:N], in1=fz[e][:, 0:N], op=ALU.mult), reads=[hh, fz[e]], writes=[gT])
        if sample:
            fm_to_rows(lambda c: stT2[:, c, :], 4, lruh_s[:, :], [stT2])
            fm_to_rows(lambda c: stT[:, c, 0:12], 12, lruc_s[:, :], [stT])
        elif last:
            fm_to_rows(lambda c: hst[:, c:c + 1], 1, lruh_p[:, :], [hst])
            fm_to_rows(lambda c: halo_r[:, c, :], 3, lruc_p[:, :], [halo_r])

    stT2 = sbt("stT2", [128, 8, 4])

    def rope(tile, np_, cosap, sinap, rdT):
        t3 = tile[0:np_, :].rearrange("p (s d) -> p s d", s=16)
        x1, x2 = t3[:, :, 0:HALF], t3[:, :, HALF:ROT]
        cb = cosap.rearrange("p (o d) -> p o d", o=1).broadcast_to([np_, 16, HALF])
        sb_ = sinap.rearrange("p (o d) -> p o d", o=1).broadcast_to([np_, 16, HALF])
        tmps = [tm[2][0:np_, i * 128:(i + 1) * 128].rearrange("p (s d) -> p s d", s=16) for i in range(4)]
        S.op('dve', lambda: V.tensor_tensor(out=tmps[0], in0=x1, in1=cb, op=ALU.mult), reads=[tile, rdT], writes=[tm[2]])
        S.op('dve', lambda: V.tensor_tensor(out=tmps[1], in0=x2, in1=sb_, op=ALU.mult), reads=[tile, rdT], writes=[tm[2]])
        S.op('dve', lambda: V.tensor_tensor(out=tmps[2], in0=x2, in1=cb, op=ALU.mult), reads=[tile, rdT], writes=[tm[2]])
        S.op('dve', lambda: V.tensor_tensor(out=tmps[3], in0=x1, in1=sb_, op=ALU.mult), reads=[tile, rdT], writes=[tm[2]])
        S.op('dve', lambda: V.tensor_tensor(out=x1, in0=tmps[0], in1=tmps[1], op=ALU.subtract), reads=[tm[2]], writes=[tile])
        S.op('dve', lambda: V.tensor_tensor(out=x2, in0=tmps[2], in1=tmps[3], op=ALU.add), reads=[tm[2]], writes=[tile])

    def attn_project(ntiles, ntok, tok_base_tile, sample):
        cnt = 0
        for kind in range(4):
            ws = [wload(("d_in", 2 * kind + hf)) for hf in range(2)]
            tgt = (qf, kf, vf, None)[kind]
            for t in range(ntiles):
                for hf in range(2):
                    w = ws[hf]
                    ps = pA[cnt % 4]
                    cnt += 1
                    hcol = hf * 512
                    for k in range(8):
                        mm(ps[0:ntok, :], uT[:, k, t * 128:t * 128 + ntok], w[:, k, :], k == 0, k == 7, [uT, w], [ps])
                    if kind == 0:
                        S.op('act', lambda ps=ps, hcol=hcol: A.activation(out=qf[0:ntok, hcol:hcol + 512], in_=ps[0:ntok, :], func=AF.Copy, scale=0.125), reads=[ps], writes=[qf])
                    elif kind == 3:
                        S.op('act', lambda ps=ps, t=t, hcol=hcol: A.activation(out=szb[0:ntok, t, hcol:hcol + 512], in_=ps[0:ntok, :], func=AF.Silu), reads=[ps], writes=[szb])
                    else:
                        S.op('act', lambda ps=ps, tgt=tgt, hcol=hcol: A.copy(out=tgt[0:ntok, hcol:hcol + 512], in_=ps[0:ntok, :]), reads=[ps], writes=[tgt])
                if kind == 3:
                    continue
                if kind < 2:
                    if sample:
                        rope(tgt, ntok, cos_s[:, :], sin_s[:, :], cos_s)
                    else:
                        tt = tok_base_tile + t
                        rope(tgt, ntok, cos_t[:, tt, :], sin_t[:, tt, :], cos_t)
                    if kind == 1:
                        dst = ks[:, :] if sample else kp[(tok_base_tile + t) * 128:(tok_base_tile + t + 1) * 128, :]
                        S.dma('sp', dst, kf[0:ntok, :], reads=[kf], writes=[])
                    S.op('pool', lambda tgt=tgt: G.tensor_copy(out=xbf[0:ntok, :], in_=tgt[0:ntok, :]), reads=[tgt], writes=[xbf])
                    for c in range(8):
                        S.op('pe', lambda c=c: PE.transpose(out=pT[:, c * 128:c * 128 + ntok], in_=xbf[0:ntok, c * 128:(c + 1) * 128], identity=ident[0:ntok, 0:ntok]),
                             reads=[xbf, ident], writes=[pT])
                    dT = qT if kind == 0 else kTc
                    S.op('act', lambda dT=dT, t=t: A.copy(out=dT[:, :, t * 128:t * 128 + ntok], in_=pT[:, :].rearrange("p (c n) -> p c n", c=8)[:, :, 0:ntok]), reads=[pT], writes=[dT])
                else:
                    dst = vs[:, :] if sample else vp[(tok_base_tile + t) * 128:(tok_base_tile + t + 1) * 128, :]
                    S.dma('sp' if sample else 'pool', dst, vf[0:ntok, :], reads=[vf], writes=[])
                    S.op('pool', lambda t=t: G.tensor_copy(out=vac[0:ntok, t, :, 0:128], in_=vf[0:ntok, :].rearrange("p (h d) -> p h d", h=8)), reads=[vf], writes=[vac])

    def attn_finish(oaps, np_, t, h, zap, dst_ap, rd_extra):
        o1, o2 = oaps
        S.op('dve', lambda: V.reciprocal(out=small[0:np_, 30:31], in_=o1[:, 128:129]), reads=rd_extra, writes=[small])
        S.op('dve', lambda: V.reciprocal(out=small[0:np_, 31:32], in_=o2[:, 128:129]), reads=rd_extra, writes=[small])
        S.op('dve', lambda: V.tensor_tensor(out=small[0:np_, 31:32], in0=small[0:np_, 31:32], in1=lam[0:np_, 0:1], op=ALU.mult), reads=[small, lam], writes=[small])
        ob = fa[6]
        S.op('dve', lambda: V.tensor_scalar_mul(out=ob[0:np_, 0:128], in0=o1[:, 0:128], scalar1=small[0:np_, 30:31]), reads=rd_extra + [small], writes=[ob])
        S.op('dve', lambda: V.scalar_tensor_tensor(out=ob[0:np_, 0:128], in0=o2[:, 0:128], scalar=small[0:np_, 31:32], in1=ob[0:np_, 0:128], op0=ALU.mult, op1=ALU.add),
             reads=rd_extra + [small, ob], writes=[ob])
        S.op('act', lambda: A.activation(out=ob[0:np_, 128:256], in_=ob[0:np_, 0:128], func=AF.Square, accum_out=small[0:np_, 32:33]), reads=[ob], writes=[ob, small])
        S.op('dve', lambda: V.tensor_scalar(out=small[0:np_, 33:34], in0=small[0:np_, 32:33], scalar1=1.0 / 128.0, scalar2=1e-5, op0=ALU.mult, op1=ALU.add), reads=[small], writes=[small])
        S.op('act', lambda: A.activation(out=small[0:np_, 33:34], in_=small[0:np_, 33:34], func=AF.Ln), reads=[small], writes=[small])
        S.op('act', lambda: A.activation(out=small[0:np_, 33:34], in_=small[0:np_, 33:34], func=AF.Exp, scale=-0.5), reads=[small], writes=[small])
        S.op('dve', lambda: V.scalar_tensor_tensor(out=ob[0:np_, 0:128], in0=ob[0:np_, 0:128], scalar=small[0:np_, 33:34], in1=subg[0:np_, :], op0=ALU.mult, op1=ALU.mult),
             reads=[ob, small, subg], writes=[ob])
        S.op('dve', lambda: V.tensor_tensor(out=dst_ap, in0=ob[0:np_, 0:128], in1=zap, op=ALU.mult), reads=[ob, szb], writes=[gtm])

    def kv_load(ci, h):
        S.dma('sp', kTh[h % 2][:, 0:ci * 512], kT_d[h, :, 0:ci * 512], reads=[kT_d], writes=[kTh[h % 2]])
        S.dma('sp', vh[h % 2][:, 0:ci * 4, :], v_d[h, :, 0:ci * 4, :], reads=[v_d], writes=[vh[h % 2]])

    def attn_prompt(ci):
        nkt_prev = ci * 4
        for h in range(8):
            kb, vb = kTh[h % 2], vh[h % 2]
            if ci > 0 and h + 1 < 8:
                kv_load(ci, h + 1)
            def oacc(sh, qs):
                i = sh * 4 + qs
                bank = (pO[0], pO[1], pX)[i // 3]
                return bank, bank[:, (i % 3) * 129:(i % 3) * 129 + 129]
            nkt = nkt_prev + 4
            cnt = 0
            for kt in range(nkt):
                jd = kt - nkt_prev
                q0 = max(jd, 0) * 128
                for sh in range(2):
                    ps = pA[cnt % 4]
                    pt_ = pTs[cnt % 3]
                    cnt += 1
                    if jd >= 0:
                        klhs = kTc[sh * 64:(sh + 1) * 64, h, jd * 128:(jd + 1) * 128]; krd = kTc
                        vrhs = vac[:, jd, h, :]; vrd = vac
                    else:
                        klhs = kb[sh * 64:(sh + 1) * 64, kt * 128:(kt + 1) * 128]; krd = kb
                        vrhs = vb[:, kt, :]; vrd = vb
                    mm(ps[:, q0:512], klhs, qT[sh * 64:(sh + 1) * 64, h, q0:512], True, True, [krd, qT], [ps])
                    S.op('act', lambda ps=ps, pt_=pt_, q0=q0: A.activation(out=pt_[:, q0:512], in_=ps[:, q0:512], func=AF.Exp), reads=[ps], writes=[pt_])
                    if jd >= 0:
                        S.op('pool', lambda pt_=pt_, q0=q0: G.tensor_tensor(out=pt_[:, q0:q0 + 128], in0=pt_[:, q0:q0 + 128], in1=tri[:, :], op=ALU.mult), reads=[pt_, tri], writes=[pt_])
                    for qs in range(max(jd, 0), 4):
                        bank, oap = oacc(sh, qs)
                        mm(oap, pt_[:, qs * 128:(qs + 1) * 128], vrhs, kt == 0 and (sh * 4 + qs) % 3 == 0, kt == nkt_prev + qs, [pt_, vrd], [bank], skip=True)
            osb = [fa[(h % 2) * 3 + i] for i in range(3)]
            for i, bank in enumerate((pO[0], pO[1], pX)):
                if i % 2 == 0:
                    S.op('act', lambda i=i, bank=bank: A.copy(out=osb[i][:, 0:387], in_=bank[:, 0:387]), reads=[bank], writes=[osb[i]])
                else:
                    S.op('dve', lambda i=i, bank=bank: V.tensor_copy(out=osb[i][:, 0:387], in_=bank[:, 0:387]), reads=[bank], writes=[osb[i]])
            for qs in range(4):
                i1, i2 = qs, 4 + qs
                o1 = osb[i1 // 3][:, (i1 % 3) * 129:(i1 % 3) * 129 + 129]
                o2 = osb[i2 // 3][:, (i2 % 3) * 129:(i2 % 3) * 129 + 129]
                attn_finish((o1, o2), 128, qs, h, szb[:, qs, h * 128:(h + 1) * 128], gtm[:, qs, h * 128:(h + 1) * 128], [osb[i1 // 3], osb[i2 // 3]])
        if ci < NCH - 1:
            S.dma('pool', kT_d.ap()[:, :, ci * 512:(ci + 1) * 512].rearrange("h p n -> p h n"), kTc[:, :, :], reads=[kTc], writes=[kT_d])
            for t4 in range(4):
                S.dma('pool', v_d.ap()[:, :, ci * 4 + t4, :].rearrange("h p d -> p h d"), vac[:, t4, :, :], reads=[vac], writes=[v_d])

    def g_to_gT(ntiles, ntok):
        for t in range(ntiles):
            for c in range(8):
                S.op('pe', lambda c=c, t=t: PE.transpose(out=pT[:, c * 128:c * 128 + ntok], in_=gtm[0:ntok, t, c * 128:(c + 1) * 128], identity=ident[0:ntok, 0:ntok]),
                     reads=[gtm, ident], writes=[pT])
            S.op('act', lambda t=t: A.copy(out=gT[:, :, t * 128:t * 128 + ntok], in_=pT[:, :].rearrange("p (c n) -> p c n", c=8)[:, :, 0:ntok]), reads=[pT], writes=[gT])

    def attn_sample():
        S.dma('sp', ptab_sb[:], ptab[0:1, :].broadcast_to([128, 4 * NPAGE]), writes=[ptab_sb])
        S.op('pool', lambda: G.iota(pidx[:], pattern=[[0, 4 * NPAGE]], base=0, channel_multiplier=1), writes=[pidx])
        S.op('dve', lambda: V.tensor_copy(out=pidf[:, 0, :], in_=pidx[:]), reads=[pidx], writes=[pidf])
        S.op('dve', lambda: V.tensor_copy(out=pidf[:, 1, :], in_=ptab_sb[:]), reads=[ptab_sb], writes=[pidf])
        S.op('dve', lambda: V.scalar_tensor_tensor(out=pidf[:, 1, :], in0=pidf[:, 1, :], scalar=128.0, in1=pidf[:, 0, :], op0=ALU.mult, op1=ALU.add), reads=[pidf], writes=[pidf])
        S.op('dve', lambda: V.tensor_copy(out=pidx[:], in_=pidf[:, 1, :]), reads=[pidf], writes=[pidx])
        S.op('pool', lambda: G.memset(qblk[:], 0.0), writes=[qblk])
        for h in range(8):
            for j in range(2):
                S.op('act', lambda h=h, j=j: A.copy(
                    out=qblk[j * 64:(j + 1) * 64, h, :, h * 8:(h + 1) * 8].rearrange("p b (t j) -> p b t j", j=2)[:, :, :, j],
                    in_=qT[j * 64:(j + 1) * 64, h, 0:16].rearrange("p (b t) -> p b t", b=4)), reads=[qT], writes=[qblk])
        NK = PAST + 16
        ck2 = ck.ap().rearrange("g p n -> (g p) n"); cv2 = cv.ap().rearrange("g p n -> (g p) n")
        if True:
            for b in range(4):
                for pg in range(NPAGE):
                    kb, ktp = kpg[pg % 2], kTp[pg % 2]
                    col = b * NPAGE + pg
                    pf = pgf[pg % 2]
                    S.idma(pf[:, :], ck2, pidx[:, col:col + 1], reads=[pidx], writes=[pf])
                    S.op('dve', lambda kb=kb, pf=pf: V.tensor_copy(out=kb[:, :], in_=pf[:, :]), reads=[pf], writes=[kb])
                    for c in range(8):
                        S.op('pe', lambda c=c, kb=kb: PE.transpose(out=pT[:, c * 128:(c + 1) * 128], in_=kb[:, c * 128:(c + 1) * 128], identity=ident[:, :]),
                             reads=[kb, ident], writes=[pT])
                    S.op('act', lambda ktp=ktp: A.copy(out=ktp[:, :, :], in_=pT[:, :].rearrange("p (c n) -> p c n", c=8)), reads=[pT], writes=[ktp])
                    ps = pA[pg % 4]
                    for hh in range(8):
                        mm(ps[0:64, 0:128], qblk[:, hh, b, :], ktp[:, hh, :], hh == 0, hh == 7, [qblk, ktp], [ps])
                    S.op('dve', lambda ps=ps, pg=pg: V.tensor_copy(out=sS[:, pg * 128:(pg + 1) * 128], in_=ps[0:64, 0:128]), reads=[ps], writes=[sS])
                ps = pA[0]
                for hh in range(8):
                    mm(ps[0:64, 0:16], qblk[:, hh, b, :], kTc[:, hh, 0:16], hh == 0, hh == 7, [qblk, kTc], [ps])
                S.op('dve', lambda ps=ps, b=b: V.tensor_tensor(out=sS[:, PAST:NK], in0=ps[0:64, 0:16], in1=cmask[:, b, :], op=ALU.add), reads=[ps, cmask], writes=[sS])
                S.op('dve', lambda: V.reduce_max(out=small[0:64, 40:41], in_=sS[:, 0:NK], axis=AX.X), reads=[sS], writes=[small])
                S.op('dve', lambda: V.tensor_scalar_mul(out=small[0:64, 41:42], in0=small[0:64, 40:41], scalar1=-1.0), reads=[small], writes=[small])
                S.op('act', lambda: A.activation(out=sS[:, 0:NK], in_=sS[:, 0:NK], func=AF.Exp, bias=small[0:64, 41:42], accum_out=small[0:64, 42:43]),
                     reads=[sS, small], writes=[sS, small])
                S.op('dve', lambda: V.reciprocal(out=small[0:64, 43:44], in_=small[0:64, 42:43]), reads=[small], writes=[small])
                S.op('dve', lambda: V.tensor_scalar_mul(out=pS[:, 0:NK], in0=sS[:, 0:NK], scalar1=small[0:64, 43:44]), reads=[sS, small], writes=[pS])
                for pg in range(NPAGE + 1):
                    nk = 128 if pg < NPAGE else 16
                    S.op('pe', lambda pg=pg, nk=nk: PE.transpose(out=pT[0:nk, 0:64], in_=pS[:, pg * 128:pg * 128 + nk], identity=ident[0:64, 0:64]),
                         reads=[pS, ident], writes=[pT])
                    S.op('act', lambda nk=nk: A.copy(out=pTS[0:nk, :], in_=pT[0:nk, 0:64]), reads=[pT], writes=[pTS])
                    if pg < NPAGE:
                        vb = vpg[pg % 2]
                        col = b * NPAGE + pg
                        pf = pgf[pg % 2]
                        S.idma(pf[:, :], cv2, pidx[:, col:col + 1], reads=[pidx], writes=[pf])
                        S.op('act', lambda vb=vb, pf=pf: A.copy(out=vb[:, :], in_=pf[:, :]), reads=[pf], writes=[vb])
                    for hh in range(8):
                        bank = pO[hh // 4]
                        oap = bank[0:8, (hh % 4) * 128:(hh % 4) * 128 + 128]
                        if pg < NPAGE:
                            rhs, rd = vb[:, hh * 128:(hh + 1) * 128], vb
                        else:
                            rhs, rd = vac[0:16, 0, hh, 0:128], vac
                        mm(oap, pTS[0:nk, hh * 8:(hh + 1) * 8], rhs, pg == 0 and hh % 4 == 0, pg == NPAGE, [pTS, rd], [bank], skip=True)
                o8, od = tm[0], tm[1]
                S.op('act', lambda: A.copy(out=o8[0:8, 0:512], in_=pO[0][0:8, :]), reads=[pO[0]], writes=[o8])
                S.op('act', lambda: A.copy(out=o8[0:8, 512:1024], in_=pO[1][0:8, :]), reads=[pO[1]], writes=[o8])
                for half in range(2):
                    mm(pA[1][0:4, :], sel4[:, 0:4], o8[0:8, half * 512:(half + 1) * 512], True, True, [sel4, o8], [pA[1]])
                    S.op('act', lambda half=half: A.copy(out=od[0:4, half * 512:(half + 1) * 512], in_=pA[1][0:4, :]), reads=[pA[1]], writes=[od])
                S.dma('sp', osm_d[b * 4:(b + 1) * 4, :], od[0:4, :], reads=[od], writes=[osm_d])
        osm = tm[2]
        S.dma('sp', osm[0:16, :], osm_d[:, :], reads=[osm_d], writes=[osm])
        for hh in range(8):
            ob = fa[6]
            S.op('act', lambda hh=hh: A.activation(out=ob[0:16, 128:256], in_=osm[0:16, hh * 128:(hh + 1) * 128], func=AF.Square, accum_out=small[0:16, 32:33]), reads=[osm], writes=[ob, small])
            S.op('dve', lambda: V.tensor_scalar(out=small[0:16, 33:34], in0=small[0:16, 32:33], scalar1=1.0 / 128.0, scalar2=1e-5, op0=ALU.mult, op1=ALU.add), reads=[small], writes=[small])
            S.op('act', lambda: A.activation(out=small[0:16, 33:34], in_=small[0:16, 33:34], func=AF.Ln), reads=[small], writes=[small])
            S.op('act', lambda: A.activation(out=small[0:16, 33:34], in_=small[0:16, 33:34], func=AF.Exp, scale=-0.5), reads=[small], writes=[small])
            S.op('dve', lambda hh=hh: V.scalar_tensor_tensor(out=ob[0:16, 0:128], in0=osm[0:16, hh * 128:(hh + 1) * 128], scalar=small[0:16, 33:34], in1=subg[0:16, :], op0=ALU.mult, op1=ALU.mult),
                 reads=[osm, small, subg], writes=[ob])
            S.op('dve', lambda hh=hh: V.tensor_tensor(out=gtm[0:16, 0, hh * 128:(hh + 1) * 128], in0=ob[0:16, 0:128], in1=szb[0:16, 0, hh * 128:(hh + 1) * 128], op=ALU.mult), reads=[ob, szb], writes=[gtm])

    osm_d = S.dram("osm_d", [16, D])
    rowi = sbt("rowi", [64, 16], I32); coli = sbt("coli", [64, 16], I32); mk = sbt("mk", [64, 2, 16])
    S.op('pool', lambda: G.iota(rowi[:], pattern=[[0, 16]], base=0, channel_multiplier=1), writes=[rowi])
    S.op('dve', lambda: V.tensor_scalar(out=rowi[:], in0=rowi[:], scalar1=1, scalar2=3, op0=ALU.arith_shift_right, op1=ALU.bitwise_and), reads=[rowi], writes=[rowi])
    S.op('pool', lambda: G.iota(coli[:], pattern=[[1, 16]], base=0, channel_multiplier=0), writes=[coli])
    S.op('dve', lambda: V.tensor_single_scalar(out=mk[:, 1, :], in_=coli[:], scalar=2, op=ALU.arith_shift_right), reads=[coli], writes=[mk]) if False else None
    cb_i = sbt("cb_i", [64, 16], I32)
    S.op('dve', lambda: V.tensor_single_scalar(out=cb_i[:], in_=coli[:], scalar=2, op=ALU.arith_shift_right), reads=[coli], writes=[cb_i])
    S.op('dve', lambda: V.tensor_single_scalar(out=coli[:], in_=coli[:], scalar=3, op=ALU.bitwise_and), reads=[coli, cb_i], writes=[coli])
    S.op('dve', lambda: V.tensor_tensor(out=coli[:], in0=coli[:], in1=rowi[:], op=ALU.subtract), reads=[coli, rowi], writes=[coli])
    S.op('dve', lambda: V.tensor_copy(out=mk[:, 0, :], in_=coli[:]), reads=[coli], writes=[mk])
    S.op('dve', lambda: V.tensor_copy(out=mk[:, 1, :], in_=cb_i[:]), reads=[cb_i], writes=[mk])
    S.op('dve', lambda: V.tensor_single_scalar(out=mk[:, 0, :], in_=mk[:, 0, :], scalar=0.0, op=ALU.is_le), reads=[mk], writes=[mk])
    for b in range(4):
        S.op('dve', lambda b=b: V.tensor_single_scalar(out=cmask[:, b, :], in_=mk[:, 1, :], scalar=float(b), op=ALU.is_equal), reads=[mk], writes=[cmask])
        S.op('dve', lambda b=b: V.tensor_tensor(out=cmask[:, b, :], in0=cmask[:, b, :], in1=mk[:, 0, :], op=ALU.mult), reads=[mk, cmask], writes=[cmask])
        S.op('dve', lambda b=b: V.tensor_scalar(out=cmask[:, b, :], in0=cmask[:, b, :], scalar1=-1.0, scalar2=30000.0, op0=ALU.add, op1=ALU.mult), reads=[cmask], writes=[cmask])
    seli = sbt("seli", [8, 8], I32); self_ = sbt("self_", [8, 8]); sel4 = sbt("sel4", [8, 4])
    S.op('pool', lambda: G.iota(seli[:, 0:4], pattern=[[-2, 4]], base=0, channel_multiplier=1), writes=[seli])
    S.op('pool', lambda: G.iota(seli[:, 4:8], pattern=[[-2, 4]], base=-1, channel_multiplier=1), writes=[seli])
    S.op('dve', lambda: V.tensor_copy(out=self_[:], in_=seli[:]), reads=[seli], writes=[self_])
    S.op('dve', lambda: V.tensor_single_scalar(out=self_[:], in_=self_[:], scalar=0.0, op=ALU.is_equal), reads=[self_], writes=[self_])
    S.op('dve', lambda: V.scalar_tensor_tensor(out=sel4[:], in0=self_[:, 4:8], scalar=lam[0:8, 0:1], in1=self_[:, 0:4], op0=ALU.mult, op1=ALU.add), reads=[self_, lam], writes=[sel4])

    def run_layers(sample, ci):
        ntiles = 1 if sample else 4
        ntok = 16 if sample else 128
        N = 16 if sample else 512
        xt = Xs if sample else X
        last = (ci == NCH - 1)
        for l in range(4):
            kind, j = l % 3, l // 3
            if sample:
                adaln_layer(l)
                bcast_layer(l, False)
                if kind == 0:
                    rows_to_fm(st_conv[j, :, :], 8, lambda c: stT[:, c, 0:8], stT)
                elif kind == 1:
                    rows_to_fm(st_lc[:, :], 12, lambda c: stT[:, c, 0:12], stT)
                    rows_to_fm(st_h[:, :], 4, lambda c: stT2[:, c, :], stT2)
                make_uT(l, Xs[:, :], 16, 0, True)
            else:
                bcast_layer(l, True)
                if l == 0:
                    for t in range(4):
                        make_uT(l, X[:, t, :], 128, t * 128, False)
            if kind == 0:
                conv_layer(l, j, N, sample, last)
                w_out_d = ('a_out', j)
            elif kind == 1:
                lru_layer(l, N, sample, last)
                w_out_d = ('r_out', 0)
            else:
                if not sample and ci > 0:
                    kv_load(ci, 0)
                attn_project(ntiles, ntok, ci * 4, sample)
                if sample:
                    attn_sample()
                else:
                    attn_prompt(ci)
                g_to_gT(ntiles, ntok)
                w_out_d = ('d_out', 0)
            if sample:
                out_proj_ln(l, w_out_d, 1, 16, Xs, lambda t: Xs[:, :], True)
            else:
                nxt = (lambda t, l=l: make_uT(l + 1, X[:, t, :], 128, t * 128, False)) if l < 3 else None
                out_proj_ln(l, w_out_d, 4, 128, X, lambda t: X[:, t, :], False, after_tile=nxt)

    S.op('pool', lambda: G.memset(vac[:], 1.0), writes=[vac])
    S.dma('sp', Xs[:, :], xs[:, :], writes=[Xs])
    run_layers(True, 0)
    S.dma('sp', ys[:, :], Xs[:, :], reads=[Xs], writes=[])
    S.barrier()
    stack.close()
    X = S.sb("X", [128, 4, D])
    fa = [S.sb("fa%d" % i, [128, 520]) for i in range(8)]
    uT = S.sb("uT", [128, 8, 512], BF16); gT = S.sb("gT", [128, 8, 512], BF16)
    qT = S.sb("qT", [128, 8, 512], BF16); kTc = S.sb("kTc", [128, 8, 512], BF16)
    vac = S.sb("vac", [128, 4, 8, 129], BF16)
    szb = S.sb("szb", [128, 4, D], BF16); gtm = S.sb("gtm", [128, 4, D], BF16)
    kTh = [S.sb("kTh%d" % i, [128, SEQ], BF16) for i in range(2)]
    vh = [S.sb("vh%d" % i, [128, SEQ // 128, 129], BF16) for i in range(2)]
    pTs = [S.sb("pTs%d" % i, [128, 512], BF16) for i in range(3)]
    S.op('pool', lambda: G.memset(vac[:], 1.0), writes=[vac])
    for i in range(2):
        S.op('pool', lambda i=i: G.memset(vh[i][:], 1.0), writes=[vh[i]])
    for ci in range(NCH):
        S.dma('sp', X[:, :, :], xp[ci * 512:(ci + 1) * 512, :].rearrange("(t p) d -> p t d", p=128), writes=[X])
        run_layers(False, ci)
        S.dma('pool', yp[ci * 512:(ci + 1) * 512, :].rearrange("(t p) d -> p t d", p=128), X[:, :, :], reads=[X], writes=[])
    S.finish()
    return nc


def _run(inp, SEQ, NPAGE, NPHYS, PAST):
    f = lambda a: np.ascontiguousarray(np.asarray(a), dtype=np.float32)
    nc = build(SEQ, NPAGE, NPHYS, PAST)
    ck = f(inp['cache_k'][0]).reshape(NPHYS, 128, D)
    cv = f(inp['cache_v'][0]).reshape(NPHYS, 128, D)
    shared = dict(
        ck=ck, cv=cv,
        w_ada=f(inp['w_ada']), b_ada=f(inp['b_ada']), ln_g=f(inp['ln_g']), ln_b=f(inp['ln_b']),
        a_w_in=f(inp['a_w_in']), a_conv_w=f(inp['a_conv_w']), a_w_out=f(inp['a_w_out']),
        r_w_in=f(inp['r_w_in']), r_conv_w=f(inp['r_conv_w']), r_conv_b=f(inp['r_conv_b']),
        r_w_ga=f(inp['r_w_ga']), r_b_ga=f(inp['r_b_ga']), r_w_gx=f(inp['r_w_gx']), r_b_gx=f(inp['r_b_gx']),
        r_lru=f(inp['r_lru_param']), r_w_out=f(inp['r_w_out']),
        d_w_in=f(inp['d_w_in']),
        d_lam=np.stack([f(inp['d_lq1'])[0], f(inp['d_lk1'])[0], f(inp['d_lq2'])[0], f(inp['d_lk2'])[0]], 0),
        d_subg=f(inp['d_subln_g']), d_w_out=f(inp['d_w_out']),
    )
    xp_, xs_ = f(inp['x_prompt']), f(inp['x_sample'])
    pt = np.ascontiguousarray(np.asarray(inp['page_table']), dtype=np.int32)
    in_maps = []
    for c in range(8):
        b = c // 2
        sl = slice(4 * c, 4 * c + 4)
        m = dict(shared)
        m.update(
            xp=xp_[b], xs=xs_[sl].reshape(16, D),
            cp=f(inp['c_prompt'])[b:b + 1], cs=f(inp['c_sample'])[sl],
            st_conv=f(inp['state_conv_a'])[:, sl].reshape(2, 8, D),
            st_h=f(inp['state_lru_h'])[0, sl], st_lc=f(inp['state_lru_conv'])[0, sl].reshape(12, D),
            ptab=pt[sl].reshape(1, 4 * NPAGE),
        )
        in_maps.append(m)
    res = run_bass_kernel_spmd(nc, in_maps, core_ids=list(range(8)))
    R = res.results
    ev = [R[2 * b] for b in range(4)]
    y_p = np.stack([r['yp'] for r in ev], 0)
    y_s = np.concatenate([r['ys'].reshape(4, 4, D) for r in R], 0)
    conv_p = np.stack([r['conv_p'] for r in ev], 1)
    conv_s = np.concatenate([r['conv_s'].reshape(2, 4, 2, D) for r in R], 1)
    lruh_p = np.stack([r['lruh_p'][0] for r in ev], 0)[None]
    lruh_s = np.concatenate([r['lruh_s'] for r in R], 0)[None]
    lruc_p = np.stack([r['lruc_p'] for r in ev], 0)[None]
    lruc_s = np.concatenate([r['lruc_s'].reshape(4, 3, D) for r in R], 0)[None]
    k_p = np.stack([r['kp'].reshape(SEQ, 16, 64) for r in ev], 0)[None]
    v_p = np.stack([r['vp'].reshape(SEQ, 8, 128) for r in ev], 0)[None]
    k_s = np.concatenate([r['ks'].reshape(4, 4, 16, 64) for r in R], 0)[None]
    v_s = np.concatenate([r['vs'].reshape(4, 4, 8, 128) for r in R], 0)[None]
    outs = (y_p, y_s, conv_p, conv_s, lruh_p, lruh_s, lruc_p, lruc_s, k_p, v_p, k_s, v_s)
    return tuple(np.ascontiguousarray(o, dtype=np.float32) for o in outs)


def kernel(**inputs):
    return _run(inputs, 4096, 64, 2560, 8192)
```

```python
import numpy as np
import concourse.bass as bass
import concourse.mybir as mybir

F32 = mybir.dt.float32
BF16 = mybir.dt.bfloat16
I32 = mybir.dt.int32
ALU = mybir.AluOpType
AF = mybir.ActivationFunctionType
AX = mybir.AxisListType


class Buf:
    def __init__(self, name):
        self.name = name
        self.wc = {}
        self.wd = []
        self.rc = {}
        self.rd = []


class T:
    def __init__(self, h, name):
        self.h = h
        self.b = Buf(name)

    def __getitem__(self, k):
        return self.h[k]

    def ap(self):
        return self.h.ap()


class Sched:
    EP = 4000
    NDS = 40

    def __init__(self, nc):
        self.nc = nc
        self.eng = dict(pe=nc.tensor, act=nc.scalar, dve=nc.vector, pool=nc.gpsimd, sp=nc.sync)
        self.cnt = {e: 0 for e in self.eng}
        self.esems = {e: [] for e in self.eng}
        self.seen = {e: {} for e in self.eng}
        self.dsems = [nc.alloc_semaphore("dq%d" % i) for i in range(self.NDS)]
        self.duse = [0] * self.NDS
        self.dn = 0
        self.nwaits = 0

    def sb(self, name, shape, dt=F32):
        return T(self.nc.alloc_sbuf_tensor(name, list(shape), dt), name)

    def ps(self, name, shape, dt=F32):
        return T(self.nc.alloc_psum_tensor(name, list(shape), dt), name)

    def dram(self, name, shape, dt=F32, kind="Internal"):
        return T(self.nc.dram_tensor(name, list(shape), dt, kind=kind), name)

    def _wait(self, e, tok):
        if tok[0] == 'c':
            _, f, c = tok
            if self.seen[e].get(('c', f), 0) >= c:
                return
            ep, v = (c - 1) // self.EP, (c - 1) % self.EP + 1
            self.eng[e].wait_ge(self.esems[f][ep], v)
            self.seen[e][('c', f)] = c
        else:
            _, i, v = tok
            if self.seen[e].get(('d', i), 0) >= v:
                return
            self.eng[e].wait_ge(self.dsems[i], v)
            self.seen[e][('d', i)] = v
        self.nwaits += 1

    def _deps(self, e, reads, writes):
        for b in reads:
            for f, c in b.wc.items():
                self._wait(e, ('c', f, c))
            for t in b.wd:
                self._wait(e, t)
        for b in writes:
            for f, c in b.rc.items():
                if f != e:
                    self._wait(e, ('c', f, c))
            for t in b.rd:
                self._wait(e, t)
            for f, c in b.wc.items():
                if f != e:
                    self._wait(e, ('c', f, c))
            for t in b.wd:
                self._wait(e, t)

    def _record(self, tok, reads, writes):
        for b in reads:
            if tok[0] == 'c':
                b.rc[tok[1]] = max(b.rc.get(tok[1], 0), tok[2])
            else:
                b.rd.append(tok)
        for b in writes:
            if b.rc or b.rd:
                b.rc = {}
                b.rd = []
                b.wc = {}
                b.wd = []
            if tok[0] == 'c':
                b.wc[tok[1]] = max(b.wc.get(tok[1], 0), tok[2])
            else:
                b.wd.append(tok)

    @staticmethod
    def _bufs(xs):
        out = []
        for x in xs:
            if x is None:
                continue
            out.append(x.b if isinstance(x, T) else x)
        return out

    def op(self, e, fn, reads=(), writes=()):
        reads = self._bufs(reads)
        writes = self._bufs(writes)
        self._deps(e, reads, writes)
        ins = fn()
        self.cnt[e] += 1
        c = self.cnt[e]
        ep = (c - 1) // self.EP
        while len(self.esems[e]) <= ep:
            self.esems[e].append(self.nc.alloc_semaphore("c_%s_%d" % (e, len(self.esems[e]))))
        ins.then_inc(self.esems[e][ep], 1)
        tok = ('c', e, c)
        self._record(tok, reads, writes)
        return tok

    def dma(self, q, out, in_, reads=(), writes=(), **kw):
        reads = self._bufs(reads)
        writes = self._bufs(writes)
        self._deps(q, reads, writes)
        i = self.dn % self.NDS
        self.dn += 1
        self.duse[i] += 1
        v = 16 * self.duse[i]
        if v > 16:
            self._wait(q, ('d', i, v - 16))
        ins = self.eng[q].dma_start(out=out, in_=in_, **kw)
        ins.then_inc(self.dsems[i], 16)
        tok = ('d', i, v)
        self._record(tok, reads, writes)
        return tok

    def idma(self, out, in_, idx_ap, reads=(), writes=()):
        return self.coll(lambda: self.nc.gpsimd.indirect_dma_start(
            out=out, out_offset=None, in_=in_, in_offset=bass.IndirectOffsetOnAxis(ap=idx_ap, axis=0)), reads=reads, writes=writes)

    def coll(self, fn, reads=(), writes=()):
        q = 'pool'
        reads = self._bufs(reads)
        writes = self._bufs(writes)
        self._deps(q, reads, writes)
        i = self.dn % self.NDS
        self.dn += 1
        self.duse[i] += 1
        v = 16 * self.duse[i]
        if v > 16:
            self._wait(q, ('d', i, v - 16))
        ins = fn()
        ins.then_inc(self.dsems[i], 16)
        tok = ('d', i, v)
        self._record(tok, reads, writes)
        return tok

    def barrier(self):
        for e in ('pe', 'act', 'dve', 'pool', 'sp'):
            for i in range(self.NDS):
                if self.duse[i]:
                    self._wait(e, ('d', i, 16 * self.duse[i]))
            for f in ('pe', 'act', 'dve', 'pool'):
                if self.cnt[f] and f != e:
                    self._wait(e, ('c', f, self.cnt[f]))

    def finish(self):
        for i in range(self.NDS):
            if self.duse[i]:
                self._wait('sp', ('d', i, 16 * self.duse[i]))
        for f in ('pe', 'act', 'dve', 'pool'):
            if self.cnt[f]:
                self._wait('sp', ('c', f, self.cnt[f]))
import math
from concourse.bass_utils import run_bass_kernel_spmd


D = 1024
ALPHA = (2 * 4) ** 0.25
LN_EPS = 1e-5
ROT = 16
HALF = 8
THETA = 500000.0
TWO_PI = 2.0 * math.pi


def build(SEQ, NPAGE, NPHYS, PAST):
    NCH = SEQ // 512
    nc = bass.Bass("TRN2", target_bir_lowering=False)
    S = Sched(nc)
    V, A, G, PE = nc.vector, nc.scalar, nc.gpsimd, nc.tensor

    def DI(name, shape, dt=F32):
        return S.dram(name, shape, dt, kind="ExternalInput")

    def DO(name, shape, dt=F32):
        return S.dram(name, shape, dt, kind="ExternalOutput")

    xp = DI("xp", [SEQ, D]); xs = DI("xs", [16, D])
    cp = DI("cp", [1, D]); cs = DI("cs", [4, D])
    st_conv = DI("st_conv", [2, 8, D])
    st_h = DI("st_h", [4, D]); st_lc = DI("st_lc", [12, D])
    ck = DI("ck", [NPHYS, 128, D]); cv = DI("cv", [NPHYS, 128, D])
    ptab = DI("ptab", [1, 4 * NPAGE], I32)
    w_ada = DI("w_ada", [4, D, 3 * D]); b_ada = DI("b_ada", [4, 3 * D])
    ln_g = DI("ln_g", [4, D]); ln_b = DI("ln_b", [4, D])
    a_w_in = DI("a_w_in", [2, D, 4 * D]); a_conv_w = DI("a_conv_w", [2, 3, D]); a_w_out = DI("a_w_out", [2, D, D])
    r_w_in = DI("r_w_in", [1, D, 2 * D]); r_conv_w = DI("r_conv_w", [1, 4, D]); r_conv_b = DI("r_conv_b", [1, D])
    r_w_ga = DI("r_w_ga", [1, 4, 256, 256]); r_b_ga = DI("r_b_ga", [1, D])
    r_w_gx = DI("r_w_gx", [1, 4, 256, 256]); r_b_gx = DI("r_b_gx", [1, D])
    r_lru = DI("r_lru", [1, D]); r_w_out = DI("r_w_out", [1, D, D])
    d_w_in = DI("d_w_in", [1, D, 4 * D]); d_lam = DI("d_lam", [4, 64])
    d_subg = DI("d_subg", [1, 128]); d_w_out = DI("d_w_out", [1, D, D])

    yp = DO("yp", [SEQ, D]); ys = DO("ys", [16, D])
    conv_p = DO("conv_p", [2, 2, D]); conv_s = DO("conv_s", [2, 8, D])
    lruh_p = DO("lruh_p", [1, D]); lruh_s = DO("lruh_s", [4, D])
    lruc_p = DO("lruc_p", [3, D]); lruc_s = DO("lruc_s", [12, D])
    kp = DO("kp", [SEQ, D]); vp = DO("vp", [SEQ, D]); ks = DO("ks", [16, D]); vs = DO("vs", [16, D])
    kT_d = S.dram("kT_d", [8, 128, SEQ], BF16)
    v_d = S.dram("v_d", [8, 128, SEQ // 128, 129], BF16)

    ident = S.sb("ident", [128, 128], BF16); identf = S.sb("identf", [128, 128])
    tri = S.sb("tri", [128, 128], BF16); onesf = S.sb("onesf", [128, 128])
    xbf = S.sb("xbf", [128, D], BF16)
    wslot = [S.sb("w%d" % i, [128, 8, 512], BF16) for i in range(4)]
    modp = S.sb("modp", [128, 4, 24])
    badaT = S.sb("badaT", [128, 4, 24])
    lngT = S.sb("lngT", [128, 4, 8]); lnbT = S.sb("lnbT", [128, 4, 8])
    bc = S.sb("bc", [128, 3, D])
    scT = S.sb("scT", [128, 8, 17])
    fb = [S.sb("fb%d" % i, [128, 512], BF16) for i in range(2)]
    fz = [S.sb("fz%d" % i, [128, 512], BF16) for i in range(2)]
    tm = [S.sb("tm%d" % i, [128, D]) for i in range(3)]
    small = S.sb("small", [128, 64])
    cw_a = S.sb("cw_a", [128, 2, 3, 8]); halo_a = S.sb("halo_a", [128, 2, 8, 2])
    cw_r = S.sb("cw_r", [128, 4, 8]); cb_r = S.sb("cb_r", [128, 8]); halo_r = S.sb("halo_r", [128, 8, 3])
    bga = S.sb("bga", [128, 8]); bgx = S.sb("bgx", [128, 8]); clam = S.sb("clam", [128, 8]); clam2 = S.sb("clam2", [128, 8])
    hst = S.sb("hst", [128, 8])
    stT = S.sb("stT", [128, 8, 12]); sttm = tm[2]
    cos_t = S.sb("cos_t", [128, SEQ // 128, HALF]); sin_t = S.sb("sin_t", [128, SEQ // 128, HALF])
    cos_s = S.sb("cos_s", [16, HALF]); sin_s = S.sb("sin_s", [16, HALF])
    lam = S.sb("lam", [128, 2]); subg = S.sb("subg", [128, 128])
    qf = S.sb("qf", [128, D]); kf = S.sb("kf", [128, D]); vf = S.sb("vf", [128, D])
    pTS = S.sb("pTS", [128, 64], BF16)
    import contextlib
    wdefs = {}
    kp_ = lambda ap: ap.rearrange("(k p) n -> p k n", p=128)
    for j in range(2):
        for c in range(8):
            wdefs[("a_in", j, c)] = [(lambda w, g4=g4: w[:, :, g4 * 128:(g4 + 1) * 128], kp_(a_w_in[j, :, g4 * D + c * 128:g4 * D + (c + 1) * 128])) for g4 in range(4)]
        for nb in range(2):
            wdefs[(("a_out", j), nb)] = [(lambda w: w[:, :, :], kp_(a_w_out[j, :, nb * 512:(nb + 1) * 512]))]
    for nb in range(4):
        wdefs[("r_in", nb)] = [(lambda w, gz=gz: w[:, :, gz * 256:(gz + 1) * 256], kp_(r_w_in[0, :, gz * D + nb * 256:gz * D + (nb + 1) * 256])) for gz in range(2)]
        wdefs[("r_g", nb)] = [(lambda w: w[:, 0:2, 0:256], kp_(r_w_ga[0, nb])), (lambda w: w[:, 2:4, 0:256], kp_(r_w_gx[0, nb]))]
    for nb in range(2):
        wdefs[(("r_out", 0), nb)] = [(lambda w: w[:, :, :], kp_(r_w_out[0, :, nb * 512:(nb + 1) * 512]))]
        wdefs[(("d_out", 0), nb)] = [(lambda w: w[:, :, :], kp_(d_w_out[0, :, nb * 512:(nb + 1) * 512]))]
    for blk in range(8):
        wdefs[("d_in", blk)] = [(lambda w: w[:, :, :], kp_(d_w_in[0, :, blk * 512:(blk + 1) * 512]))]
    widx = {k: i for i, k in enumerate(wdefs)}
    wbuf = [Buf("wsc%d" % i) for i in range(len(wdefs))]
    wsc = S.dram("wsc", [len(wdefs), 128, 4096], BF16)

    def layer_keys(l):
        kind, j = l % 3, l // 3
        if kind == 0:
            return [("a_in", j, c) for c in range(8)] + [(("a_out", j), nb) for nb in range(2)]
        if kind == 1:
            ks_ = []
            for nb in range(4):
                ks_ += [("r_in", nb), ("r_g", nb)]
            return ks_ + [(("r_out", 0), nb) for nb in range(2)]
        return [("d_in", blk) for blk in range(8)] + [(("d_out", 0), nb) for nb in range(2)]
    WSEQ = []
    for _rep in range(1 + NCH):
        for l in range(4):
            WSEQ += layer_keys(l)

    for key, parts in wdefs.items():
        dst3 = wsc[widx[key], :, :].rearrange("p (k n) -> p k n", k=8)
        for (dfn, src) in parts:
            S.dma('pool', dfn(dst3), src, writes=[wbuf[widx[key]]])
    stack = contextlib.ExitStack()

    def sbt(name, shape, dt=F32):
        return T(stack.enter_context(nc.sbuf_tensor(name, list(shape), dt)), name)

    uT = sbt("uT_s", [128, 8, 16], BF16); gT = sbt("gT_s", [128, 8, 16], BF16)
    qT = sbt("qT_s", [128, 8, 16], BF16); kTc = sbt("kTc_s", [128, 8, 16], BF16)
    vac = sbt("vac_s", [16, 1, 8, 129], BF16)
    szb = sbt("szb_s", [16, 1, D], BF16); gtm = sbt("gtm_s", [16, 1, D], BF16)
    X = None; kTh = None; vh = None; pTs = None
    mods = sbt("mods", [16, 3 * D]); c17 = tm[0]
    Xs = sbt("Xs", [16, D]); badar = sbt("badar", [16, 512])
    wadaf = [sbt("wadaf0", [128, 8, 256])]
    fa = [sbt("fa%d_s" % i, [128, 264]) for i in range(8)]
    ptab_sb = sbt("ptab_sb", [128, 4 * NPAGE], I32); pidx = sbt("pidx", [128, 4 * NPAGE], I32)
    pgf = [sbt("pgf%d" % i, [128, D]) for i in range(2)]
    pidf = sbt("pidf", [128, 2, 4 * NPAGE])
    kpg = [sbt("kpg%d" % i, [128, D], BF16) for i in range(2)]
    vpg = [sbt("vpg%d" % i, [128, D], BF16) for i in range(2)]
    kTp = [sbt("kTp%d" % i, [128, 8, 128], BF16) for i in range(2)]
    qblk = sbt("qblk", [128, 8, 4, 64], BF16)
    sS = sbt("sS", [64, PAST + 16])
    pS = sbt("pS", [64, PAST + 16], BF16)
    cmask = sbt("cmask", [64, 4, 16])

    pA = [S.ps("pA%d" % i, [128, 512]) for i in range(4)]
    pO = [S.ps("pO%d" % i, [128, 512]) for i in range(2)]
    pT = S.ps("pT", [128, 1024], BF16)
    pX = S.ps("pX", [128, 512])
    pTf = T(pT.h.bitcast(F32), "pTf"); pTf.b = pT.b

    def vec_fm(dst_ap, src_row_ap, rd, wr):
        S.dma('sp', dst_ap, src_row_ap.rearrange("(c p) -> p c", p=128), writes=[wr], reads=rd,
              allow_slow_non_contiguous=True)

    it = sbt("it", [128, 128], I32)
    S.op('pool', lambda: G.iota(it[:], pattern=[[1, 128]], base=0, channel_multiplier=-1), writes=[it])
    S.op('dve', lambda: V.tensor_copy(out=identf[:], in_=it[:]), reads=[it], writes=[identf])
    S.op('dve', lambda: V.tensor_single_scalar(out=tm[0][:, 0:128], in_=identf[:], scalar=0.0, op=ALU.is_ge), reads=[identf], writes=[tm[0]])
    S.op('dve', lambda: V.tensor_copy(out=tri[:], in_=tm[0][:, 0:128]), reads=[tm[0]], writes=[tri])
    S.op('dve', lambda: V.tensor_single_scalar(out=identf[:], in_=identf[:], scalar=0.0, op=ALU.is_equal), reads=[identf], writes=[identf])
    S.op('dve', lambda: V.tensor_copy(out=ident[:], in_=identf[:]), reads=[identf], writes=[ident])
    S.op('pool', lambda: G.memset(halo_a[:], 0.0), writes=[halo_a])
    S.op('pool', lambda: G.memset(halo_r[:], 0.0), writes=[halo_r])
    S.op('pool', lambda: G.memset(hst[:], 0.0), writes=[hst])

    pos_i = sbt("pos_i", [128, SEQ // 128], I32)
    S.op('pool', lambda: G.iota(pos_i[:], pattern=[[128, SEQ // 128]], base=0, channel_multiplier=1), writes=[pos_i])
    posf = fa[0]
    S.op('dve', lambda: V.tensor_copy(out=posf[:, 0:SEQ // 128], in_=pos_i[:]), reads=[pos_i], writes=[posf])
    NTL = SEQ // 128

    def rope_tables(cosd, sind, pos_ap, npart, n):
        for i in range(HALF):
            inv = math.exp(-2.0 * math.log(THETA) * i / ROT) / TWO_PI
            for (dst, off) in ((sind, 0.0), (cosd, 0.25)):
                S.op('dve', lambda dst=dst, off=off, inv=inv, i=i: V.tensor_scalar(
                    out=dst[0:npart, :, i] if n > 1 else dst[0:npart, i:i + 1], in0=pos_ap, scalar1=inv, scalar2=off,
                    op0=ALU.mult, op1=ALU.add), reads=[posf], writes=[dst])
        for dst in (sind, cosd):
            full = dst[0:npart, :, :] if n > 1 else dst[0:npart, :]
            ki = rki[0:npart, 0:n * HALF] if n == 1 else rki[0:npart, 0:n * HALF].rearrange("p (a b) -> p a b", b=HALF)
            kf = rkf[0:npart, 0:n * HALF] if n == 1 else rkf[0:npart, 0:n * HALF].rearrange("p (a b) -> p a b", b=HALF)
            S.op('dve', lambda full=full, ki=ki: V.tensor_copy(out=ki, in_=full), reads=[dst], writes=[rki])
            S.op('dve', lambda ki=ki, kf=kf: V.tensor_copy(out=kf, in_=ki), reads=[rki], writes=[rkf])
            S.op('dve', lambda full=full, kf=kf: V.tensor_tensor(out=full, in0=full, in1=kf, op=ALU.subtract), reads=[dst, rkf], writes=[dst])
            S.op('dve', lambda full=full, kf=kf: V.tensor_single_scalar(out=kf, in_=full, scalar=0.5, op=ALU.is_gt), reads=[dst], writes=[rkf])
            S.op('dve', lambda full=full, kf=kf: V.tensor_tensor(out=full, in0=full, in1=kf, op=ALU.subtract), reads=[dst, rkf], writes=[dst])
            S.op('dve', lambda full=full, kf=kf: V.tensor_single_scalar(out=kf, in_=full, scalar=-0.5, op=ALU.is_lt), reads=[dst], writes=[rkf])
            S.op('dve', lambda full=full, kf=kf: V.tensor_tensor(out=full, in0=full, in1=kf, op=ALU.add), reads=[dst, rkf], writes=[dst])
            S.op('act', lambda full=full: A.activation(out=full, in_=full, func=AF.Sin, scale=TWO_PI * 0.999999), reads=[dst], writes=[dst])

    rki = sbt("rki", [128, NTL * HALF], I32); rkf = sbt("rkf", [128, NTL * HALF])
    rope_tables(cos_t, sin_t, posf[:, 0:NTL], 128, NTL)
    pos_s = sbt("pos_s", [16, 1], I32)
    S.op('pool', lambda: G.iota(pos_s[:], pattern=[[0, 1]], base=0, channel_multiplier=1), writes=[pos_s])
    S.op('dve', lambda: V.tensor_single_scalar(out=pos_s[:], in_=pos_s[:], scalar=3, op=ALU.bitwise_and), reads=[pos_s], writes=[pos_s])
    S.op('dve', lambda: V.tensor_copy(out=posf[0:16, 0:1], in_=pos_s[:]), reads=[pos_s, cos_t, sin_t], writes=[posf])
    S.op('dve', lambda: V.tensor_scalar_add(out=posf[0:16, 0:1], in0=posf[0:16, 0:1], scalar1=float(PAST)), reads=[posf], writes=[posf])
    rope_tables(cos_s, sin_s, posf[0:16, 0:1], 16, 1)

    for l in range(4):
        vec_fm(lngT[:, l, :], ln_g[l, :], [], lngT)
        vec_fm(lnbT[:, l, :], ln_b[l, :], [], lnbT)
        for g3 in range(3):
            vec_fm(badaT[:, l, 8 * g3:8 * g3 + 8], b_ada[l, g3 * D:(g3 + 1) * D], [], badaT)
    for j in range(2):
        for k in range(3):
            vec_fm(cw_a[:, j, k, :], a_conv_w[j, k, :], [], cw_a)
    for k in range(4):
        vec_fm(cw_r[:, k, :], r_conv_w[0, k, :], [], cw_r)
    vec_fm(cb_r[:], r_conv_b[0, :], [], cb_r)
    vec_fm(bga[:], r_b_ga[0, :], [], bga)
    vec_fm(bgx[:], r_b_gx[0, :], [], bgx)
    vec_fm(clam[:], r_lru[0, :], [], clam)
    S.op('act', lambda: A.activation(out=clam[:], in_=clam[:], func=AF.Exp, scale=-1.0), reads=[clam], writes=[clam])
    S.op('act', lambda: A.activation(out=clam[:], in_=clam[:], func=AF.Ln, bias=1.0), reads=[clam], writes=[clam])
    S.op('dve', lambda: V.tensor_scalar_mul(out=clam2[:], in0=clam[:], scalar1=-16.0), reads=[clam], writes=[clam2])
    S.op('dve', lambda: V.tensor_scalar_mul(out=clam[:], in0=clam[:], scalar1=-8.0), reads=[clam, clam2], writes=[clam])
    lam_init = 0.8 - 0.6 * math.exp(-0.3 * 2)
    lq = sbt("lq", [128, 4, 64])
    S.dma('sp', lq[:], d_lam.ap().rearrange("(o a) d -> o a d", o=1).broadcast_to([128, 4, 64]), writes=[lq])
    S.op('dve', lambda: V.tensor_tensor(out=lq[:, 0, :], in0=lq[:, 0, :], in1=lq[:, 1, :], op=ALU.mult), reads=[lq], writes=[lq])
    S.op('dve', lambda: V.tensor_tensor(out=lq[:, 2, :], in0=lq[:, 2, :], in1=lq[:, 3, :], op=ALU.mult), reads=[lq], writes=[lq])
    S.op('dve', lambda: V.tensor_reduce(out=small[:, 0:1], in_=lq[:, 0, :], axis=AX.X, op=ALU.add), reads=[lq], writes=[small])
    S.op('dve', lambda: V.tensor_reduce(out=small[:, 1:2], in_=lq[:, 2, :], axis=AX.X, op=ALU.add), reads=[lq], writes=[small])
    S.op('act', lambda: A.activation(out=small[:, 0:2], in_=small[:, 0:2], func=AF.Exp), reads=[small], writes=[small])
    S.op('dve', lambda: V.tensor_tensor(out=lam[:, 0:1], in0=small[:, 0:1], in1=small[:, 1:2], op=ALU.subtract), reads=[small], writes=[lam])
    S.op('dve', lambda: V.tensor_scalar(out=lam[:, 0:1], in0=lam[:, 0:1], scalar1=lam_init, scalar2=-1.0, op0=ALU.add, op1=ALU.mult), reads=[lam], writes=[lam])
    S.dma('sp', subg[:], d_subg[0:1, :].broadcast_to([128, 128]), writes=[subg])
    S.op('dve', lambda: V.tensor_scalar_mul(out=subg[:], in0=subg[:], scalar1=1.0 - lam_init), reads=[subg], writes=[subg])

    NSLOT = 4
    wstate = dict(wp=0, issued=0)

    def wload(key):
        assert WSEQ[wstate['wp']] == key, (WSEQ[wstate['wp']], key)
        while wstate['issued'] < min(len(WSEQ), wstate['wp'] + NSLOT - 1):
            k2 = WSEQ[wstate['issued']]
            idx = widx[k2]
            w = wslot[wstate['issued'] % NSLOT]
            S.dma('sp', w[:].rearrange("p k n -> p (k n)"), wsc[idx, :, :], reads=[wbuf[idx]], writes=[w])
            wstate['issued'] += 1
        w = wslot[wstate['wp'] % NSLOT]
        wstate['wp'] += 1
        return w

    def mm(out_ap, lhsT, rhs, start, stop, reads, writes, skip=False):
        if skip:
            S.op('pe', lambda: PE.matmul(out_ap, lhsT, rhs, start=start, stop=stop, skip_group_check=True), reads=reads, writes=writes)
        else:
            S.op('pe', lambda: PE.matmul(out_ap, lhsT, rhs, start=start, stop=stop), reads=reads, writes=writes)

    S.dma('sp', c17[0:1, :], cp[0:1, :], writes=[c17])
    for b in range(4):
        S.dma('sp', c17[1 + 4 * b:5 + 4 * b, :], cs[b:b + 1, :].broadcast_to([4, D]), writes=[c17])
    S.op('act', lambda: A.activation(out=c17[0:17, :], in_=c17[0:17, :], func=AF.Silu), reads=[c17], writes=[c17])
    for c in range(8):
        S.op('pe', lambda c=c: PE.transpose(out=pX[:, 0:17], in_=c17[0:17, c * 128:(c + 1) * 128], identity=identf[0:17, 0:17]),
             reads=[c17, identf], writes=[pX])
        S.op('dve', lambda c=c: V.tensor_copy(out=scT[:, c, :], in_=pX[:, 0:17]), reads=[pX], writes=[scT])

    def adaln_layer(l):
        for nb in range(12):
            wf = wadaf[0]
            S.dma('sp', wf[:], w_ada[l, :, nb * 256:(nb + 1) * 256].rearrange("(k p) n -> p k n", p=128), writes=[wf])
            S.dma('sp', badar[:, 0:256], b_ada[l:l + 1, nb * 256:(nb + 1) * 256].broadcast_to([16, 256]), writes=[badar])
            for k in range(8):
                mm(pO[0][0:16, 0:256], scT[:, k, 1:17], wf[:, k, :], k == 0, k == 7, [scT, wf], [pO[0]])
            S.op('dve', lambda nb=nb: V.tensor_tensor(out=mods[:, nb * 256:(nb + 1) * 256], in0=pO[0][0:16, 0:256], in1=badar[:, 0:256], op=ALU.add),
                 reads=[pO[0], badar], writes=[mods])
            for cc in range(2):
                for k in range(8):
                    mm(pX[:, cc:cc + 1], wf[:, k, cc * 128:(cc + 1) * 128], scT[:, k, 0:1], k == 0, k == 7, [scT, wf], [pX])
            S.op('dve', lambda nb=nb: V.tensor_tensor(out=modp[:, l, nb * 2:nb * 2 + 2], in0=pX[:, 0:2], in1=badaT[:, l, nb * 2:nb * 2 + 2], op=ALU.add),
                 reads=[pX, badaT], writes=[modp])
        S.op('dve', lambda: V.tensor_scalar_add(out=modp[:, l, 8:16], in0=modp[:, l, 8:16], scalar1=1.0), reads=[modp], writes=[modp])
        S.op('dve', lambda: V.tensor_scalar_add(out=mods[:, D:2 * D], in0=mods[:, D:2 * D], scalar1=1.0), reads=[mods], writes=[mods])

    def bcast_layer(l, with_gate):
        srcs = [(0, modp[:, l, 16:24], modp)] if with_gate else []
        srcs += [(1, lngT[:, l, :], lngT), (2, lnbT[:, l, :], lnbT)]
        n = 0
        for (slot, src, srcT) in srcs:
            for half in range(2):
                dg = tm[1 + n % 2]
                pb = (pX, pO[0])[n % 2]
                n += 1
                for c4 in range(4):
                    S.op('dve', lambda half=half, src=src, dg=dg, c4=c4: V.tensor_scalar_mul(
                        out=dg[:, c4 * 128:(c4 + 1) * 128], in0=identf[:, :], scalar1=src[:, half * 4 + c4:half * 4 + c4 + 1]),
                        reads=[identf, srcT], writes=[dg])
                mm(pb[:, :], onesf[:, :], dg[:, 0:512], True, True, [dg, onesf], [pb])
                S.op('act', lambda slot=slot, half=half, pb=pb: A.copy(out=bc[:, slot, half * 512:(half + 1) * 512], in_=pb[:, :]), reads=[pb], writes=[bc])

    S.op('pool', lambda: G.memset(onesf[:], 1.0), writes=[onesf])

    def make_uT(l, xtile_ap, ntok, tok0, sample):
        if sample:
            S.op('dve', lambda: V.tensor_tensor(out=tm[0][0:16, :], in0=xtile_ap, in1=mods[:, D:2 * D], op=ALU.mult), reads=[Xs, mods], writes=[tm[0]])
            S.op('dve', lambda: V.tensor_tensor(out=xbf[0:16, :], in0=tm[0][0:16, :], in1=mods[:, 0:D], op=ALU.add), reads=[tm[0], mods], writes=[xbf])
        else:
            S.op('pool', lambda: G.tensor_copy(out=xbf[:, :], in_=xtile_ap), reads=[X], writes=[xbf])
        for c in range(8):
            S.op('pe', lambda c=c: PE.transpose(out=pT[:, c * 128:c * 128 + ntok], in_=xbf[0:ntok, c * 128:(c + 1) * 128], identity=ident[0:ntok, 0:ntok]),
                 reads=[xbf, ident], writes=[pT])
        for c in range(8):
            if sample:
                S.op('act', lambda c=c: A.copy(out=uT[:, c, tok0:tok0 + ntok], in_=pT[:, c * 128:c * 128 + ntok]), reads=[pT], writes=[uT])
            else:
                S.op('act', lambda c=c: A.activation(out=uT[:, c, tok0:tok0 + ntok], in_=pT[:, c * 128:c * 128 + ntok], func=AF.Identity,
                                                     bias=modp[:, l, c:c + 1], scale=modp[:, l, 8 + c:9 + c]), reads=[pT, modp], writes=[uT])

    def out_proj_ln(l, w_out_d, ntiles, ntok, xt, xap, sample):
        wo = [wload((w_out_d, nb)) for nb in range(2)]
        for t in range(ntiles):
            r = tm[0]
            for nb in range(2):
                for k in range(8):
                    mm(pO[nb][0:ntok, :], gT[:, k, t * 128:t * 128 + ntok], wo[nb][:, k, :], k == 0, k == 7, [gT, wo[nb]], [pO[nb]])
                gate_ap = mods[:, 2 * D + nb * 512:2 * D + (nb + 1) * 512] if sample else bc[:, 0, nb * 512:(nb + 1) * 512]
                S.op('dve', lambda nb=nb, gate_ap=gate_ap: V.tensor_tensor(out=tm[1][0:ntok, nb * 512:(nb + 1) * 512], in0=pO[nb][0:ntok, :], in1=gate_ap, op=ALU.mult),
                     reads=[pO[nb], mods if sample else bc], writes=[tm[1]])
            xa = xap(t)
            S.op('dve', lambda xa=xa: V.scalar_tensor_tensor(out=r[0:ntok, :], in0=xa, scalar=ALPHA, in1=tm[1][0:ntok, :], op0=ALU.mult, op1=ALU.add),
                 reads=[xt, tm[1]], writes=[r])
            for hh in range(2):
                S.op('dve', lambda hh=hh: V.bn_stats(out=small[0:ntok, 8 + 6 * hh:14 + 6 * hh], in_=r[0:ntok, hh * 512:(hh + 1) * 512]), reads=[r], writes=[small])
            S.op('dve', lambda: V.bn_aggr(out=small[0:ntok, 20:22], in_=small[0:ntok, 8:20]), reads=[small], writes=[small])
            S.op('dve', lambda: V.tensor_scalar_add(out=small[0:ntok, 22:23], in0=small[0:ntok, 21:22], scalar1=LN_EPS), reads=[small], writes=[small])
            S.op('act', lambda: A.activation(out=small[0:ntok, 22:23], in_=small[0:ntok, 22:23], func=AF.Ln), reads=[small], writes=[small])
            S.op('act', lambda: A.activation(out=small[0:ntok, 22:23], in_=small[0:ntok, 22:23], func=AF.Exp, scale=-0.5), reads=[small], writes=[small])
            S.op('dve', lambda: V.scalar_tensor_tensor(out=small[0:ntok, 23:24], in0=small[0:ntok, 20:21], scalar=-1.0, in1=small[0:ntok, 22:23], op0=ALU.mult, op1=ALU.mult), reads=[small], writes=[small])
            S.op('act', lambda: A.activation(out=tm[1][0:ntok, :], in_=r[0:ntok, :], func=AF.Identity, bias=small[0:ntok, 23:24], scale=small[0:ntok, 22:23]),
                 reads=[r, small], writes=[tm[1]])
            S.op('pool', lambda: G.tensor_tensor(out=tm[1][0:ntok, :], in0=tm[1][0:ntok, :], in1=bc[0:ntok, 1, :], op=ALU.mult), reads=[tm[1], bc], writes=[tm[1]])
            S.op('pool', lambda xa=xa: G.tensor_tensor(out=xa, in0=tm[1][0:ntok, :], in1=bc[0:ntok, 2, :], op=ALU.add), reads=[tm[1], bc], writes=[xt])

    def fm_to_rows(src_fn, nrow, dst_dram_ap, rd):
        for c in range(8):
            S.op('pe', lambda c=c: PE.transpose(out=pX[0:nrow, (c % 4) * 128:(c % 4) * 128 + 128], in_=src_fn(c), identity=identf[:, :]),
                 reads=rd + [identf], writes=[pX])
            if c % 4 == 3:
                h0 = (c // 4) * 512
                S.op('act', lambda h0=h0: A.copy(out=sttm[0:nrow, h0:h0 + 512], in_=pX[0:nrow, :]), reads=[pX], writes=[sttm])
        S.dma('sp', dst_dram_ap, sttm[0:nrow, :], reads=[sttm], writes=[])

    def rows_to_fm(src_dram_ap, nrow, dst_fn, wr):
        S.dma('sp', sttm[0:nrow, :], src_dram_ap, writes=[sttm])
        for c in range(8):
            S.op('pe', lambda c=c: PE.transpose(out=pX[:, 0:nrow], in_=sttm[0:nrow, c * 128:(c + 1) * 128], identity=identf[0:nrow, 0:nrow]),
                 reads=[sttm, identf], writes=[pX])
            S.op('dve', lambda c=c: V.tensor_copy(out=dst_fn(c), in_=pX[:, 0:nrow]), reads=[pX], writes=[wr])

    def conv_layer(l, j, ntok, sample, last):
        N = ntok
        for c in range(8):
            w = wload(("a_in", j, c))
            pq = pA if c % 2 == 0 else [pO[0], pO[1], pX, pA[3]]
            for g4 in range(4):
                for k in range(8):
                    mm(pq[g4][:, 0:N], w[:, k, g4 * 128:(g4 + 1) * 128], uT[:, k, 0:N], k == 0, k == 7, [w, uT], [pq[g4]])
            hs, pe_, y, sz = fa[0 + 4 * (c % 2)], fa[1 + 4 * (c % 2)], fa[2 + 4 * (c % 2)], fa[3 + 4 * (c % 2)]
            S.op('act', lambda: A.copy(out=hs[:, 0:N], in_=pq[0][:, 0:N]), reads=[pq[0]], writes=[hs])
            S.op('act', lambda: A.activation(out=sz[:, 0:N], in_=pq[3][:, 0:N], func=AF.Silu), reads=[pq[3]], writes=[sz])
            if not sample:
                S.op('pool', lambda c=c: G.tensor_copy(out=pe_[:, 0:2], in_=halo_a[:, j, c, :]), reads=[halo_a], writes=[pe_])
                S.op('dve', lambda: V.tensor_tensor(out=pe_[:, 2:2 + N], in0=pq[2][:, 0:N], in1=hs[:, 0:N], op=ALU.mult), reads=[pq[2], hs], writes=[pe_])
                S.op('dve', lambda: V.tensor_tensor(out=sz[:, 0:N], in0=pq[1][:, 0:N], in1=sz[:, 0:N], op=ALU.mult), reads=[pq[1], sz], writes=[sz])
                S.op('pool', lambda c=c: G.tensor_copy(out=halo_a[:, j, c, :], in_=pe_[:, N:N + 2]), reads=[pe_], writes=[halo_a])
                v0, v1, v2, yo = pe_[:, 0:N], pe_[:, 1:N + 1], pe_[:, 2:N + 2], y[:, 0:N]
            else:
                p3 = pe_[:, 0:24].rearrange("p (b t) -> p b t", b=4)
                S.op('pool', lambda c=c: G.tensor_copy(out=p3[:, :, 0:2], in_=stT[:, c, 0:8].rearrange("p (b r) -> p b r", b=4)), reads=[stT], writes=[pe_])
                S.op('dve', lambda: V.tensor_tensor(out=p3[:, :, 2:6], in0=pq[2][:, 0:16].rearrange("p (b t) -> p b t", b=4),
                                                    in1=hs[:, 0:16].rearrange("p (b t) -> p b t", b=4), op=ALU.mult), reads=[pq[2], hs], writes=[pe_])
                S.op('dve', lambda: V.tensor_tensor(out=sz[:, 0:N], in0=pq[1][:, 0:N], in1=sz[:, 0:N], op=ALU.mult), reads=[pq[1], sz], writes=[sz])
                S.op('pool', lambda c=c: G.tensor_copy(out=stT[:, c, 0:8].rearrange("p (b r) -> p b r", b=4), in_=p3[:, :, 4:6]), reads=[pe_], writes=[stT])
                v0, v1, v2 = p3[:, :, 0:4], p3[:, :, 1:5], p3[:, :, 2:6]
                yo = y[:, 0:16].rearrange("p (b t) -> p b t", b=4)
            S.op('dve', lambda c=c: V.tensor_scalar_mul(out=yo, in0=v0, scalar1=cw_a[:, j, 0, c:c + 1]), reads=[pe_, cw_a], writes=[y])
            S.op('dve', lambda c=c: V.scalar_tensor_tensor(out=yo, in0=v1, scalar=cw_a[:, j, 1, c:c + 1], in1=yo, op0=ALU.mult, op1=ALU.add), reads=[pe_, cw_a, y], writes=[y])
            S.op('dve', lambda c=c: V.scalar_tensor_tensor(out=yo, in0=v2, scalar=cw_a[:, j, 2, c:c + 1], in1=yo, op0=ALU.mult, op1=ALU.add), reads=[pe_, cw_a, y], writes=[y])
            S.op('dve', lambda c=c: V.tensor_tensor(out=gT[:, c, 0:N], in0=sz[:, 0:N], in1=y[:, 0:N], op=ALU.mult), reads=[sz, y], writes=[gT])
        if sample:
            fm_to_rows(lambda c: stT[:, c, 0:8], 8, conv_s[j, :, :], [stT])
        elif last:
            fm_to_rows(lambda c: halo_a[:, j, c, :], 2, conv_p[j, :, :], [halo_a])

    def lru_layer(l, ntok, sample, last):
        N = ntok
        for nb in range(4):
            w = wload(("r_in", nb))
            xbk = [pA[0], pA[1]] if nb % 2 == 0 else [pX, pA[1]]
            for e in range(2):
                for gz in range(2):
                    dst = xbk[e] if gz == 0 else pA[2 + e]
                    for k in range(8):
                        mm(dst[:, 0:N], w[:, k, gz * 256 + e * 128:gz * 256 + (e + 1) * 128], uT[:, k, 0:N], k == 0, k == 7, [w, uT], [dst])
            for e in range(2):
                S.op('act', lambda e=e: A.activation(out=fz[e][:, 0:N], in_=pA[2 + e][:, 0:N], func=AF.Silu), reads=[pA[2 + e]], writes=[fz[e]])
            wg = wload(("r_g", nb))
            xcs = []
            for e in range(2):
                ch = nb * 2 + e
                xe, xc = fa[e], fa[2 + e]
                if not sample:
                    S.op('pool', lambda ch=ch, xe=xe: G.tensor_copy(out=xe[:, 0:3], in_=halo_r[:, ch, :]), reads=[halo_r], writes=[xe])
                    S.op('act', lambda e=e, xe=xe: A.copy(out=xe[:, 3:3 + N], in_=xbk[e][:, 0:N]), reads=[xbk[e]], writes=[xe])
                    S.op('pool', lambda ch=ch, xe=xe: G.tensor_copy(out=halo_r[:, ch, :], in_=xe[:, N:N + 3]), reads=[xe], writes=[halo_r])
                    vk = [xe[:, k:k + N] for k in range(4)]
                    xo = xc[:, 0:N]
                else:
                    x3 = xe[:, 0:28].rearrange("p (b t) -> p b t", b=4)
                    S.op('pool', lambda ch=ch, x3=x3: G.tensor_copy(out=x3[:, :, 0:3], in_=stT[:, ch, 0:12].rearrange("p (b r) -> p b r", b=4)), reads=[stT], writes=[xe])
                    S.op('act', lambda e=e, x3=x3: A.copy(out=x3[:, :, 3:7], in_=xbk[e][:, 0:16].rearrange("p (b t) -> p b t", b=4)), reads=[xbk[e]], writes=[xe])
                    S.op('pool', lambda ch=ch, x3=x3: G.tensor_copy(out=stT[:, ch, 0:12].rearrange("p (b r) -> p b r", b=4), in_=x3[:, :, 4:7]), reads=[xe], writes=[stT])
                    vk = [x3[:, :, k:k + 4] for k in range(4)]
                    xo = xc[:, 0:16].rearrange("p (b t) -> p b t", b=4)
                S.op('dve', lambda ch=ch, xo=xo, vk=vk: V.tensor_scalar(out=xo, in0=vk[0], scalar1=cw_r[:, 0, ch:ch + 1], scalar2=cb_r[:, ch:ch + 1], op0=ALU.mult, op1=ALU.add),
                     reads=[xe, cw_r, cb_r], writes=[xc])
                for k in range(1, 4):
                    S.op('dve', lambda ch=ch, xo=xo, vk=vk, k=k: V.scalar_tensor_tensor(out=xo, in0=vk[k], scalar=cw_r[:, k, ch:ch + 1], in1=xo, op0=ALU.mult, op1=ALU.add),
                         reads=[xe, cw_r, xc], writes=[xc])
                S.op('act', lambda e=e, xc=xc: A.copy(out=fb[e][:, 0:N], in_=xc[:, 0:N]), reads=[xc], writes=[fb[e]])
                xcs.append(xc)
            for e in range(2):
                ch = nb * 2 + e
                xc = xcs[e]
                for (ko, dst) in ((0, pO[0]), (2, pO[1])):
                    for kk in range(2):
                        mm(dst[:, 0:N], wg[:, ko + kk, e * 128:(e + 1) * 128], fb[kk][:, 0:N], kk == 0, kk == 1, [wg, fb[kk]], [dst])
                rr, gi, aa, bb, hh = fa[4], fa[5], fa[6], fa[7], fa[4]
                S.op('act', lambda ch=ch: A.activation(out=rr[:, 0:N], in_=pO[0][:, 0:N], func=AF.Sigmoid, bias=bga[:, ch:ch + 1]), reads=[pO[0], bga], writes=[rr])
                S.op('act', lambda ch=ch: A.activation(out=gi[:, 0:N], in_=pO[1][:, 0:N], func=AF.Sigmoid, bias=bgx[:, ch:ch + 1]), reads=[pO[1], bgx], writes=[gi])
                S.op('act', lambda ch=ch: A.activation(out=aa[:, 0:N], in_=rr[:, 0:N], func=AF.Exp, scale=clam[:, ch:ch + 1]), reads=[rr, clam], writes=[aa])
                S.op('act', lambda ch=ch: A.activation(out=bb[:, 0:N], in_=rr[:, 0:N], func=AF.Exp, scale=clam2[:, ch:ch + 1]), reads=[rr, clam2], writes=[bb])
                S.op('act', lambda: A.activation(out=bb[:, 0:N], in_=bb[:, 0:N], func=AF.Sqrt, bias=1.0, scale=-1.0), reads=[bb], writes=[bb])
                S.op('dve', lambda xc=xc: V.tensor_tensor(out=gi[:, 0:N], in0=gi[:, 0:N], in1=xc[:, 0:N], op=ALU.mult), reads=[gi, xc], writes=[gi])
                S.op('dve', lambda: V.tensor_tensor(out=bb[:, 0:N], in0=bb[:, 0:N], in1=gi[:, 0:N], op=ALU.mult), reads=[bb, gi], writes=[bb])
                if not sample:
                    S.op('dve', lambda ch=ch: V.tensor_tensor_scan(out=hh[:, 0:N], data0=aa[:, 0:N], data1=bb[:, 0:N], initial=hst[:, ch:ch + 1], op0=ALU.mult, op1=ALU.add),
                         reads=[aa, bb, hst], writes=[hh])
                    S.op('pool', lambda ch=ch: G.tensor_copy(out=hst[:, ch:ch + 1], in_=hh[:, N - 1:N]), reads=[hh], writes=[hst])
                else:
                    a3 = aa[:, 0:16].rearrange("p (b t) -> p b t", b=4)
                    b3 = bb[:, 0:16].rearrange("p (b t) -> p b t", b=4)
                    h3 = hh[:, 0:16].rearrange("p (b t) -> p b t", b=4)
                    for t in range(4):
                        prev = stT2[:, ch, :] if t == 0 else h3[:, :, t - 1]
                        S.op('dve', lambda t=t, prev=prev: V.tensor_tensor(out=h3[:, :, t], in0=a3[:, :, t], in1=prev, op=ALU.mult), reads=[aa, hh, stT2], writes=[hh])
                        S.op('dve', lambda t=t: V.tensor_tensor(out=h3[:, :, t], in0=h3[:, :, t], in1=b3[:, :, t], op=ALU.add), reads=[bb, hh], writes=[hh])
                    S.op('pool', lambda ch=ch: G.tensor_copy(out=stT2[:, ch, :], in_=h3[:, :, 3]), reads=[hh], writes=[stT2])
                S.op('dve', lambda ch=ch, e=e: V.tensor_tensor(out=gT[:, ch, 0:N], in0=hh[:, 0:N], in1=fz[e][:, 0:N], op=ALU.mult), reads=[hh, fz[e]], writes=[gT])
        if sample:
            fm_to_rows(lambda c: stT2[:, c, :], 4, lruh_s[:, :], [stT2])
            fm_to_rows(lambda c: stT[:, c, 0:12], 12, lruc_s[:, :], [stT])
        elif last:
            fm_to_rows(lambda c: hst[:, c:c + 1], 1, lruh_p[:, :], [hst])
            fm_to_rows(lambda c: halo_r[:, c, :], 3, lruc_p[:, :], [halo_r])

    stT2 = sbt("stT2", [128, 8, 4])

    def rope(tile, np_, cosap, sinap, rdT):
        t3 = tile[0:np_, :].rearrange("p (s d) -> p s d", s=16)
        x1, x2 = t3[:, :, 0:HALF], t3[:, :, HALF:ROT]
        cb = cosap.rearrange("p (o d) -> p o d", o=1).broadcast_to([np_, 16, HALF])
        sb_ = sinap.rearrange("p (o d) -> p o d", o=1).broadcast_to([np_, 16, HALF])
        tmps = [tm[2][0:np_, i * 128:(i + 1) * 128].rearrange("p (s d) -> p s d", s=16) for i in range(4)]
        S.op('dve', lambda: V.tensor_tensor(out=tmps[0], in0=x1, in1=cb, op=ALU.mult), reads=[tile, rdT], writes=[tm[2]])
        S.op('dve', lambda: V.tensor_tensor(out=tmps[1], in0=x2, in1=sb_, op=ALU.mult), reads=[tile, rdT], writes=[tm[2]])
        S.op('dve', lambda: V.tensor_tensor(out=tmps[2], in0=x2, in1=cb, op=ALU.mult), reads=[tile, rdT], writes=[tm[2]])
        S.op('dve', lambda: V.tensor_tensor(out=tmps[3], in0=x1, in1=sb_, op=ALU.mult), reads=[tile, rdT], writes=[tm[2]])
        S.op('dve', lambda: V.tensor_tensor(out=x1, in0=tmps[0], in1=tmps[1], op=ALU.subtract), reads=[tm[2]], writes=[tile])
        S.op('dve', lambda: V.tensor_tensor(out=x2, in0=tmps[2], in1=tmps[3], op=ALU.add), reads=[tm[2]], writes=[tile])

    def attn_project(ntiles, ntok, tok_base_tile, sample):
        cnt = 0
        for kind in range(4):
            ws = [wload(("d_in", 2 * kind + hf)) for hf in range(2)]
            tgt = (qf, kf, vf, None)[kind]
            for t in range(ntiles):
                for hf in range(2):
                    w = ws[hf]
                    ps = pA[cnt % 4]
                    cnt += 1
                    hcol = hf * 512
                    for k in range(8):
                        mm(ps[0:ntok, :], uT[:, k, t * 128:t * 128 + ntok], w[:, k, :], k == 0, k == 7, [uT, w], [ps])
                    if kind == 0:
                        S.op('act', lambda ps=ps, hcol=hcol: A.activation(out=qf[0:ntok, hcol:hcol + 512], in_=ps[0:ntok, :], func=AF.Copy, scale=0.125), reads=[ps], writes=[qf])
                    elif kind == 3:
                        S.op('act', lambda ps=ps, t=t, hcol=hcol: A.activation(out=szb[0:ntok, t, hcol:hcol + 512], in_=ps[0:ntok, :], func=AF.Silu), reads=[ps], writes=[szb])
                    else:
                        S.op('act', lambda ps=ps, tgt=tgt, hcol=hcol: A.copy(out=tgt[0:ntok, hcol:hcol + 512], in_=ps[0:ntok, :]), reads=[ps], writes=[tgt])
                if kind == 3:
                    continue
                if kind < 2:
                    if sample:
                        rope(tgt, ntok, cos_s[:, :], sin_s[:, :], cos_s)
                    else:
                        tt = tok_base_tile + t
                        rope(tgt, ntok, cos_t[:, tt, :], sin_t[:, tt, :], cos_t)
                    if kind == 1:
                        dst = ks[:, :] if sample else kp[(tok_base_tile + t) * 128:(tok_base_tile + t + 1) * 128, :]
                        S.dma('sp', dst, kf[0:ntok, :], reads=[kf], writes=[])
                    S.op('pool', lambda tgt=tgt: G.tensor_copy(out=xbf[0:ntok, :], in_=tgt[0:ntok, :]), reads=[tgt], writes=[xbf])
                    for c in range(8):
                        S.op('pe', lambda c=c: PE.transpose(out=pT[:, c * 128:c * 128 + ntok], in_=xbf[0:ntok, c * 128:(c + 1) * 128], identity=ident[0:ntok, 0:ntok]),
                             reads=[xbf, ident], writes=[pT])
                    dT = qT if kind == 0 else kTc
                    S.op('act', lambda dT=dT, t=t: A.copy(out=dT[:, :, t * 128:t * 128 + ntok], in_=pT[:, :].rearrange("p (c n) -> p c n", c=8)[:, :, 0:ntok]), reads=[pT], writes=[dT])
                else:
                    dst = vs[:, :] if sample else vp[(tok_base_tile + t) * 128:(tok_base_tile + t + 1) * 128, :]
                    S.dma('sp' if sample else 'pool', dst, vf[0:ntok, :], reads=[vf], writes=[])
                    S.op('pool', lambda t=t: G.tensor_copy(out=vac[0:ntok, t, :, 0:128], in_=vf[0:ntok, :].rearrange("p (h d) -> p h d", h=8)), reads=[vf], writes=[vac])

    def attn_finish(oaps, np_, t, h, zap, dst_ap, rd_extra):
        o1, o2 = oaps
        S.op('dve', lambda: V.reciprocal(out=small[0:np_, 30:31], in_=o1[:, 128:129]), reads=rd_extra, writes=[small])
        S.op('dve', lambda: V.reciprocal(out=small[0:np_, 31:32], in_=o2[:, 128:129]), reads=rd_extra, writes=[small])
        S.op('dve', lambda: V.tensor_tensor(out=small[0:np_, 31:32], in0=small[0:np_, 31:32], in1=lam[0:np_, 0:1], op=ALU.mult), reads=[small, lam], writes=[small])
        ob = fa[6]
        S.op('dve', lambda: V.tensor_scalar_mul(out=ob[0:np_, 0:128], in0=o1[:, 0:128], scalar1=small[0:np_, 30:31]), reads=rd_extra + [small], writes=[ob])
        S.op('dve', lambda: V.scalar_tensor_tensor(out=ob[0:np_, 0:128], in0=o2[:, 0:128], scalar=small[0:np_, 31:32], in1=ob[0:np_, 0:128], op0=ALU.mult, op1=ALU.add),
             reads=rd_extra + [small, ob], writes=[ob])
        S.op('act', lambda: A.activation(out=ob[0:np_, 128:256], in_=ob[0:np_, 0:128], func=AF.Square, accum_out=small[0:np_, 32:33]), reads=[ob], writes=[ob, small])
        S.op('dve', lambda: V.tensor_scalar(out=small[0:np_, 33:34], in0=small[0:np_, 32:33], scalar1=1.0 / 128.0, scalar2=1e-5, op0=ALU.mult, op1=ALU.add), reads=[small], writes=[small])
        S.op('act', lambda: A.activation(out=small[0:np_, 33:34], in_=small[0:np_, 33:34], func=AF.Ln), reads=[small], writes=[small])
        S.op('act', lambda: A.activation(out=small[0:np_, 33:34], in_=small[0:np_, 33:34], func=AF.Exp, scale=-0.5), reads=[small], writes=[small])
        S.op('dve', lambda: V.scalar_tensor_tensor(out=ob[0:np_, 0:128], in0=ob[0:np_, 0:128], scalar=small[0:np_, 33:34], in1=subg[0:np_, :], op0=ALU.mult, op1=ALU.mult),
             reads=[ob, small, subg], writes=[ob])
        S.op('dve', lambda: V.tensor_tensor(out=dst_ap, in0=ob[0:np_, 0:128], in1=zap, op=ALU.mult), reads=[ob, szb], writes=[gtm])

    def kv_load(ci, h):
        S.dma('sp', kTh[h % 2][:, 0:ci * 512], kT_d[h, :, 0:ci * 512], reads=[kT_d], writes=[kTh[h % 2]])
        S.dma('sp', vh[h % 2][:, 0:ci * 4, :], v_d[h, :, 0:ci * 4, :], reads=[v_d], writes=[vh[h % 2]])

    def attn_prompt(ci):
        nkt_prev = ci * 4
        for h in range(8):
            kb, vb = kTh[h % 2], vh[h % 2]
            if ci > 0 and h + 1 < 8:
                kv_load(ci, h + 1)
            def oacc(sh, qs):
                i = sh * 4 + qs
                bank = (pO[0], pO[1], pX)[i // 3]
                return bank, bank[:, (i % 3) * 129:(i % 3) * 129 + 129]
            nkt = nkt_prev + 4
            cnt = 0
            for kt in range(nkt):
                jd = kt - nkt_prev
                q0 = max(jd, 0) * 128
                for sh in range(2):
                    ps = pA[cnt % 4]
                    pt_ = pTs[cnt % 3]
                    cnt += 1
                    if jd >= 0:
                        klhs = kTc[sh * 64:(sh + 1) * 64, h, jd * 128:(jd + 1) * 128]; krd = kTc
                        vrhs = vac[:, jd, h, :]; vrd = vac
                    else:
                        klhs = kb[sh * 64:(sh + 1) * 64, kt * 128:(kt + 1) * 128]; krd = kb
                        vrhs = vb[:, kt, :]; vrd = vb
                    mm(ps[:, q0:512], klhs, qT[sh * 64:(sh + 1) * 64, h, q0:512], True, True, [krd, qT], [ps])
                    S.op('act', lambda ps=ps, pt_=pt_, q0=q0: A.activation(out=pt_[:, q0:512], in_=ps[:, q0:512], func=AF.Exp), reads=[ps], writes=[pt_])
                    if jd >= 0:
                        S.op('pool', lambda pt_=pt_, q0=q0: G.tensor_tensor(out=pt_[:, q0:q0 + 128], in0=pt_[:, q0:q0 + 128], in1=tri[:, :], op=ALU.mult), reads=[pt_, tri], writes=[pt_])
                    for qs in range(max(jd, 0), 4):
                        bank, oap = oacc(sh, qs)
                        mm(oap, pt_[:, qs * 128:(qs + 1) * 128], vrhs, kt == 0 and (sh * 4 + qs) % 3 == 0, kt == nkt_prev + qs, [pt_, vrd], [bank], skip=True)
            for qs in range(4):
                b1, o1 = oacc(0, qs)
                b2, o2 = oacc(1, qs)
                attn_finish((o1, o2), 128, qs, h, szb[:, qs, h * 128:(h + 1) * 128], gtm[:, qs, h * 128:(h + 1) * 128], [b1, b2])
        if ci < NCH - 1:
            S.dma('pool', kT_d.ap()[:, :, ci * 512:(ci + 1) * 512].rearrange("h p n -> p h n"), kTc[:, :, :], reads=[kTc], writes=[kT_d])
            for t4 in range(4):
                S.dma('pool', v_d.ap()[:, :, ci * 4 + t4, :].rearrange("h p d -> p h d"), vac[:, t4, :, :], reads=[vac], writes=[v_d])

    def g_to_gT(ntiles, ntok):
        for t in range(ntiles):
            for c in range(8):
                S.op('pe', lambda c=c, t=t: PE.transpose(out=pT[:, c * 128:c * 128 + ntok], in_=gtm[0:ntok, t, c * 128:(c + 1) * 128], identity=ident[0:ntok, 0:ntok]),
                     reads=[gtm, ident], writes=[pT])
            S.op('act', lambda t=t: A.copy(out=gT[:, :, t * 128:t * 128 + ntok], in_=pT[:, :].rearrange("p (c n) -> p c n", c=8)[:, :, 0:ntok]), reads=[pT], writes=[gT])

    def attn_sample():
        S.dma('sp', ptab_sb[:], ptab[0:1, :].broadcast_to([128, 4 * NPAGE]), writes=[ptab_sb])
        S.op('pool', lambda: G.iota(pidx[:], pattern=[[0, 4 * NPAGE]], base=0, channel_multiplier=1), writes=[pidx])
        S.op('dve', lambda: V.tensor_copy(out=pidf[:, 0, :], in_=pidx[:]), reads=[pidx], writes=[pidf])
        S.op('dve', lambda: V.tensor_copy(out=pidf[:, 1, :], in_=ptab_sb[:]), reads=[ptab_sb], writes=[pidf])
        S.op('dve', lambda: V.scalar_tensor_tensor(out=pidf[:, 1, :], in0=pidf[:, 1, :], scalar=128.0, in1=pidf[:, 0, :], op0=ALU.mult, op1=ALU.add), reads=[pidf], writes=[pidf])
        S.op('dve', lambda: V.tensor_copy(out=pidx[:], in_=pidf[:, 1, :]), reads=[pidf], writes=[pidx])
        S.op('pool', lambda: G.memset(qblk[:], 0.0), writes=[qblk])
        for h in range(8):
            for j in range(2):
                S.op('act', lambda h=h, j=j: A.copy(
                    out=qblk[j * 64:(j + 1) * 64, h, :, h * 8:(h + 1) * 8].rearrange("p b (t j) -> p b t j", j=2)[:, :, :, j],
                    in_=qT[j * 64:(j + 1) * 64, h, 0:16].rearrange("p (b t) -> p b t", b=4)), reads=[qT], writes=[qblk])
        NK = PAST + 16
        ck2 = ck.ap().rearrange("g p n -> (g p) n"); cv2 = cv.ap().rearrange("g p n -> (g p) n")
        if True:
            for b in range(4):
                for pg in range(NPAGE):
                    kb, ktp = kpg[pg % 2], kTp[pg % 2]
                    col = b * NPAGE + pg
                    pf = pgf[pg % 2]
                    S.idma(pf[:, :], ck2, pidx[:, col:col + 1], reads=[pidx], writes=[pf])
                    S.op('dve', lambda kb=kb, pf=pf: V.tensor_copy(out=kb[:, :], in_=pf[:, :]), reads=[pf], writes=[kb])
                    for c in range(8):
                        S.op('pe', lambda c=c, kb=kb: PE.transpose(out=pT[:, c * 128:(c + 1) * 128], in_=kb[:, c * 128:(c + 1) * 128], identity=ident[:, :]),
                             reads=[kb, ident], writes=[pT])
                    S.op('act', lambda ktp=ktp: A.copy(out=ktp[:, :, :], in_=pT[:, :].rearrange("p (c n) -> p c n", c=8)), reads=[pT], writes=[ktp])
                    ps = pA[pg % 4]
                    for hh in range(8):
                        mm(ps[0:64, 0:128], qblk[:, hh, b, :], ktp[:, hh, :], hh == 0, hh == 7, [qblk, ktp], [ps])
                    S.op('dve', lambda ps=ps, pg=pg: V.tensor_copy(out=sS[:, pg * 128:(pg + 1) * 128], in_=ps[0:64, 0:128]), reads=[ps], writes=[sS])
                ps = pA[0]
                for hh in range(8):
                    mm(ps[0:64, 0:16], qblk[:, hh, b, :], kTc[:, hh, 0:16], hh == 0, hh == 7, [qblk, kTc], [ps])
                S.op('dve', lambda ps=ps, b=b: V.tensor_tensor(out=sS[:, PAST:NK], in0=ps[0:64, 0:16], in1=cmask[:, b, :], op=ALU.add), reads=[ps, cmask], writes=[sS])
                S.op('dve', lambda: V.reduce_max(out=small[0:64, 40:41], in_=sS[:, 0:NK], axis=AX.X), reads=[sS], writes=[small])
                S.op('dve', lambda: V.tensor_scalar_mul(out=small[0:64, 41:42], in0=small[0:64, 40:41], scalar1=-1.0), reads=[small], writes=[small])
                S.op('act', lambda: A.activation(out=sS[:, 0:NK], in_=sS[:, 0:NK], func=AF.Exp, bias=small[0:64, 41:42], accum_out=small[0:64, 42:43]),
                     reads=[sS, small], writes=[sS, small])
                S.op('dve', lambda: V.reciprocal(out=small[0:64, 43:44], in_=small[0:64, 42:43]), reads=[small], writes=[small])
                S.op('dve', lambda: V.tensor_scalar_mul(out=pS[:, 0:NK], in0=sS[:, 0:NK], scalar1=small[0:64, 43:44]), reads=[sS, small], writes=[pS])
                for pg in range(NPAGE + 1):
                    nk = 128 if pg < NPAGE else 16
                    S.op('pe', lambda pg=pg, nk=nk: PE.transpose(out=pT[0:nk, 0:64], in_=pS[:, pg * 128:pg * 128 + nk], identity=ident[0:64, 0:64]),
                         reads=[pS, ident], writes=[pT])
                    S.op('act', lambda nk=nk: A.copy(out=pTS[0:nk, :], in_=pT[0:nk, 0:64]), reads=[pT], writes=[pTS])
                    if pg < NPAGE:
                        vb = vpg[pg % 2]
                        col = b * NPAGE + pg
                        pf = pgf[pg % 2]
                        S.idma(pf[:, :], cv2, pidx[:, col:col + 1], reads=[pidx], writes=[pf])
                        S.op('act', lambda vb=vb, pf=pf: A.copy(out=vb[:, :], in_=pf[:, :]), reads=[pf], writes=[vb])
                    for hh in range(8):
                        bank = pO[hh // 4]
                        oap = bank[0:8, (hh % 4) * 128:(hh % 4) * 128 + 128]
                        if pg < NPAGE:
                            rhs, rd = vb[:, hh * 128:(hh + 1) * 128], vb
                        else:
                            rhs, rd = vac[0:16, 0, hh, 0:128], vac
                        mm(oap, pTS[0:nk, hh * 8:(hh + 1) * 8], rhs, pg == 0 and hh % 4 == 0, pg == NPAGE, [pTS, rd], [bank], skip=True)
                o8, od = tm[0], tm[1]
                S.op('act', lambda: A.copy(out=o8[0:8, 0:512], in_=pO[0][0:8, :]), reads=[pO[0]], writes=[o8])
                S.op('act', lambda: A.copy(out=o8[0:8, 512:1024], in_=pO[1][0:8, :]), reads=[pO[1]], writes=[o8])
                for half in range(2):
                    mm(pA[1][0:4, :], sel4[:, 0:4], o8[0:8, half * 512:(half + 1) * 512], True, True, [sel4, o8], [pA[1]])
                    S.op('act', lambda half=half: A.copy(out=od[0:4, half * 512:(half + 1) * 512], in_=pA[1][0:4, :]), reads=[pA[1]], writes=[od])
                S.dma('sp', osm_d[b * 4:(b + 1) * 4, :], od[0:4, :], reads=[od], writes=[osm_d])
        osm = tm[2]
        S.dma('sp', osm[0:16, :], osm_d[:, :], reads=[osm_d], writes=[osm])
        for hh in range(8):
            ob = fa[6]
            S.op('act', lambda hh=hh: A.activation(out=ob[0:16, 128:256], in_=osm[0:16, hh * 128:(hh + 1) * 128], func=AF.Square, accum_out=small[0:16, 32:33]), reads=[osm], writes=[ob, small])
            S.op('dve', lambda: V.tensor_scalar(out=small[0:16, 33:34], in0=small[0:16, 32:33], scalar1=1.0 / 128.0, scalar2=1e-5, op0=ALU.mult, op1=ALU.add), reads=[small], writes=[small])
            S.op('act', lambda: A.activation(out=small[0:16, 33:34], in_=small[0:16, 33:34], func=AF.Ln), reads=[small], writes=[small])
            S.op('act', lambda: A.activation(out=small[0:16, 33:34], in_=small[0:16, 33:34], func=AF.Exp, scale=-0.5), reads=[small], writes=[small])
            S.op('dve', lambda hh=hh: V.scalar_tensor_tensor(out=ob[0:16, 0:128], in0=osm[0:16, hh * 128:(hh + 1) * 128], scalar=small[0:16, 33:34], in1=subg[0:16, :], op0=ALU.mult, op1=ALU.mult),
                 reads=[osm, small, subg], writes=[ob])
            S.op('dve', lambda hh=hh: V.tensor_tensor(out=gtm[0:16, 0, hh * 128:(hh + 1) * 128], in0=ob[0:16, 0:128], in1=szb[0:16, 0, hh * 128:(hh + 1) * 128], op=ALU.mult), reads=[ob, szb], writes=[gtm])

    osm_d = S.dram("osm_d", [16, D])
    rowi = sbt("rowi", [64, 16], I32); coli = sbt("coli", [64, 16], I32); mk = sbt("mk", [64, 2, 16])
    S.op('pool', lambda: G.iota(rowi[:], pattern=[[0, 16]], base=0, channel_multiplier=1), writes=[rowi])
    S.op('dve', lambda: V.tensor_scalar(out=rowi[:], in0=rowi[:], scalar1=1, scalar2=3, op0=ALU.arith_shift_right, op1=ALU.bitwise_and), reads=[rowi], writes=[rowi])
    S.op('pool', lambda: G.iota(coli[:], pattern=[[1, 16]], base=0, channel_multiplier=0), writes=[coli])
    S.op('dve', lambda: V.tensor_single_scalar(out=mk[:, 1, :], in_=coli[:], scalar=2, op=ALU.arith_shift_right), reads=[coli], writes=[mk]) if False else None
    cb_i = sbt("cb_i", [64, 16], I32)
    S.op('dve', lambda: V.tensor_single_scalar(out=cb_i[:], in_=coli[:], scalar=2, op=ALU.arith_shift_right), reads=[coli], writes=[cb_i])
    S.op('dve', lambda: V.tensor_single_scalar(out=coli[:], in_=coli[:], scalar=3, op=ALU.bitwise_and), reads=[coli, cb_i], writes=[coli])
    S.op('dve', lambda: V.tensor_tensor(out=coli[:], in0=coli[:], in1=rowi[:], op=ALU.subtract), reads=[coli, rowi], writes=[coli])
    S.op('dve', lambda: V.tensor_copy(out=mk[:, 0, :], in_=coli[:]), reads=[coli], writes=[mk])
    S.op('dve', lambda: V.tensor_copy(out=mk[:, 1, :], in_=cb_i[:]), reads=[cb_i], writes=[mk])
    S.op('dve', lambda: V.tensor_single_scalar(out=mk[:, 0, :], in_=mk[:, 0, :], scalar=0.0, op=ALU.is_le), reads=[mk], writes=[mk])
    for b in range(4):
        S.op('dve', lambda b=b: V.tensor_single_scalar(out=cmask[:, b, :], in_=mk[:, 1, :], scalar=float(b), op=ALU.is_equal), reads=[mk], writes=[cmask])
        S.op('dve', lambda b=b: V.tensor_tensor(out=cmask[:, b, :], in0=cmask[:, b, :], in1=mk[:, 0, :], op=ALU.mult), reads=[mk, cmask], writes=[cmask])
        S.op('dve', lambda b=b: V.tensor_scalar(out=cmask[:, b, :], in0=cmask[:, b, :], scalar1=-1.0, scalar2=30000.0, op0=ALU.add, op1=ALU.mult), reads=[cmask], writes=[cmask])
    seli = sbt("seli", [8, 8], I32); self_ = sbt("self_", [8, 8]); sel4 = sbt("sel4", [8, 4])
    S.op('pool', lambda: G.iota(seli[:, 0:4], pattern=[[-2, 4]], base=0, channel_multiplier=1), writes=[seli])
    S.op('pool', lambda: G.iota(seli[:, 4:8], pattern=[[-2, 4]], base=-1, channel_multiplier=1), writes=[seli])
    S.op('dve', lambda: V.tensor_copy(out=self_[:], in_=seli[:]), reads=[seli], writes=[self_])
    S.op('dve', lambda: V.tensor_single_scalar(out=self_[:], in_=self_[:], scalar=0.0, op=ALU.is_equal), reads=[self_], writes=[self_])
    S.op('dve', lambda: V.scalar_tensor_tensor(out=sel4[:], in0=self_[:, 4:8], scalar=lam[0:8, 0:1], in1=self_[:, 0:4], op0=ALU.mult, op1=ALU.add), reads=[self_, lam], writes=[sel4])

    def run_layers(sample, ci):
        ntiles = 1 if sample else 4
        ntok = 16 if sample else 128
        N = 16 if sample else 512
        xt = Xs if sample else X
        last = (ci == NCH - 1)
        for l in range(4):
            kind, j = l % 3, l // 3
            if sample:
                adaln_layer(l)
                bcast_layer(l, False)
                if kind == 0:
                    rows_to_fm(st_conv[j, :, :], 8, lambda c: stT[:, c, 0:8], stT)
                elif kind == 1:
                    rows_to_fm(st_lc[:, :], 12, lambda c: stT[:, c, 0:12], stT)
                    rows_to_fm(st_h[:, :], 4, lambda c: stT2[:, c, :], stT2)
                make_uT(l, Xs[:, :], 16, 0, True)
            else:
                bcast_layer(l, True)
                for t in range(4):
                    make_uT(l, X[:, t, :], 128, t * 128, False)
            if kind == 0:
                conv_layer(l, j, N, sample, last)
                w_out_d = ('a_out', j)
            elif kind == 1:
                lru_layer(l, N, sample, last)
                w_out_d = ('r_out', 0)
            else:
                if not sample and ci > 0:
                    kv_load(ci, 0)
                attn_project(ntiles, ntok, ci * 4, sample)
                if sample:
                    attn_sample()
                else:
                    attn_prompt(ci)
                g_to_gT(ntiles, ntok)
                w_out_d = ('d_out', 0)
            if sample:
                out_proj_ln(l, w_out_d, 1, 16, Xs, lambda t: Xs[:, :], True)
            else:
                out_proj_ln(l, w_out_d, 4, 128, X, lambda t: X[:, t, :], False)

    S.op('pool', lambda: G.memset(vac[:], 1.0), writes=[vac])
    S.dma('sp', Xs[:, :], xs[:, :], writes=[Xs])
    run_layers(True, 0)
    S.dma('sp', ys[:, :], Xs[:, :], reads=[Xs], writes=[])
    S.barrier()
    stack.close()
    X = S.sb("X", [128, 4, D])
    fa = [S.sb("fa%d" % i, [128, 520]) for i in range(8)]
    uT = S.sb("uT", [128, 8, 512], BF16); gT = S.sb("gT", [128, 8, 512], BF16)
    qT = S.sb("qT", [128, 8, 512], BF16); kTc = S.sb("kTc", [128, 8, 512], BF16)
    vac = S.sb("vac", [128, 4, 8, 129], BF16)
    szb = S.sb("szb", [128, 4, D], BF16); gtm = S.sb("gtm", [128, 4, D], BF16)
    kTh = [S.sb("kTh%d" % i, [128, SEQ], BF16) for i in range(2)]
    vh = [S.sb("vh%d" % i, [128, SEQ // 128, 129], BF16) for i in range(2)]
    pTs = [S.sb("pTs%d" % i, [128, 512], BF16) for i in range(3)]
    S.op('pool', lambda: G.memset(vac[:], 1.0), writes=[vac])
    for i in range(2):
        S.op('pool', lambda i=i: G.memset(vh[i][:], 1.0), writes=[vh[i]])
    for ci in range(NCH):
        S.dma('sp', X[:, :, :], xp[ci * 512:(ci + 1) * 512, :].rearrange("(t p) d -> p t d", p=128), writes=[X])
        run_layers(False, ci)
        S.dma('pool', yp[ci * 512:(ci + 1) * 512, :].rearrange("(t p) d -> p t d", p=128), X[:, :, :], reads=[X], writes=[])
    S.finish()
    return nc


def _run(inp, SEQ, NPAGE, NPHYS, PAST):
    f = lambda a: np.ascontiguousarray(np.asarray(a), dtype=np.float32)
    nc = build(SEQ, NPAGE, NPHYS, PAST)
    ck = f(inp['cache_k'][0]).reshape(NPHYS, 128, D)
    cv = f(inp['cache_v'][0]).reshape(NPHYS, 128, D)
    shared = dict(
        ck=ck, cv=cv,
        w_ada=f(inp['w_ada']), b_ada=f(inp['b_ada']), ln_g=f(inp['ln_g']), ln_b=f(inp['ln_b']),
        a_w_in=f(inp['a_w_in']), a_conv_w=f(inp['a_conv_w']), a_w_out=f(inp['a_w_out']),
        r_w_in=f(inp['r_w_in']), r_conv_w=f(inp['r_conv_w']), r_conv_b=f(inp['r_conv_b']),
        r_w_ga=f(inp['r_w_ga']), r_b_ga=f(inp['r_b_ga']), r_w_gx=f(inp['r_w_gx']), r_b_gx=f(inp['r_b_gx']),
        r_lru=f(inp['r_lru_param']), r_w_out=f(inp['r_w_out']),
        d_w_in=f(inp['d_w_in']),
        d_lam=np.stack([f(inp['d_lq1'])[0], f(inp['d_lk1'])[0], f(inp['d_lq2'])[0], f(inp['d_lk2'])[0]], 0),
        d_subg=f(inp['d_subln_g']), d_w_out=f(inp['d_w_out']),
    )
    xp_, xs_ = f(inp['x_prompt']), f(inp['x_sample'])
    pt = np.ascontiguousarray(np.asarray(inp['page_table']), dtype=np.int32)
    in_maps = []
    for c in range(8):
        b = c // 2
        sl = slice(4 * c, 4 * c + 4)
        m = dict(shared)
        m.update(
            xp=xp_[b], xs=xs_[sl].reshape(16, D),
            cp=f(inp['c_prompt'])[b:b + 1], cs=f(inp['c_sample'])[sl],
            st_conv=f(inp['state_conv_a'])[:, sl].reshape(2, 8, D),
            st_h=f(inp['state_lru_h'])[0, sl], st_lc=f(inp['state_lru_conv'])[0, sl].reshape(12, D),
            ptab=pt[sl].reshape(1, 4 * NPAGE),
        )
        in_maps.append(m)
    res = run_bass_kernel_spmd(nc, in_maps, core_ids=list(range(8)))
    R = res.results
    ev = [R[2 * b] for b in range(4)]
    y_p = np.stack([r['yp'] for r in ev], 0)
    y_s = np.concatenate([r['ys'].reshape(4, 4, D) for r in R], 0)
    conv_p = np.stack([r['conv_p'] for r in ev], 1)
    conv_s = np.concatenate([r['conv_s'].reshape(2, 4, 2, D) for r in R], 1)
    lruh_p = np.stack([r['lruh_p'][0] for r in ev], 0)[None]
    lruh_s = np.concatenate([r['lruh_s'] for r in R], 0)[None]
    lruc_p = np.stack([r['lruc_p'] for r in ev], 0)[None]
    lruc_s = np.concatenate([r['lruc_s'].reshape(4, 3, D) for r in R], 0)[None]
    k_p = np.stack([r['kp'].reshape(SEQ, 16, 64) for r in ev], 0)[None]
    v_p = np.stack([r['vp'].reshape(SEQ, 8, 128) for r in ev], 0)[None]
    k_s = np.concatenate([r['ks'].reshape(4, 4, 16, 64) for r in R], 0)[None]
    v_s = np.concatenate([r['vs'].reshape(4, 4, 8, 128) for r in R], 0)[None]
    outs = (y_p, y_s, conv_p, conv_s, lruh_p, lruh_s, lruc_p, lruc_s, k_p, v_p, k_s, v_s)
    return tuple(np.ascontiguousarray(o, dtype=np.float32) for o in outs)


def kernel(**inputs):
    return _run(inputs, 4096, 64, 2560, 8192)
```

```python
import numpy as np
import concourse.bass as bass
import concourse.mybir as mybir

F32 = mybir.dt.float32
BF16 = mybir.dt.bfloat16
I32 = mybir.dt.int32
ALU = mybir.AluOpType
AF = mybir.ActivationFunctionType
AX = mybir.AxisListType


class Buf:
    def __init__(self, name):
        self.name = name
        self.wc = {}
        self.wd = []
        self.rc = {}
        self.rd = []


class T:
    def __init__(self, h, name):
        self.h = h
        self.b = Buf(name)

    def __getitem__(self, k):
        return self.h[k]

    def ap(self):
        return self.h.ap()


class Sched:
    EP = 4000
    NDS = 40

    def __init__(self, nc):
        self.nc = nc
        self.eng = dict(pe=nc.tensor, act=nc.scalar, dve=nc.vector, pool=nc.gpsimd, sp=nc.sync)
        self.cnt = {e: 0 for e in self.eng}
        self.esems = {e: [] for e in self.eng}
        self.seen = {e: {} for e in self.eng}
        self.dsems = [nc.alloc_semaphore("dq%d" % i) for i in range(self.NDS)]
        self.duse = [0] * self.NDS
        self.dn = 0
        self.nwaits = 0

    def sb(self, name, shape, dt=F32):
        return T(self.nc.alloc_sbuf_tensor(name, list(shape), dt), name)

    def ps(self, name, shape, dt=F32):
        return T(self.nc.alloc_psum_tensor(name, list(shape), dt), name)

    def dram(self, name, shape, dt=F32, kind="Internal"):
        return T(self.nc.dram_tensor(name, list(shape), dt, kind=kind), name)

    def _wait(self, e, tok):
        if tok[0] == 'c':
            _, f, c = tok
            if self.seen[e].get(('c', f), 0) >= c:
                return
            ep, v = (c - 1) // self.EP, (c - 1) % self.EP + 1
            self.eng[e].wait_ge(self.esems[f][ep], v)
            self.seen[e][('c', f)] = c
        else:
            _, i, v = tok
            if self.seen[e].get(('d', i), 0) >= v:
                return
            self.eng[e].wait_ge(self.dsems[i], v)
            self.seen[e][('d', i)] = v
        self.nwaits += 1

    def _deps(self, e, reads, writes):
        for b in reads:
            for f, c in b.wc.items():
                self._wait(e, ('c', f, c))
            for t in b.wd:
                self._wait(e, t)
        for b in writes:
            for f, c in b.rc.items():
                if f != e:
                    self._wait(e, ('c', f, c))
            for t in b.rd:
                self._wait(e, t)
            for f, c in b.wc.items():
                if f != e:
                    self._wait(e, ('c', f, c))
            for t in b.wd:
                self._wait(e, t)

    def _record(self, tok, reads, writes):
        for b in reads:
            if tok[0] == 'c':
                b.rc[tok[1]] = max(b.rc.get(tok[1], 0), tok[2])
            else:
                b.rd.append(tok)
        for b in writes:
            if b.rc or b.rd:
                b.rc = {}
                b.rd = []
                b.wc = {}
                b.wd = []
            if tok[0] == 'c':
                b.wc[tok[1]] = max(b.wc.get(tok[1], 0), tok[2])
            else:
                b.wd.append(tok)

    @staticmethod
    def _bufs(xs):
        out = []
        for x in xs:
            if x is None:
                continue
            out.append(x.b if isinstance(x, T) else x)
        return out

    def op(self, e, fn, reads=(), writes=()):
        reads = self._bufs(reads)
        writes = self._bufs(writes)
        self._deps(e, reads, writes)
        ins = fn()
        self.cnt[e] += 1
        c = self.cnt[e]
        ep = (c - 1) // self.EP
        while len(self.esems[e]) <= ep:
            self.esems[e].append(self.nc.alloc_semaphore("c_%s_%d" % (e, len(self.esems[e]))))
        ins.then_inc(self.esems[e][ep], 1)
        tok = ('c', e, c)
        self._record(tok, reads, writes)
        return tok

    def dma(self, q, out, in_, reads=(), writes=(), **kw):
        reads = self._bufs(reads)
        writes = self._bufs(writes)
        self._deps(q, reads, writes)
        i = self.dn % self.NDS
        self.dn += 1
        self.duse[i] += 1
        v = 16 * self.duse[i]
        if v > 16:
            self._wait(q, ('d', i, v - 16))
        ins = self.eng[q].dma_start(out=out, in_=in_, **kw)
        ins.then_inc(self.dsems[i], 16)
        tok = ('d', i, v)
        self._record(tok, reads, writes)
        return tok

    def idma(self, out, in_, idx_ap, reads=(), writes=()):
        return self.coll(lambda: self.nc.gpsimd.indirect_dma_start(
            out=out, out_offset=None, in_=in_, in_offset=bass.IndirectOffsetOnAxis(ap=idx_ap, axis=0)), reads=reads, writes=writes)

    def coll(self, fn, reads=(), writes=()):
        q = 'pool'
        reads = self._bufs(reads)
        writes = self._bufs(writes)
        self._deps(q, reads, writes)
        i = self.dn % self.NDS
        self.dn += 1
        self.duse[i] += 1
        v = 16 * self.duse[i]
        if v > 16:
            self._wait(q, ('d', i, v - 16))
        ins = fn()
        ins.then_inc(self.dsems[i], 16)
        tok = ('d', i, v)
        self._record(tok, reads, writes)
        return tok

    def barrier(self):
        for e in ('pe', 'act', 'dve', 'pool', 'sp'):
            for i in range(self.NDS):
                if self.duse[i]:
                    self._wait(e, ('d', i, 16 * self.duse[i]))
            for f in ('pe', 'act', 'dve', 'pool'):
                if self.cnt[f] and f != e:
                    self._wait(e, ('c', f, self.cnt[f]))

    def finish(self):
        for i in range(self.NDS):
            if self.duse[i]:
                self._wait('sp', ('d', i, 16 * self.duse[i]))
        for f in ('pe', 'act', 'dve', 'pool'):
            if self.cnt[f]:
                self._wait('sp', ('c', f, self.cnt[f]))
import math
from concourse.bass_utils import run_bass_kernel_spmd


D = 1024
ALPHA = (2 * 4) ** 0.25
LN_EPS = 1e-5
ROT = 16
HALF = 8
THETA = 500000.0
TWO_PI = 2.0 * math.pi


def build(SEQ, NPAGE, NPHYS, PAST):
    NCH = SEQ // 512
    nc = bass.Bass("TRN2", target_bir_lowering=False)
    S = Sched(nc)
    V, A, G, PE = nc.vector, nc.scalar, nc.gpsimd, nc.tensor

    def DI(name, shape, dt=F32):
        return S.dram(name, shape, dt, kind="ExternalInput")

    def DO(name, shape, dt=F32):
        return S.dram(name, shape, dt, kind="ExternalOutput")

    xp = DI("xp", [SEQ, D]); xs = DI("xs", [16, D])
    cp = DI("cp", [1, D]); cs = DI("cs", [4, D])
    st_conv = DI("st_conv", [2, 8, D])
    st_h = DI("st_h", [4, D]); st_lc = DI("st_lc", [12, D])
    ck = DI("ck", [NPHYS, 128, D]); cv = DI("cv", [NPHYS, 128, D])
    ptab = DI("ptab", [1, 4 * NPAGE], I32)
    w_ada = DI("w_ada", [4, D, 3 * D]); b_ada = DI("b_ada", [4, 3 * D])
    ln_g = DI("ln_g", [4, D]); ln_b = DI("ln_b", [4, D])
    a_w_in = DI("a_w_in", [2, D, 4 * D]); a_conv_w = DI("a_conv_w", [2, 3, D]); a_w_out = DI("a_w_out", [2, D, D])
    r_w_in = DI("r_w_in", [1, D, 2 * D]); r_conv_w = DI("r_conv_w", [1, 4, D]); r_conv_b = DI("r_conv_b", [1, D])
    r_w_ga = DI("r_w_ga", [1, 4, 256, 256]); r_b_ga = DI("r_b_ga", [1, D])
    r_w_gx = DI("r_w_gx", [1, 4, 256, 256]); r_b_gx = DI("r_b_gx", [1, D])
    r_lru = DI("r_lru", [1, D]); r_w_out = DI("r_w_out", [1, D, D])
    d_w_in = DI("d_w_in", [1, D, 4 * D]); d_lam = DI("d_lam", [4, 64])
    d_subg = DI("d_subg", [1, 128]); d_w_out = DI("d_w_out", [1, D, D])

    yp = DO("yp", [SEQ, D]); ys = DO("ys", [16, D])
    conv_p = DO("conv_p", [2, 2, D]); conv_s = DO("conv_s", [2, 8, D])
    lruh_p = DO("lruh_p", [1, D]); lruh_s = DO("lruh_s", [4, D])
    lruc_p = DO("lruc_p", [3, D]); lruc_s = DO("lruc_s", [12, D])
    kp = DO("kp", [SEQ, D]); vp = DO("vp", [SEQ, D]); ks = DO("ks", [16, D]); vs = DO("vs", [16, D])
    kT_d = S.dram("kT_d", [8, 128, SEQ], BF16)
    v_d = S.dram("v_d", [8, 128, SEQ // 128, 129], BF16)

    ident = S.sb("ident", [128, 128], BF16); identf = S.sb("identf", [128, 128])
    tri = S.sb("tri", [128, 128], BF16); onesf = S.sb("onesf", [128, 128])
    xbf = S.sb("xbf", [128, D], BF16)
    wslot = [S.sb("w%d" % i, [128, 8, 512], BF16) for i in range(4)]
    modp = S.sb("modp", [128, 4, 24])
    badaT = S.sb("badaT", [128, 4, 24])
    lngT = S.sb("lngT", [128, 4, 8]); lnbT = S.sb("lnbT", [128, 4, 8])
    bc = S.sb("bc", [128, 3, D])
    scT = S.sb("scT", [128, 8, 17])
    fb = [S.sb("fb%d" % i, [128, 512], BF16) for i in range(2)]
    fz = [S.sb("fz%d" % i, [128, 512], BF16) for i in range(2)]
    tm = [S.sb("tm%d" % i, [128, D]) for i in range(3)]
    small = S.sb("small", [128, 64])
    cw_a = S.sb("cw_a", [128, 2, 3, 8]); halo_a = S.sb("halo_a", [128, 2, 8, 2])
    cw_r = S.sb("cw_r", [128, 4, 8]); cb_r = S.sb("cb_r", [128, 8]); halo_r = S.sb("halo_r", [128, 8, 3])
    bga = S.sb("bga", [128, 8]); bgx = S.sb("bgx", [128, 8]); clam = S.sb("clam", [128, 8]); clam2 = S.sb("clam2", [128, 8])
    hst = S.sb("hst", [128, 8])
    stT = S.sb("stT", [128, 8, 12]); sttm = tm[2]
    cos_t = S.sb("cos_t", [128, SEQ // 128, HALF]); sin_t = S.sb("sin_t", [128, SEQ // 128, HALF])
    cos_s = S.sb("cos_s", [16, HALF]); sin_s = S.sb("sin_s", [16, HALF])
    lam = S.sb("lam", [128, 2]); subg = S.sb("subg", [128, 128])
    qf = S.sb("qf", [128, D]); kf = S.sb("kf", [128, D]); vf = S.sb("vf", [128, D])
    pTS = S.sb("pTS", [128, 64], BF16)
    import contextlib
    wdefs = {}
    kp_ = lambda ap: ap.rearrange("(k p) n -> p k n", p=128)
    for j in range(2):
        for c in range(8):
            wdefs[("a_in", j, c)] = [(lambda w, g4=g4: w[:, :, g4 * 128:(g4 + 1) * 128], kp_(a_w_in[j, :, g4 * D + c * 128:g4 * D + (c + 1) * 128])) for g4 in range(4)]
        for nb in range(2):
            wdefs[(("a_out", j), nb)] = [(lambda w: w[:, :, :], kp_(a_w_out[j, :, nb * 512:(nb + 1) * 512]))]
    for nb in range(4):
        wdefs[("r_in", nb)] = [(lambda w, gz=gz: w[:, :, gz * 256:(gz + 1) * 256], kp_(r_w_in[0, :, gz * D + nb * 256:gz * D + (nb + 1) * 256])) for gz in range(2)]
        wdefs[("r_g", nb)] = [(lambda w: w[:, 0:2, 0:256], kp_(r_w_ga[0, nb])), (lambda w: w[:, 2:4, 0:256], kp_(r_w_gx[0, nb]))]
    for nb in range(2):
        wdefs[(("r_out", 0), nb)] = [(lambda w: w[:, :, :], kp_(r_w_out[0, :, nb * 512:(nb + 1) * 512]))]
        wdefs[(("d_out", 0), nb)] = [(lambda w: w[:, :, :], kp_(d_w_out[0, :, nb * 512:(nb + 1) * 512]))]
    for blk in range(8):
        wdefs[("d_in", blk)] = [(lambda w: w[:, :, :], kp_(d_w_in[0, :, blk * 512:(blk + 1) * 512]))]
    widx = {k: i for i, k in enumerate(wdefs)}
    wbuf = [Buf("wsc%d" % i) for i in range(len(wdefs))]
    wsc = S.dram("wsc", [len(wdefs), 128, 4096], BF16)

    def layer_keys(l):
        kind, j = l % 3, l // 3
        if kind == 0:
            return [("a_in", j, c) for c in range(8)] + [(("a_out", j), nb) for nb in range(2)]
        if kind == 1:
            ks_ = []
            for nb in range(4):
                ks_ += [("r_in", nb), ("r_g", nb)]
            return ks_ + [(("r_out", 0), nb) for nb in range(2)]
        return [("d_in", blk) for blk in range(8)] + [(("d_out", 0), nb) for nb in range(2)]
    WSEQ = []
    for _rep in range(1 + NCH):
        for l in range(4):
            WSEQ += layer_keys(l)

    for key, parts in wdefs.items():
        dst3 = wsc[widx[key], :, :].rearrange("p (k n) -> p k n", k=8)
        for (dfn, src) in parts:
            S.dma('pool', dfn(dst3), src, writes=[wbuf[widx[key]]])
    stack = contextlib.ExitStack()

    def sbt(name, shape, dt=F32):
        return T(stack.enter_context(nc.sbuf_tensor(name, list(shape), dt)), name)

    uT = sbt("uT_s", [128, 8, 16], BF16); gT = sbt("gT_s", [128, 8, 16], BF16)
    qT = sbt("qT_s", [128, 8, 16], BF16); kTc = sbt("kTc_s", [128, 8, 16], BF16)
    vac = sbt("vac_s", [16, 1, 8, 129], BF16)
    szb = sbt("szb_s", [16, 1, D], BF16); gtm = sbt("gtm_s", [16, 1, D], BF16)
    X = None; kTh = None; vh = None; pTs = None
    mods = sbt("mods", [16, 3 * D]); c17 = tm[0]
    Xs = sbt("Xs", [16, D]); badar = sbt("badar", [16, 512])
    wadaf = [sbt("wadaf0", [128, 8, 256])]
    fa = [sbt("fa%d_s" % i, [128, 264]) for i in range(8)]
    ptab_sb = sbt("ptab_sb", [128, 4 * NPAGE], I32); pidx = sbt("pidx", [128, 4 * NPAGE], I32)
    pgf = [sbt("pgf%d" % i, [128, D]) for i in range(2)]
    pidf = sbt("pidf", [128, 2, 4 * NPAGE])
    kpg = [sbt("kpg%d" % i, [128, D], BF16) for i in range(2)]
    vpg = [sbt("vpg%d" % i, [128, D], BF16) for i in range(2)]
    kTp = [sbt("kTp%d" % i, [128, 8, 128], BF16) for i in range(2)]
    qblk = sbt("qblk", [128, 8, 4, 64], BF16)
    sS = sbt("sS", [64, PAST + 16])
    pS = sbt("pS", [64, PAST + 16], BF16)
    cmask = sbt("cmask", [64, 4, 16])

    pA = [S.ps("pA%d" % i, [128, 512]) for i in range(4)]
    pO = [S.ps("pO%d" % i, [128, 512]) for i in range(2)]
    pT = S.ps("pT", [128, 1024], BF16)
    pX = S.ps("pX", [128, 512])
    pTf = T(pT.h.bitcast(F32), "pTf"); pTf.b = pT.b

    def vec_fm(dst_ap, src_row_ap, rd, wr):
        S.dma('sp', dst_ap, src_row_ap.rearrange("(c p) -> p c", p=128), writes=[wr], reads=rd,
              allow_slow_non_contiguous=True)

    it = sbt("it", [128, 128], I32)
    S.op('pool', lambda: G.iota(it[:], pattern=[[1, 128]], base=0, channel_multiplier=-1), writes=[it])
    S.op('dve', lambda: V.tensor_copy(out=identf[:], in_=it[:]), reads=[it], writes=[identf])
    S.op('dve', lambda: V.tensor_single_scalar(out=tm[0][:, 0:128], in_=identf[:], scalar=0.0, op=ALU.is_ge), reads=[identf], writes=[tm[0]])
    S.op('dve', lambda: V.tensor_copy(out=tri[:], in_=tm[0][:, 0:128]), reads=[tm[0]], writes=[tri])
    S.op('dve', lambda: V.tensor_single_scalar(out=identf[:], in_=identf[:], scalar=0.0, op=ALU.is_equal), reads=[identf], writes=[identf])
    S.op('dve', lambda: V.tensor_copy(out=ident[:], in_=identf[:]), reads=[identf], writes=[ident])
    S.op('pool', lambda: G.memset(halo_a[:], 0.0), writes=[halo_a])
    S.op('pool', lambda: G.memset(halo_r[:], 0.0), writes=[halo_r])
    S.op('pool', lambda: G.memset(hst[:], 0.0), writes=[hst])

    pos_i = sbt("pos_i", [128, SEQ // 128], I32)
    S.op('pool', lambda: G.iota(pos_i[:], pattern=[[128, SEQ // 128]], base=0, channel_multiplier=1), writes=[pos_i])
    posf = fa[0]
    S.op('dve', lambda: V.tensor_copy(out=posf[:, 0:SEQ // 128], in_=pos_i[:]), reads=[pos_i], writes=[posf])
    NTL = SEQ // 128

    def rope_tables(cosd, sind, pos_ap, npart, n):
        for i in range(HALF):
            inv = math.exp(-2.0 * math.log(THETA) * i / ROT) / TWO_PI
            for (dst, off) in ((sind, 0.0), (cosd, 0.25)):
                S.op('dve', lambda dst=dst, off=off, inv=inv, i=i: V.tensor_scalar(
                    out=dst[0:npart, :, i] if n > 1 else dst[0:npart, i:i + 1], in0=pos_ap, scalar1=inv, scalar2=off,
                    op0=ALU.mult, op1=ALU.add), reads=[posf], writes=[dst])
        for dst in (sind, cosd):
            full = dst[0:npart, :, :] if n > 1 else dst[0:npart, :]
            ki = rki[0:npart, 0:n * HALF] if n == 1 else rki[0:npart, 0:n * HALF].rearrange("p (a b) -> p a b", b=HALF)
            kf = rkf[0:npart, 0:n * HALF] if n == 1 else rkf[0:npart, 0:n * HALF].rearrange("p (a b) -> p a b", b=HALF)
            S.op('dve', lambda full=full, ki=ki: V.tensor_copy(out=ki, in_=full), reads=[dst], writes=[rki])
            S.op('dve', lambda ki=ki, kf=kf: V.tensor_copy(out=kf, in_=ki), reads=[rki], writes=[rkf])
            S.op('dve', lambda full=full, kf=kf: V.tensor_tensor(out=full, in0=full, in1=kf, op=ALU.subtract), reads=[dst, rkf], writes=[dst])
            S.op('dve', lambda full=full, kf=kf: V.tensor_single_scalar(out=kf, in_=full, scalar=0.5, op=ALU.is_gt), reads=[dst], writes=[rkf])
            S.op('dve', lambda full=full, kf=kf: V.tensor_tensor(out=full, in0=full, in1=kf, op=ALU.subtract), reads=[dst, rkf], writes=[dst])
            S.op('dve', lambda full=full, kf=kf: V.tensor_single_scalar(out=kf, in_=full, scalar=-0.5, op=ALU.is_lt), reads=[dst], writes=[rkf])
            S.op('dve', lambda full=full, kf=kf: V.tensor_tensor(out=full, in0=full, in1=kf, op=ALU.add), reads=[dst, rkf], writes=[dst])
            S.op('act', lambda full=full: A.activation(out=full, in_=full, func=AF.Sin, scale=TWO_PI * 0.999999), reads=[dst], writes=[dst])

    rki = sbt("rki", [128, NTL * HALF], I32); rkf = sbt("rkf", [128, NTL * HALF])
    rope_tables(cos_t, sin_t, posf[:, 0:NTL], 128, NTL)
    pos_s = sbt("pos_s", [16, 1], I32)
    S.op('pool', lambda: G.iota(pos_s[:], pattern=[[0, 1]], base=0, channel_multiplier=1), writes=[pos_s])
    S.op('dve', lambda: V.tensor_single_scalar(out=pos_s[:], in_=pos_s[:], scalar=3, op=ALU.bitwise_and), reads=[pos_s], writes=[pos_s])
    S.op('dve', lambda: V.tensor_copy(out=posf[0:16, 0:1], in_=pos_s[:]), reads=[pos_s, cos_t, sin_t], writes=[posf])
    S.op('dve', lambda: V.tensor_scalar_add(out=posf[0:16, 0:1], in0=posf[0:16, 0:1], scalar1=float(PAST)), reads=[posf], writes=[posf])
    rope_tables(cos_s, sin_s, posf[0:16, 0:1], 16, 1)

    for l in range(4):
        vec_fm(lngT[:, l, :], ln_g[l, :], [], lngT)
        vec_fm(lnbT[:, l, :], ln_b[l, :], [], lnbT)
        for g3 in range(3):
            vec_fm(badaT[:, l, 8 * g3:8 * g3 + 8], b_ada[l, g3 * D:(g3 + 1) * D], [], badaT)
    for j in range(2):
        for k in range(3):
            vec_fm(cw_a[:, j, k, :], a_conv_w[j, k, :], [], cw_a)
    for k in range(4):
        vec_fm(cw_r[:, k, :], r_conv_w[0, k, :], [], cw_r)
    vec_fm(cb_r[:], r_conv_b[0, :], [], cb_r)
    vec_fm(bga[:], r_b_ga[0, :], [], bga)
    vec_fm(bgx[:], r_b_gx[0, :], [], bgx)
    vec_fm(clam[:], r_lru[0, :], [], clam)
    S.op('act', lambda: A.activation(out=clam[:], in_=clam[:], func=AF.Exp, scale=-1.0), reads=[clam], writes=[clam])
    S.op('act', lambda: A.activation(out=clam[:], in_=clam[:], func=AF.Ln, bias=1.0), reads=[clam], writes=[clam])
    S.op('dve', lambda: V.tensor_scalar_mul(out=clam2[:], in0=clam[:], scalar1=-16.0), reads=[clam], writes=[clam2])
    S.op('dve', lambda: V.tensor_scalar_mul(out=clam[:], in0=clam[:], scalar1=-8.0), reads=[clam, clam2], writes=[clam])
    lam_init = 0.8 - 0.6 * math.exp(-0.3 * 2)
    lq = sbt("lq", [128, 4, 64])
    S.dma('sp', lq[:], d_lam.ap().rearrange("(o a) d -> o a d", o=1).broadcast_to([128, 4, 64]), writes=[lq])
    S.op('dve', lambda: V.tensor_tensor(out=lq[:, 0, :], in0=lq[:, 0, :], in1=lq[:, 1, :], op=ALU.mult), reads=[lq], writes=[lq])
    S.op('dve', lambda: V.tensor_tensor(out=lq[:, 2, :], in0=lq[:, 2, :], in1=lq[:, 3, :], op=ALU.mult), reads=[lq], writes=[lq])
    S.op('dve', lambda: V.tensor_reduce(out=small[:, 0:1], in_=lq[:, 0, :], axis=AX.X, op=ALU.add), reads=[lq], writes=[small])
    S.op('dve', lambda: V.tensor_reduce(out=small[:, 1:2], in_=lq[:, 2, :], axis=AX.X, op=ALU.add), reads=[lq], writes=[small])
    S.op('act', lambda: A.activation(out=small[:, 0:2], in_=small[:, 0:2], func=AF.Exp), reads=[small], writes=[small])
    S.op('dve', lambda: V.tensor_tensor(out=lam[:, 0:1], in0=small[:, 0:1], in1=small[:, 1:2], op=ALU.subtract), reads=[small], writes=[lam])
    S.op('dve', lambda: V.tensor_scalar(out=lam[:, 0:1], in0=lam[:, 0:1], scalar1=lam_init, scalar2=-1.0, op0=ALU.add, op1=ALU.mult), reads=[lam], writes=[lam])
    S.dma('sp', subg[:], d_subg[0:1, :].broadcast_to([128, 128]), writes=[subg])
    S.op('dve', lambda: V.tensor_scalar_mul(out=subg[:], in0=subg[:], scalar1=1.0 - lam_init), reads=[subg], writes=[subg])

    NSLOT = 4
    wstate = dict(wp=0, issued=0)

    def wload(key):
        assert WSEQ[wstate['wp']] == key, (WSEQ[wstate['wp']], key)
        while wstate['issued'] < min(len(WSEQ), wstate['wp'] + NSLOT - 1):
            k2 = WSEQ[wstate['issued']]
            idx = widx[k2]
            w = wslot[wstate['issued'] % NSLOT]
            S.dma('sp', w[:].rearrange("p k n -> p (k n)"), wsc[idx, :, :], reads=[wbuf[idx]], writes=[w])
            wstate['issued'] += 1
        w = wslot[wstate['wp'] % NSLOT]
        wstate['wp'] += 1
        return w

    def mm(out_ap, lhsT, rhs, start, stop, reads, writes, skip=False):
        if skip:
            S.op('pe', lambda: PE.matmul(out_ap, lhsT, rhs, start=start, stop=stop, skip_group_check=True), reads=reads, writes=writes)
        else:
            S.op('pe', lambda: PE.matmul(out_ap, lhsT, rhs, start=start, stop=stop), reads=reads, writes=writes)

    S.dma('sp', c17[0:1, :], cp[0:1, :], writes=[c17])
    for b in range(4):
        S.dma('sp', c17[1 + 4 * b:5 + 4 * b, :], cs[b:b + 1, :].broadcast_to([4, D]), writes=[c17])
    S.op('act', lambda: A.activation(out=c17[0:17, :], in_=c17[0:17, :], func=AF.Silu), reads=[c17], writes=[c17])
    for c in range(8):
        S.op('pe', lambda c=c: PE.transpose(out=pX[:, 0:17], in_=c17[0:17, c * 128:(c + 1) * 128], identity=identf[0:17, 0:17]),
             reads=[c17, identf], writes=[pX])
        S.op('dve', lambda c=c: V.tensor_copy(out=scT[:, c, :], in_=pX[:, 0:17]), reads=[pX], writes=[scT])

    def adaln_layer(l):
        for nb in range(12):
            wf = wadaf[0]
            S.dma('sp', wf[:], w_ada[l, :, nb * 256:(nb + 1) * 256].rearrange("(k p) n -> p k n", p=128), writes=[wf])
            S.dma('sp', badar[:, 0:256], b_ada[l:l + 1, nb * 256:(nb + 1) * 256].broadcast_to([16, 256]), writes=[badar])
            for k in range(8):
                mm(pO[0][0:16, 0:256], scT[:, k, 1:17], wf[:, k, :], k == 0, k == 7, [scT, wf], [pO[0]])
            S.op('dve', lambda nb=nb: V.tensor_tensor(out=mods[:, nb * 256:(nb + 1) * 256], in0=pO[0][0:16, 0:256], in1=badar[:, 0:256], op=ALU.add),
                 reads=[pO[0], badar], writes=[mods])
            for cc in range(2):
                for k in range(8):
                    mm(pX[:, cc:cc + 1], wf[:, k, cc * 128:(cc + 1) * 128], scT[:, k, 0:1], k == 0, k == 7, [scT, wf], [pX])
            S.op('dve', lambda nb=nb: V.tensor_tensor(out=modp[:, l, nb * 2:nb * 2 + 2], in0=pX[:, 0:2], in1=badaT[:, l, nb * 2:nb * 2 + 2], op=ALU.add),
                 reads=[pX, badaT], writes=[modp])
        S.op('dve', lambda: V.tensor_scalar_add(out=modp[:, l, 8:16], in0=modp[:, l, 8:16], scalar1=1.0), reads=[modp], writes=[modp])
        S.op('dve', lambda: V.tensor_scalar_add(out=mods[:, D:2 * D], in0=mods[:, D:2 * D], scalar1=1.0), reads=[mods], writes=[mods])

    def bcast_layer(l, with_gate):
        srcs = [(0, modp[:, l, 16:24], modp)] if with_gate else []
        srcs += [(1, lngT[:, l, :], lngT), (2, lnbT[:, l, :], lnbT)]
        n = 0
        for (slot, src, srcT) in srcs:
            for half in range(2):
                dg = tm[1 + n % 2]
                pb = (pX, pO[0])[n % 2]
                n += 1
                for c4 in range(4):
                    S.op('dve', lambda half=half, src=src, dg=dg, c4=c4: V.tensor_scalar_mul(
                        out=dg[:, c4 * 128:(c4 + 1) * 128], in0=identf[:, :], scalar1=src[:, half * 4 + c4:half * 4 + c4 + 1]),
                        reads=[identf, srcT], writes=[dg])
                mm(pb[:, :], onesf[:, :], dg[:, 0:512], True, True, [dg, onesf], [pb])
                S.op('act', lambda slot=slot, half=half, pb=pb: A.copy(out=bc[:, slot, half * 512:(half + 1) * 512], in_=pb[:, :]), reads=[pb], writes=[bc])

    S.op('pool', lambda: G.memset(onesf[:], 1.0), writes=[onesf])

    def make_uT(l, xtile_ap, ntok, tok0, sample):
        if sample:
            S.op('dve', lambda: V.tensor_tensor(out=tm[0][0:16, :], in0=xtile_ap, in1=mods[:, D:2 * D], op=ALU.mult), reads=[Xs, mods], writes=[tm[0]])
            S.op('dve', lambda: V.tensor_tensor(out=xbf[0:16, :], in0=tm[0][0:16, :], in1=mods[:, 0:D], op=ALU.add), reads=[tm[0], mods], writes=[xbf])
        else:
            S.op('pool', lambda: G.tensor_copy(out=xbf[:, :], in_=xtile_ap), reads=[X], writes=[xbf])
        for c in range(8):
            S.op('pe', lambda c=c: PE.transpose(out=pT[:, c * 128:c * 128 + ntok], in_=xbf[0:ntok, c * 128:(c + 1) * 128], identity=ident[0:ntok, 0:ntok]),
                 reads=[xbf, ident], writes=[pT])
        for c in range(8):
            if sample:
                S.op('act', lambda c=c: A.copy(out=uT[:, c, tok0:tok0 + ntok], in_=pT[:, c * 128:c * 128 + ntok]), reads=[pT], writes=[uT])
            else:
                S.op('act', lambda c=c: A.activation(out=uT[:, c, tok0:tok0 + ntok], in_=pT[:, c * 128:c * 128 + ntok], func=AF.Identity,
                                                     bias=modp[:, l, c:c + 1], scale=modp[:, l, 8 + c:9 + c]), reads=[pT, modp], writes=[uT])

    def out_proj_ln(l, w_out_d, ntiles, ntok, xt, xap, sample):
        wo = [wload((w_out_d, nb)) for nb in range(2)]
        for t in range(ntiles):
            r = tm[0]
            for nb in range(2):
                for k in range(8):
                    mm(pO[nb][0:ntok, :], gT[:, k, t * 128:t * 128 + ntok], wo[nb][:, k, :], k == 0, k == 7, [gT, wo[nb]], [pO[nb]])
                gate_ap = mods[:, 2 * D + nb * 512:2 * D + (nb + 1) * 512] if sample else bc[:, 0, nb * 512:(nb + 1) * 512]
                S.op('dve', lambda nb=nb, gate_ap=gate_ap: V.tensor_tensor(out=tm[1][0:ntok, nb * 512:(nb + 1) * 512], in0=pO[nb][0:ntok, :], in1=gate_ap, op=ALU.mult),
                     reads=[pO[nb], mods if sample else bc], writes=[tm[1]])
            xa = xap(t)
            S.op('dve', lambda xa=xa: V.scalar_tensor_tensor(out=r[0:ntok, :], in0=xa, scalar=ALPHA, in1=tm[1][0:ntok, :], op0=ALU.mult, op1=ALU.add),
                 reads=[xt, tm[1]], writes=[r])
            for hh in range(2):
                S.op('dve', lambda hh=hh: V.bn_stats(out=small[0:ntok, 8 + 6 * hh:14 + 6 * hh], in_=r[0:ntok, hh * 512:(hh + 1) * 512]), reads=[r], writes=[small])
            S.op('dve', lambda: V.bn_aggr(out=small[0:ntok, 20:22], in_=small[0:ntok, 8:20]), reads=[small], writes=[small])
            S.op('dve', lambda: V.tensor_scalar_add(out=small[0:ntok, 22:23], in0=small[0:ntok, 21:22], scalar1=LN_EPS), reads=[small], writes=[small])
            S.op('act', lambda: A.activation(out=small[0:ntok, 22:23], in_=small[0:ntok, 22:23], func=AF.Ln), reads=[small], writes=[small])
            S.op('act', lambda: A.activation(out=small[0:ntok, 22:23], in_=small[0:ntok, 22:23], func=AF.Exp, scale=-0.5), reads=[small], writes=[small])
            S.op('dve', lambda: V.scalar_tensor_tensor(out=small[0:ntok, 23:24], in0=small[0:ntok, 20:21], scalar=-1.0, in1=small[0:ntok, 22:23], op0=ALU.mult, op1=ALU.mult), reads=[small], writes=[small])
            S.op('act', lambda: A.activation(out=tm[1][0:ntok, :], in_=r[0:ntok, :], func=AF.Identity, bias=small[0:ntok, 23:24], scale=small[0:ntok, 22:23]),
                 reads=[r, small], writes=[tm[1]])
            S.op('pool', lambda: G.tensor_tensor(out=tm[1][0:ntok, :], in0=tm[1][0:ntok, :], in1=bc[0:ntok, 1, :], op=ALU.mult), reads=[tm[1], bc], writes=[tm[1]])
            S.op('pool', lambda xa=xa: G.tensor_tensor(out=xa, in0=tm[1][0:ntok, :], in1=bc[0:ntok, 2, :], op=ALU.add), reads=[tm[1], bc], writes=[xt])

    def fm_to_rows(src_fn, nrow, dst_dram_ap, rd):
        for c in range(8):
            S.op('pe', lambda c=c: PE.transpose(out=pX[0:nrow, (c % 4) * 128:(c % 4) * 128 + 128], in_=src_fn(c), identity=identf[:, :]),
                 reads=rd + [identf], writes=[pX])
            if c % 4 == 3:
                h0 = (c // 4) * 512
                S.op('act', lambda h0=h0: A.copy(out=sttm[0:nrow, h0:h0 + 512], in_=pX[0:nrow, :]), reads=[pX], writes=[sttm])
        S.dma('sp', dst_dram_ap, sttm[0:nrow, :], reads=[sttm], writes=[])

    def rows_to_fm(src_dram_ap, nrow, dst_fn, wr):
        S.dma('sp', sttm[0:nrow, :], src_dram_ap, writes=[sttm])
        for c in range(8):
            S.op('pe', lambda c=c: PE.transpose(out=pX[:, 0:nrow], in_=sttm[0:nrow, c * 128:(c + 1) * 128], identity=identf[0:nrow, 0:nrow]),
                 reads=[sttm, identf], writes=[pX])
            S.op('dve', lambda c=c: V.tensor_copy(out=dst_fn(c), in_=pX[:, 0:nrow]), reads=[pX], writes=[wr])

    def conv_layer(l, j, ntok, sample, last):
        N = ntok
        for c in range(8):
            w = wload(("a_in", j, c))
            pq = pA if c % 2 == 0 else [pO[0], pO[1], pX, pA[3]]
            for g4 in range(4):
                for k in range(8):
                    mm(pq[g4][:, 0:N], w[:, k, g4 * 128:(g4 + 1) * 128], uT[:, k, 0:N], k == 0, k == 7, [w, uT], [pq[g4]])
            hs, pe_, y, sz = fa[0 + 4 * (c % 2)], fa[1 + 4 * (c % 2)], fa[2 + 4 * (c % 2)], fa[3 + 4 * (c % 2)]
            S.op('act', lambda: A.copy(out=hs[:, 0:N], in_=pq[0][:, 0:N]), reads=[pq[0]], writes=[hs])
            S.op('act', lambda: A.activation(out=sz[:, 0:N], in_=pq[3][:, 0:N], func=AF.Silu), reads=[pq[3]], writes=[sz])
            if not sample:
                S.op('pool', lambda c=c: G.tensor_copy(out=pe_[:, 0:2], in_=halo_a[:, j, c, :]), reads=[halo_a], writes=[pe_])
                S.op('dve', lambda: V.tensor_tensor(out=pe_[:, 2:2 + N], in0=pq[2][:, 0:N], in1=hs[:, 0:N], op=ALU.mult), reads=[pq[2], hs], writes=[pe_])
                S.op('dve', lambda: V.tensor_tensor(out=sz[:, 0:N], in0=pq[1][:, 0:N], in1=sz[:, 0:N], op=ALU.mult), reads=[pq[1], sz], writes=[sz])
                S.op('pool', lambda c=c: G.tensor_copy(out=halo_a[:, j, c, :], in_=pe_[:, N:N + 2]), reads=[pe_], writes=[halo_a])
                v0, v1, v2, yo = pe_[:, 0:N], pe_[:, 1:N + 1], pe_[:, 2:N + 2], y[:, 0:N]
            else:
                p3 = pe_[:, 0:24].rearrange("p (b t) -> p b t", b=4)
                S.op('pool', lambda c=c: G.tensor_copy(out=p3[:, :, 0:2], in_=stT[:, c, 0:8].rearrange("p (b r) -> p b r", b=4)), reads=[stT], writes=[pe_])
                S.op('dve', lambda: V.tensor_tensor(out=p3[:, :, 2:6], in0=pq[2][:, 0:16].rearrange("p (b t) -> p b t", b=4),
                                                    in1=hs[:, 0:16].rearrange("p (b t) -> p b t", b=4), op=ALU.mult), reads=[pq[2], hs], writes=[pe_])
                S.op('dve', lambda: V.tensor_tensor(out=sz[:, 0:N], in0=pq[1][:, 0:N], in1=sz[:, 0:N], op=ALU.mult), reads=[pq[1], sz], writes=[sz])
                S.op('pool', lambda c=c: G.tensor_copy(out=stT[:, c, 0:8].rearrange("p (b r) -> p b r", b=4), in_=p3[:, :, 4:6]), reads=[pe_], writes=[stT])
                v0, v1, v2 = p3[:, :, 0:4], p3[:, :, 1:5], p3[:, :, 2:6]
                yo = y[:, 0:16].rearrange("p (b t) -> p b t", b=4)
            S.op('dve', lambda c=c: V.tensor_scalar_mul(out=yo, in0=v0, scalar1=cw_a[:, j, 0, c:c + 1]), reads=[pe_, cw_a], writes=[y])
            S.op('dve', lambda c=c: V.scalar_tensor_tensor(out=yo, in0=v1, scalar=cw_a[:, j, 1, c:c + 1], in1=yo, op0=ALU.mult, op1=ALU.add), reads=[pe_, cw_a, y], writes=[y])
            S.op('dve', lambda c=c: V.scalar_tensor_tensor(out=yo, in0=v2, scalar=cw_a[:, j, 2, c:c + 1], in1=yo, op0=ALU.mult, op1=ALU.add), reads=[pe_, cw_a, y], writes=[y])
            S.op('dve', lambda c=c: V.tensor_tensor(out=gT[:, c, 0:N], in0=sz[:, 0:N], in1=y[:, 0:N], op=ALU.mult), reads=[sz, y], writes=[gT])
        if sample:
            fm_to_rows(lambda c: stT[:, c, 0:8], 8, conv_s[j, :, :], [stT])
        elif last:
            fm_to_rows(lambda c: halo_a[:, j, c, :], 2, conv_p[j, :, :], [halo_a])

    def lru_layer(l, ntok, sample, last):
        N = ntok
        for nb in range(4):
            w = wload(("r_in", nb))
            xbk = [pA[0], pA[1]] if nb % 2 == 0 else [pX, pA[1]]
            for e in range(2):
                for gz in range(2):
                    dst = xbk[e] if gz == 0 else pA[2 + e]
                    for k in range(8):
                        mm(dst[:, 0:N], w[:, k, gz * 256 + e * 128:gz * 256 + (e + 1) * 128], uT[:, k, 0:N], k == 0, k == 7, [w, uT], [dst])
            for e in range(2):
                S.op('act', lambda e=e: A.activation(out=fz[e][:, 0:N], in_=pA[2 + e][:, 0:N], func=AF.Silu), reads=[pA[2 + e]], writes=[fz[e]])
            wg = wload(("r_g", nb))
            xcs = []
            for e in range(2):
                ch = nb * 2 + e
                xe, xc = fa[e], fa[2 + e]
                if not sample:
                    S.op('pool', lambda ch=ch, xe=xe: G.tensor_copy(out=xe[:, 0:3], in_=halo_r[:, ch, :]), reads=[halo_r], writes=[xe])
                    S.op('act', lambda e=e, xe=xe: A.copy(out=xe[:, 3:3 + N], in_=xbk[e][:, 0:N]), reads=[xbk[e]], writes=[xe])
                    S.op('pool', lambda ch=ch, xe=xe: G.tensor_copy(out=halo_r[:, ch, :], in_=xe[:, N:N + 3]), reads=[xe], writes=[halo_r])
                    vk = [xe[:, k:k + N] for k in range(4)]
                    xo = xc[:, 0:N]
                else:
                    x3 = xe[:, 0:28].rearrange("p (b t) -> p b t", b=4)
                    S.op('pool', lambda ch=ch, x3=x3: G.tensor_copy(out=x3[:, :, 0:3], in_=stT[:, ch, 0:12].rearrange("p (b r) -> p b r", b=4)), reads=[stT], writes=[xe])
                    S.op('act', lambda e=e, x3=x3: A.copy(out=x3[:, :, 3:7], in_=xbk[e][:, 0:16].rearrange("p (b t) -> p b t", b=4)), reads=[xbk[e]], writes=[xe])
                    S.op('pool', lambda ch=ch, x3=x3: G.tensor_copy(out=stT[:, ch, 0:12].rearrange("p (b r) -> p b r", b=4), in_=x3[:, :, 4:7]), reads=[xe], writes=[stT])
                    vk = [x3[:, :, k:k + 4] for k in range(4)]
                    xo = xc[:, 0:16].rearrange("p (b t) -> p b t", b=4)
                S.op('dve', lambda ch=ch, xo=xo, vk=vk: V.tensor_scalar(out=xo, in0=vk[0], scalar1=cw_r[:, 0, ch:ch + 1], scalar2=cb_r[:, ch:ch + 1], op0=ALU.mult, op1=ALU.add),
                     reads=[xe, cw_r, cb_r], writes=[xc])
                for k in range(1, 4):
                    S.op('dve', lambda ch=ch, xo=xo, vk=vk, k=k: V.scalar_tensor_tensor(out=xo, in0=vk[k], scalar=cw_r[:, k, ch:ch + 1], in1=xo, op0=ALU.mult, op1=ALU.add),
                         reads=[xe, cw_r, xc], writes=[xc])
                S.op('act', lambda e=e, xc=xc: A.copy(out=fb[e][:, 0:N], in_=xc[:, 0:N]), reads=[xc], writes=[fb[e]])
                xcs.append(xc)
            for e in range(2):
                ch = nb * 2 + e
                xc = xcs[e]
                for (ko, dst) in ((0, pO[0]), (2, pO[1])):
                    for kk in range(2):
                        mm(dst[:, 0:N], wg[:, ko + kk, e * 128:(e + 1) * 128], fb[kk][:, 0:N], kk == 0, kk == 1, [wg, fb[kk]], [dst])
                rr, gi, aa, bb, hh = fa[4], fa[5], fa[6], fa[7], fa[4]
                S.op('act', lambda ch=ch: A.activation(out=rr[:, 0:N], in_=pO[0][:, 0:N], func=AF.Sigmoid, bias=bga[:, ch:ch + 1]), reads=[pO[0], bga], writes=[rr])
                S.op('act', lambda ch=ch: A.activation(out=gi[:, 0:N], in_=pO[1][:, 0:N], func=AF.Sigmoid, bias=bgx[:, ch:ch + 1]), reads=[pO[1], bgx], writes=[gi])
                S.op('act', lambda ch=ch: A.activation(out=aa[:, 0:N], in_=rr[:, 0:N], func=AF.Exp, scale=clam[:, ch:ch + 1]), reads=[rr, clam], writes=[aa])
                S.op('act', lambda ch=ch: A.activation(out=bb[:, 0:N], in_=rr[:, 0:N], func=AF.Exp, scale=clam2[:, ch:ch + 1]), reads=[rr, clam2], writes=[bb])
                S.op('act', lambda: A.activation(out=bb[:, 0:N], in_=bb[:, 0:N], func=AF.Sqrt, bias=1.0, scale=-1.0), reads=[bb], writes=[bb])
                S.op('dve', lambda xc=xc: V.tensor_tensor(out=gi[:, 0:N], in0=gi[:, 0:N], in1=xc[:, 0:N], op=ALU.mult), reads=[gi, xc], writes=[gi])
                S.op('dve', lambda: V.tensor_tensor(out=bb[:, 0:N], in0=bb[:, 0:N], in1=gi[:, 0:N], op=ALU.mult), reads=[bb, gi], writes=[bb])
                if not sample:
                    S.op('dve', lambda ch=ch: V.tensor_tensor_scan(out=hh[:, 0:N], data0=aa[:, 0:N], data1=bb[:, 0:N], initial=hst[:, ch:ch + 1], op0=ALU.mult, op1=ALU.add),
                         reads=[aa, bb, hst], writes=[hh])
                    S.op('pool', lambda ch=ch: G.tensor_copy(out=hst[:, ch:ch + 1], in_=hh[:, N - 1:N]), reads=[hh], writes=[hst])
                else:
                    a3 = aa[:, 0:16].rearrange("p (b t) -> p b t", b=4)
                    b3 = bb[:, 0:16].rearrange("p (b t) -> p b t", b=4)
                    h3 = hh[:, 0:16].rearrange("p (b t) -> p b t", b=4)
                    for t in range(4):
                        prev = stT2[:, ch, :] if t == 0 else h3[:, :, t - 1]
                        S.op('dve', lambda t=t, prev=prev: V.tensor_tensor(out=h3[:, :, t], in0=a3[:, :, t], in1=prev, op=ALU.mult), reads=[aa, hh, stT2], writes=[hh])
                        S.op('dve', lambda t=t: V.tensor_tensor(out=h3[:, :, t], in0=h3[:, :, t], in1=b3[:, :, t], op=ALU.add), reads=[bb, hh], writes=[hh])
                    S.op('pool', lambda ch=ch: G.tensor_copy(out=stT2[:, ch, :], in_=h3[:, :, 3]), reads=[hh], writes=[stT2])
                S.op('dve', lambda ch=ch, e=e: V.tensor_tensor(out=gT[:, ch, 0:N], in0=hh[:, 0:N], in1=fz[e][:, 0:N], op=ALU.mult), reads=[hh, fz[e]], writes=[gT])
        if sample:
            fm_to_rows(lambda c: stT2[:, c, :], 4, lruh_s[:, :], [stT2])
            fm_to_rows(lambda c: stT[:, c, 0:12], 12, lruc_s[:, :], [stT])
        elif last:
            fm_to_rows(lambda c: hst[:, c:c + 1], 1, lruh_p[:, :], [hst])
            fm_to_rows(lambda c: halo_r[:, c, :], 3, lruc_p[:, :], [halo_r])

    stT2 = sbt("stT2", [128, 8, 4])

    def rope(tile, np_, cosap, sinap, rdT):
        t3 = tile[0:np_, :].rearrange("p (s d) -> p s d", s=16)
        x1, x2 = t3[:, :, 0:HALF], t3[:, :, HALF:ROT]
        cb = cosap.rearrange("p (o d) -> p o d", o=1).broadcast_to([np_, 16, HALF])
        sb_ = sinap.rearrange("p (o d) -> p o d", o=1).broadcast_to([np_, 16, HALF])
        tmps = [tm[2][0:np_, i * 128:(i + 1) * 128].rearrange("p (s d) -> p s d", s=16) for i in range(4)]
        S.op('dve', lambda: V.tensor_tensor(out=tmps[0], in0=x1, in1=cb, op=ALU.mult), reads=[tile, rdT], writes=[tm[2]])
        S.op('dve', lambda: V.tensor_tensor(out=tmps[1], in0=x2, in1=sb_, op=ALU.mult), reads=[tile, rdT], writes=[tm[2]])
        S.op('dve', lambda: V.tensor_tensor(out=tmps[2], in0=x2, in1=cb, op=ALU.mult), reads=[tile, rdT], writes=[tm[2]])
        S.op('dve', lambda: V.tensor_tensor(out=tmps[3], in0=x1, in1=sb_, op=ALU.mult), reads=[tile, rdT], writes=[tm[2]])
        S.op('dve', lambda: V.tensor_tensor(out=x1, in0=tmps[0], in1=tmps[1], op=ALU.subtract), reads=[tm[2]], writes=[tile])
        S.op('dve', lambda: V.tensor_tensor(out=x2, in0=tmps[2], in1=tmps[3], op=ALU.add), reads=[tm[2]], writes=[tile])

    def attn_project(ntiles, ntok, tok_base_tile, sample):
        cnt = 0
        for kind in range(4):
            ws = [wload(("d_in", 2 * kind + hf)) for hf in range(2)]
            tgt = (qf, kf, vf, None)[kind]
            for t in range(ntiles):
                for hf in range(2):
                    w = ws[hf]
                    ps = pA[cnt % 4]
                    cnt += 1
                    hcol = hf * 512
                    for k in range(8):
                        mm(ps[0:ntok, :], uT[:, k, t * 128:t * 128 + ntok], w[:, k, :], k == 0, k == 7, [uT, w], [ps])
                    if kind == 0:
                        S.op('act', lambda ps=ps, hcol=hcol: A.activation(out=qf[0:ntok, hcol:hcol + 512], in_=ps[0:ntok, :], func=AF.Copy, scale=0.125), reads=[ps], writes=[qf])
                    elif kind == 3:
                        S.op('act', lambda ps=ps, t=t, hcol=hcol: A.activation(out=szb[0:ntok, t, hcol:hcol + 512], in_=ps[0:ntok, :], func=AF.Silu), reads=[ps], writes=[szb])
                    else:
                        S.op('act', lambda ps=ps, tgt=tgt, hcol=hcol: A.copy(out=tgt[0:ntok, hcol:hcol + 512], in_=ps[0:ntok, :]), reads=[ps], writes=[tgt])
                if kind == 3:
                    continue
                if kind < 2:
                    if sample:
                        rope(tgt, ntok, cos_s[:, :], sin_s[:, :], cos_s)
                    else:
                        tt = tok_base_tile + t
                        rope(tgt, ntok, cos_t[:, tt, :], sin_t[:, tt, :], cos_t)
                    if kind == 1:
                        dst = ks[:, :] if sample else kp[(tok_base_tile + t) * 128:(tok_base_tile + t + 1) * 128, :]
                        S.dma('sp', dst, kf[0:ntok, :], reads=[kf], writes=[])
                    S.op('pool', lambda tgt=tgt: G.tensor_copy(out=xbf[0:ntok, :], in_=tgt[0:ntok, :]), reads=[tgt], writes=[xbf])
                    for c in range(8):
                        S.op('pe', lambda c=c: PE.transpose(out=pT[:, c * 128:c * 128 + ntok], in_=xbf[0:ntok, c * 128:(c + 1) * 128], identity=ident[0:ntok, 0:ntok]),
                             reads=[xbf, ident], writes=[pT])
                    dT = qT if kind == 0 else kTc
                    S.op('act', lambda dT=dT, t=t: A.copy(out=dT[:, :, t * 128:t * 128 + ntok], in_=pT[:, :].rearrange("p (c n) -> p c n", c=8)[:, :, 0:ntok]), reads=[pT], writes=[dT])
                else:
                    dst = vs[:, :] if sample else vp[(tok_base_tile + t) * 128:(tok_base_tile + t + 1) * 128, :]
                    S.dma('sp' if sample else 'pool', dst, vf[0:ntok, :], reads=[vf], writes=[])
                    S.op('pool', lambda t=t: G.tensor_copy(out=vac[0:ntok, t, :, 0:128], in_=vf[0:ntok, :].rearrange("p (h d) -> p h d", h=8)), reads=[vf], writes=[vac])

    def attn_finish(oaps, np_, t, h, zap, dst_ap, rd_extra):
        o1, o2 = oaps
        S.op('dve', lambda: V.reciprocal(out=small[0:np_, 30:31], in_=o1[:, 128:129]), reads=rd_extra, writes=[small])
        S.op('dve', lambda: V.reciprocal(out=small[0:np_, 31:32], in_=o2[:, 128:129]), reads=rd_extra, writes=[small])
        S.op('dve', lambda: V.tensor_tensor(out=small[0:np_, 31:32], in0=small[0:np_, 31:32], in1=lam[0:np_, 0:1], op=ALU.mult), reads=[small, lam], writes=[small])
        ob = fa[6]
        S.op('dve', lambda: V.tensor_scalar_mul(out=ob[0:np_, 0:128], in0=o1[:, 0:128], scalar1=small[0:np_, 30:31]), reads=rd_extra + [small], writes=[ob])
        S.op('dve', lambda: V.scalar_tensor_tensor(out=ob[0:np_, 0:128], in0=o2[:, 0:128], scalar=small[0:np_, 31:32], in1=ob[0:np_, 0:128], op0=ALU.mult, op1=ALU.add),
             reads=rd_extra + [small, ob], writes=[ob])
        S.op('act', lambda: A.activation(out=ob[0:np_, 128:256], in_=ob[0:np_, 0:128], func=AF.Square, accum_out=small[0:np_, 32:33]), reads=[ob], writes=[ob, small])
        S.op('dve', lambda: V.tensor_scalar(out=small[0:np_, 33:34], in0=small[0:np_, 32:33], scalar1=1.0 / 128.0, scalar2=1e-5, op0=ALU.mult, op1=ALU.add), reads=[small], writes=[small])
        S.op('act', lambda: A.activation(out=small[0:np_, 33:34], in_=small[0:np_, 33:34], func=AF.Ln), reads=[small], writes=[small])
        S.op('act', lambda: A.activation(out=small[0:np_, 33:34], in_=small[0:np_, 33:34], func=AF.Exp, scale=-0.5), reads=[small], writes=[small])
        S.op('dve', lambda: V.scalar_tensor_tensor(out=ob[0:np_, 0:128], in0=ob[0:np_, 0:128], scalar=small[0:np_, 33:34], in1=subg[0:np_, :], op0=ALU.mult, op1=ALU.mult),
             reads=[ob, small, subg], writes=[ob])
        S.op('dve', lambda: V.tensor_tensor(out=dst_ap, in0=ob[0:np_, 0:128], in1=zap, op=ALU.mult), reads=[ob, szb], writes=[gtm])

    def kv_load(ci, h):
        S.dma('sp', kTh[h % 2][:, 0:ci * 512], kT_d[h, :, 0:ci * 512], reads=[kT_d], writes=[kTh[h % 2]])
        S.dma('sp', vh[h % 2][:, 0:ci * 4, :], v_d[h, :, 0:ci * 4, :], reads=[v_d], writes=[vh[h % 2]])

    def attn_prompt(ci):
        nkt_prev = ci * 4
        for h in range(8):
            kb, vb = kTh[h % 2], vh[h % 2]
            if ci > 0 and h + 1 < 8:
                kv_load(ci, h + 1)
            def oacc(sh, qs):
                i = sh * 4 + qs
                bank = (pO[0], pO[1], pX)[i // 3]
                return bank, bank[:, (i % 3) * 129:(i % 3) * 129 + 129]
            nkt = nkt_prev + 4
            cnt = 0
            for kt in range(nkt):
                jd = kt - nkt_prev
                q0 = max(jd, 0) * 128
                for sh in range(2):
                    ps = pA[cnt % 4]
                    pt_ = pTs[cnt % 3]
                    cnt += 1
                    if jd >= 0:
                        klhs = kTc[sh * 64:(sh + 1) * 64, h, jd * 128:(jd + 1) * 128]; krd = kTc
                        vrhs = vac[:, jd, h, :]; vrd = vac
                    else:
                        klhs = kb[sh * 64:(sh + 1) * 64, kt * 128:(kt + 1) * 128]; krd = kb
                        vrhs = vb[:, kt, :]; vrd = vb
                    mm(ps[:, q0:512], klhs, qT[sh * 64:(sh + 1) * 64, h, q0:512], True, True, [krd, qT], [ps])
                    S.op('act', lambda ps=ps, pt_=pt_, q0=q0: A.activation(out=pt_[:, q0:512], in_=ps[:, q0:512], func=AF.Exp), reads=[ps], writes=[pt_])
                    if jd >= 0:
                        S.op('pool', lambda pt_=pt_, q0=q0: G.tensor_tensor(out=pt_[:, q0:q0 + 128], in0=pt_[:, q0:q0 + 128], in1=tri[:, :], op=ALU.mult), reads=[pt_, tri], writes=[pt_])
                    for qs in range(max(jd, 0), 4):
                        bank, oap = oacc(sh, qs)
                        mm(oap, pt_[:, qs * 128:(qs + 1) * 128], vrhs, kt == 0 and (sh * 4 + qs) % 3 == 0, kt == nkt_prev + qs, [pt_, vrd], [bank], skip=True)
            osb = [fa[(h % 2) * 3 + i] for i in range(3)]
            for i, bank in enumerate((pO[0], pO[1], pX)):
                if i % 2 == 0:
                    S.op('act', lambda i=i, bank=bank: A.copy(out=osb[i][:, 0:387], in_=bank[:, 0:387]), reads=[bank], writes=[osb[i]])
                else:
                    S.op('dve', lambda i=i, bank=bank: V.tensor_copy(out=osb[i][:, 0:387], in_=bank[:, 0:387]), reads=[bank], writes=[osb[i]])
            for qs in range(4):
                i1, i2 = qs, 4 + qs
                o1 = osb[i1 // 3][:, (i1 % 3) * 129:(i1 % 3) * 129 + 129]
                o2 = osb[i2 // 3][:, (i2 % 3) * 129:(i2 % 3) * 129 + 129]
                attn_finish((o1, o2), 128, qs, h, szb[:, qs, h * 128:(h + 1) * 128], gtm[:, qs, h * 128:(h + 1) * 128], [osb[i1 // 3], osb[i2 // 3]])
        if ci < NCH - 1:
            S.dma('pool', kT_d.ap()[:, :, ci * 512:(ci + 1) * 512].rearrange("h p n -> p h n"), kTc[:, :, :], reads=[kTc], writes=[kT_d])
            for t4 in range(4):
                S.dma('pool', v_d.ap()[:, :, ci * 4 + t4, :].rearrange("h p d -> p h d"), vac[:, t4, :, :], reads=[vac], writes=[v_d])

    def g_to_gT(ntiles, ntok):
        for t in range(ntiles):
            for c in range(8):
                S.op('pe', lambda c=c, t=t: PE.transpose(out=pT[:, c * 128:c * 128 + ntok], in_=gtm[0:ntok, t, c * 128:(c + 1) * 128], identity=ident[0:ntok, 0:ntok]),
                     reads=[gtm, ident], writes=[pT])
            S.op('act', lambda t=t: A.copy(out=gT[:, :, t * 128:t * 128 + ntok], in_=pT[:, :].rearrange("p (c n) -> p c n", c=8)[:, :, 0:ntok]), reads=[pT], writes=[gT])

    def attn_sample():
        S.dma('sp', ptab_sb[:], ptab[0:1, :].broadcast_to([128, 4 * NPAGE]), writes=[ptab_sb])
        S.op('pool', lambda: G.iota(pidx[:], pattern=[[0, 4 * NPAGE]], base=0, channel_multiplier=1), writes=[pidx])
        S.op('dve', lambda: V.tensor_copy(out=pidf[:, 0, :], in_=pidx[:]), reads=[pidx], writes=[pidf])
        S.op('dve', lambda: V.tensor_copy(out=pidf[:, 1, :], in_=ptab_sb[:]), reads=[ptab_sb], writes=[pidf])
        S.op('dve', lambda: V.scalar_tensor_tensor(out=pidf[:, 1, :], in0=pidf[:, 1, :], scalar=128.0, in1=pidf[:, 0, :], op0=ALU.mult, op1=ALU.add), reads=[pidf], writes=[pidf])
        S.op('dve', lambda: V.tensor_copy(out=pidx[:], in_=pidf[:, 1, :]), reads=[pidf], writes=[pidx])
        S.op('pool', lambda: G.memset(qblk[:], 0.0), writes=[qblk])
        for h in range(8):
            for j in range(2):
                S.op('act', lambda h=h, j=j: A.copy(
                    out=qblk[j * 64:(j + 1) * 64, h, :, h * 8:(h + 1) * 8].rearrange("p b (t j) -> p b t j", j=2)[:, :, :, j],
                    in_=qT[j * 64:(j + 1) * 64, h, 0:16].rearrange("p (b t) -> p b t", b=4)), reads=[qT], writes=[qblk])
        NK = PAST + 16
        ck2 = ck.ap().rearrange("g p n -> (g p) n"); cv2 = cv.ap().rearrange("g p n -> (g p) n")
        if True:
            for b in range(4):
                for pg in range(NPAGE):
                    kb, ktp = kpg[pg % 2], kTp[pg % 2]
                    col = b * NPAGE + pg
                    pf = pgf[pg % 2]
                    S.idma(pf[:, :], ck2, pidx[:, col:col + 1], reads=[pidx], writes=[pf])
                    S.op('dve', lambda kb=kb, pf=pf: V.tensor_copy(out=kb[:, :], in_=pf[:, :]), reads=[pf], writes=[kb])
                    for c in range(8):
                        S.op('pe', lambda c=c, kb=kb: PE.transpose(out=pT[:, c * 128:(c + 1) * 128], in_=kb[:, c * 128:(c + 1) * 128], identity=ident[:, :]),
                             reads=[kb, ident], writes=[pT])
                    S.op('act', lambda ktp=ktp: A.copy(out=ktp[:, :, :], in_=pT[:, :].rearrange("p (c n) -> p c n", c=8)), reads=[pT], writes=[ktp])
                    ps = pA[pg % 4]
                    for hh in range(8):
                        mm(ps[0:64, 0:128], qblk[:, hh, b, :], ktp[:, hh, :], hh == 0, hh == 7, [qblk, ktp], [ps])
                    S.op('dve', lambda ps=ps, pg=pg: V.tensor_copy(out=sS[:, pg * 128:(pg + 1) * 128], in_=ps[0:64, 0:128]), reads=[ps], writes=[sS])
                ps = pA[0]
                for hh in range(8):
                    mm(ps[0:64, 0:16], qblk[:, hh, b, :], kTc[:, hh, 0:16], hh == 0, hh == 7, [qblk, kTc], [ps])
                S.op('dve', lambda ps=ps, b=b: V.tensor_tensor(out=sS[:, PAST:NK], in0=ps[0:64, 0:16], in1=cmask[:, b, :], op=ALU.add), reads=[ps, cmask], writes=[sS])
                S.op('dve', lambda: V.reduce_max(out=small[0:64, 40:41], in_=sS[:, 0:NK], axis=AX.X), reads=[sS], writes=[small])
                S.op('dve', lambda: V.tensor_scalar_mul(out=small[0:64, 41:42], in0=small[0:64, 40:41], scalar1=-1.0), reads=[small], writes=[small])
                S.op('act', lambda: A.activation(out=sS[:, 0:NK], in_=sS[:, 0:NK], func=AF.Exp, bias=small[0:64, 41:42], accum_out=small[0:64, 42:43]),
                     reads=[sS, small], writes=[sS, small])
                S.op('dve', lambda: V.reciprocal(out=small[0:64, 43:44], in_=small[0:64, 42:43]), reads=[small], writes=[small])
                S.op('dve', lambda: V.tensor_scalar_mul(out=pS[:, 0:NK], in0=sS[:, 0:NK], scalar1=small[0:64, 43:44]), reads=[sS, small], writes=[pS])
                for pg in range(NPAGE + 1):
                    nk = 128 if pg < NPAGE else 16
                    S.op('pe', lambda pg=pg, nk=nk: PE.transpose(out=pT[0:nk, 0:64], in_=pS[:, pg * 128:pg * 128 + nk], identity=ident[0:64, 0:64]),
                         reads=[pS, ident], writes=[pT])
                    S.op('act', lambda nk=nk: A.copy(out=pTS[0:nk, :], in_=pT[0:nk, 0:64]), reads=[pT], writes=[pTS])
                    if pg < NPAGE:
                        vb = vpg[pg % 2]
                        col = b * NPAGE + pg
                        pf = pgf[pg % 2]
                        S.idma(pf[:, :], cv2, pidx[:, col:col + 1], reads=[pidx], writes=[pf])
                        S.op('act', lambda vb=vb, pf=pf: A.copy(out=vb[:, :], in_=pf[:, :]), reads=[pf], writes=[vb])
                    for hh in range(8):
                        bank = pO[hh // 4]
                        oap = bank[0:8, (hh % 4) * 128:(hh % 4) * 128 + 128]
                        if pg < NPAGE:
                            rhs, rd = vb[:, hh * 128:(hh + 1) * 128], vb
                        else:
                            rhs, rd = vac[0:16, 0, hh, 0:128], vac
                        mm(oap, pTS[0:nk, hh * 8:(hh + 1) * 8], rhs, pg == 0 and hh % 4 == 0, pg == NPAGE, [pTS, rd], [bank], skip=True)
                o8, od = tm[0], tm[1]
                S.op('act', lambda: A.copy(out=o8[0:8, 0:512], in_=pO[0][0:8, :]), reads=[pO[0]], writes=[o8])
                S.op('act', lambda: A.copy(out=o8[0:8, 512:1024], in_=pO[1][0:8, :]), reads=[pO[1]], writes=[o8])
                for half in range(2):
                    mm(pA[1][0:4, :], sel4[:, 0:4], o8[0:8, half * 512:(half + 1) * 512], True, True, [sel4, o8], [pA[1]])
                    S.op('act', lambda half=half: A.copy(out=od[0:4, half * 512:(half + 1) * 512], in_=pA[1][0:4, :]), reads=[pA[1]], writes=[od])
                S.dma('sp', osm_d[b * 4:(b + 1) * 4, :], od[0:4, :], reads=[od], writes=[osm_d])
        osm = tm[2]
        S.dma('sp', osm[0:16, :], osm_d[:, :], reads=[osm_d], writes=[osm])
        for hh in range(8):
            ob = fa[6]
            S.op('act', lambda hh=hh: A.activation(out=ob[0:16, 128:256], in_=osm[0:16, hh * 128:(hh + 1) * 128], func=AF.Square, accum_out=small[0:16, 32:33]), reads=[osm], writes=[ob, small])
            S.op('dve', lambda: V.tensor_scalar(out=small[0:16, 33:34], in0=small[0:16, 32:33], scalar1=1.0 / 128.0, scalar2=1e-5, op0=ALU.mult, op1=ALU.add), reads=[small], writes=[small])
            S.op('act', lambda: A.activation(out=small[0:16, 33:34], in_=small[0:16, 33:34], func=AF.Ln), reads=[small], writes=[small])
            S.op('act', lambda: A.activation(out=small[0:16, 33:34], in_=small[0:16, 33:34], func=AF.Exp, scale=-0.5), reads=[small], writes=[small])
            S.op('dve', lambda hh=hh: V.scalar_tensor_tensor(out=ob[0:16, 0:128], in0=osm[0:16, hh * 128:(hh + 1) * 128], scalar=small[0:16, 33:34], in1=subg[0:16, :], op0=ALU.mult, op1=ALU.mult),
                 reads=[osm, small, subg], writes=[ob])
            S.op('dve', lambda hh=hh: V.tensor_tensor(out=gtm[0:16, 0, hh * 128:(hh + 1) * 128], in0=ob[0:16, 0:128], in1=szb[0:16, 0, hh * 128:(hh + 1) * 128], op=ALU.mult), reads=[ob, szb], writes=[gtm])

    osm_d = S.dram("osm_d", [16, D])
    rowi = sbt("rowi", [64, 16], I32); coli = sbt("coli", [64, 16], I32); mk = sbt("mk", [64, 2, 16])
    S.op('pool', lambda: G.iota(rowi[:], pattern=[[0, 16]], base=0, channel_multiplier=1), writes=[rowi])
    S.op('dve', lambda: V.tensor_scalar(out=rowi[:], in0=rowi[:], scalar1=1, scalar2=3, op0=ALU.arith_shift_right, op1=ALU.bitwise_and), reads=[rowi], writes=[rowi])
    S.op('pool', lambda: G.iota(coli[:], pattern=[[1, 16]], base=0, channel_multiplier=0), writes=[coli])
    S.op('dve', lambda: V.tensor_single_scalar(out=mk[:, 1, :], in_=coli[:], scalar=2, op=ALU.arith_shift_right), reads=[coli], writes=[mk]) if False else None
    cb_i = sbt("cb_i", [64, 16], I32)
    S.op('dve', lambda: V.tensor_single_scalar(out=cb_i[:], in_=coli[:], scalar=2, op=ALU.arith_shift_right), reads=[coli], writes=[cb_i])
    S.op('dve', lambda: V.tensor_single_scalar(out=coli[:], in_=coli[:], scalar=3, op=ALU.bitwise_and), reads=[coli, cb_i], writes=[coli])
    S.op('dve', lambda: V.tensor_tensor(out=coli[:], in0=coli[:], in1=rowi[:], op=ALU.subtract), reads=[coli, rowi], writes=[coli])
    S.op('dve', lambda: V.tensor_copy(out=mk[:, 0, :], in_=coli[:]), reads=[coli], writes=[mk])
    S.op('dve', lambda: V.tensor_copy(out=mk[:, 1, :], in_=cb_i[:]), reads=[cb_i], writes=[mk])
    S.op('dve', lambda: V.tensor_single_scalar(out=mk[:, 0, :], in_=mk[:, 0, :], scalar=0.0, op=ALU.is_le), reads=[mk], writes=[mk])
    for b in range(4):
        S.op('dve', lambda b=b: V.tensor_single_scalar(out=cmask[:, b, :], in_=mk[:, 1, :], scalar=float(b), op=ALU.is_equal), reads=[mk], writes=[cmask])
        S.op('dve', lambda b=b: V.tensor_tensor(out=cmask[:, b, :], in0=cmask[:, b, :], in1=mk[:, 0, :], op=ALU.mult), reads=[mk, cmask], writes=[cmask])
        S.op('dve', lambda b=b: V.tensor_scalar(out=cmask[:, b, :], in0=cmask[:, b, :], scalar1=-1.0, scalar2=30000.0, op0=ALU.add, op1=ALU.mult), reads=[cmask], writes=[cmask])
    seli = sbt("seli", [8, 8], I32); self_ = sbt("self_", [8, 8]); sel4 = sbt("sel4", [8, 4])
    S.op('pool', lambda: G.iota(seli[:, 0:4], pattern=[[-2, 4]], base=0, channel_multiplier=1), writes=[seli])
    S.op('pool', lambda: G.iota(seli[:, 4:8], pattern=[[-2, 4]], base=-1, channel_multiplier=1), writes=[seli])
    S.op('dve', lambda: V.tensor_copy(out=self_[:], in_=seli[:]), reads=[seli], writes=[self_])
    S.op('dve', lambda: V.tensor_single_scalar(out=self_[:], in_=self_[:], scalar=0.0, op=ALU.is_equal), reads=[self_], writes=[self_])
    S.op('dve', lambda: V.scalar_tensor_tensor(out=sel4[:], in0=self_[:, 4:8], scalar=lam[0:8, 0:1], in1=self_[:, 0:4], op0=ALU.mult, op1=ALU.add), reads=[self_, lam], writes=[sel4])

    def run_layers(sample, ci):
        ntiles = 1 if sample else 4
        ntok = 16 if sample else 128
        N = 16 if sample else 512
        xt = Xs if sample else X
        last = (ci == NCH - 1)
        for l in range(4):
            kind, j = l % 3, l // 3
            if sample:
                adaln_layer(l)
                bcast_layer(l, False)
                if kind == 0:
                    rows_to_fm(st_conv[j, :, :], 8, lambda c: stT[:, c, 0:8], stT)
                elif kind == 1:
                    rows_to_fm(st_lc[:, :], 12, lambda c: stT[:, c, 0:12], stT)
                    rows_to_fm(st_h[:, :], 4, lambda c: stT2[:, c, :], stT2)
                make_uT(l, Xs[:, :], 16, 0, True)
            else:
                bcast_layer(l, True)
                for t in range(4):
                    make_uT(l, X[:, t, :], 128, t * 128, False)
            if kind == 0:
                conv_layer(l, j, N, sample, last)
                w_out_d = ('a_out', j)
            elif kind == 1:
                lru_layer(l, N, sample, last)
                w_out_d = ('r_out', 0)
            else:
                if not sample and ci > 0:
                    kv_load(ci, 0)
                attn_project(ntiles, ntok, ci * 4, sample)
                if sample:
                    attn_sample()
                else:
                    attn_prompt(ci)
                g_to_gT(ntiles, ntok)
                w_out_d = ('d_out', 0)
            if sample:
                out_proj_ln(l, w_out_d, 1, 16, Xs, lambda t: Xs[:, :], True)
            else:
                out_proj_ln(l, w_out_d, 4, 128, X, lambda t: X[:, t, :], False)

    S.op('pool', lambda: G.memset(vac[:], 1.0), writes=[vac])
    S.dma('sp', Xs[:, :], xs[:, :], writes=[Xs])
    run_layers(True, 0)
    S.dma('sp', ys[:, :], Xs[:, :], reads=[Xs], writes=[])
    S.barrier()
    stack.close()
    X = S.sb("X", [128, 4, D])
    fa = [S.sb("fa%d" % i, [128, 520]) for i in range(8)]
    uT = S.sb("uT", [128, 8, 512], BF16); gT = S.sb("gT", [128, 8, 512], BF16)
    qT = S.sb("qT", [128, 8, 512], BF16); kTc = S.sb("kTc", [128, 8, 512], BF16)
    vac = S.sb("vac", [128, 4, 8, 129], BF16)
    szb = S.sb("szb", [128, 4, D], BF16); gtm = S.sb("gtm", [128, 4, D], BF16)
    kTh = [S.sb("kTh%d" % i, [128, SEQ], BF16) for i in range(2)]
    vh = [S.sb("vh%d" % i, [128, SEQ // 128, 129], BF16) for i in range(2)]
    pTs = [S.sb("pTs%d" % i, [128, 512], BF16) for i in range(3)]
    S.op('pool', lambda: G.memset(vac[:], 1.0), writes=[vac])
    for i in range(2):
        S.op('pool', lambda i=i: G.memset(vh[i][:], 1.0), writes=[vh[i]])
    for ci in range(NCH):
        S.dma('sp', X[:, :, :], xp[ci * 512:(ci + 1) * 512, :].rearrange("(t p) d -> p t d", p=128), writes=[X])
        run_layers(False, ci)
        S.dma('pool', yp[ci * 512:(ci + 1) * 512, :].rearrange("(t p) d -> p t d", p=128), X[:, :, :], reads=[X], writes=[])
    S.finish()
    return nc


def _run(inp, SEQ, NPAGE, NPHYS, PAST):
    f = lambda a: np.ascontiguousarray(np.asarray(a), dtype=np.float32)
    nc = build(SEQ, NPAGE, NPHYS, PAST)
    ck = f(inp['cache_k'][0]).reshape(NPHYS, 128, D)
    cv = f(inp['cache_v'][0]).reshape(NPHYS, 128, D)
    shared = dict(
        ck=ck, cv=cv,
        w_ada=f(inp['w_ada']), b_ada=f(inp['b_ada']), ln_g=f(inp['ln_g']), ln_b=f(inp['ln_b']),
        a_w_in=f(inp['a_w_in']), a_conv_w=f(inp['a_conv_w']), a_w_out=f(inp['a_w_out']),
        r_w_in=f(inp['r_w_in']), r_conv_w=f(inp['r_conv_w']), r_conv_b=f(inp['r_conv_b']),
        r_w_ga=f(inp['r_w_ga']), r_b_ga=f(inp['r_b_ga']), r_w_gx=f(inp['r_w_gx']), r_b_gx=f(inp['r_b_gx']),
        r_lru=f(inp['r_lru_param']), r_w_out=f(inp['r_w_out']),
        d_w_in=f(inp['d_w_in']),
        d_lam=np.stack([f(inp['d_lq1'])[0], f(inp['d_lk1'])[0], f(inp['d_lq2'])[0], f(inp['d_lk2'])[0]], 0),
        d_subg=f(inp['d_subln_g']), d_w_out=f(inp['d_w_out']),
    )
    xp_, xs_ = f(inp['x_prompt']), f(inp['x_sample'])
    pt = np.ascontiguousarray(np.asarray(inp['page_table']), dtype=np.int32)
    in_maps = []
    for c in range(8):
        b = c // 2
        sl = slice(4 * c, 4 * c + 4)
        m = dict(shared)
        m.update(
            xp=xp_[b], xs=xs_[sl].reshape(16, D),
            cp=f(inp['c_prompt'])[b:b + 1], cs=f(inp['c_sample'])[sl],
            st_conv=f(inp['state_conv_a'])[:, sl].reshape(2, 8, D),
            st_h=f(inp['state_lru_h'])[0, sl], st_lc=f(inp['state_lru_conv'])[0, sl].reshape(12, D),
            ptab=pt[sl].reshape(1, 4 * NPAGE),
        )
        in_maps.append(m)
    res = run_bass_kernel_spmd(nc, in_maps, core_ids=list(range(8)))
    R = res.results
    ev = [R[2 * b] for b in range(4)]
    y_p = np.stack([r['yp'] for r in ev], 0)
    y_s = np.concatenate([r['ys'].reshape(4, 4, D) for r in R], 0)
    conv_p = np.stack([r['conv_p'] for r in ev], 1)
    conv_s = np.concatenate([r['conv_s'].reshape(2, 4, 2, D) for r in R], 1)
    lruh_p = np.stack([r['lruh_p'][0] for r in ev], 0)[None]
    lruh_s = np.concatenate([r['lruh_s'] for r in R], 0)[None]
    lruc_p = np.stack([r['lruc_p'] for r in ev], 0)[None]
    lruc_s = np.concatenate([r['lruc_s'].reshape(4, 3, D) for r in R], 0)[None]
    k_p = np.stack([r['kp'].reshape(SEQ, 16, 64) for r in ev], 0)[None]
    v_p = np.stack([r['vp'].reshape(SEQ, 8, 128) for r in ev], 0)[None]
    k_s = np.concatenate([r['ks'].reshape(4, 4, 16, 64) for r in R], 0)[None]
    v_s = np.concatenate([r['vs'].reshape(4, 4, 8, 128) for r in R], 0)[None]
    outs = (y_p, y_s, conv_p, conv_s, lruh_p, lruh_s, lruc_p, lruc_s, k_p, v_p, k_s, v_s)
    return tuple(np.ascontiguousarray(o, dtype=np.float32) for o in outs)


def kernel(**inputs):
    return _run(inputs, 4096, 64, 2560, 8192)
```
